# Optimizing a Trainium2 kernel written in Bass

```python
import jax, jax.numpy as jnp
from jax import lax
import numpy as np

D_MODEL = 1024
BATCH = 16
SEQ = 256
DEPTH = 2
DEC_BATCH = 8
DEC_SEQ = 4096
PAST_LEN = 512

GRID_W = 64
N_HEADS = 8
N_KV_HEADS = 2
HEAD_DIM = 64
Q_W = N_HEADS * HEAD_DIM
KV_W = N_KV_HEADS * HEAD_DIM
AXIS_DIM = HEAD_DIM // 2
ROPE_THETA = 10000.0
Q_BLOCK = 128
CONV_W = 512
CONV_K = 31
POOL_W = 512
POOL_GROUPS = 4
POOL_GC = POOL_W // POOL_GROUPS
POOL_WINDOWS = (2, 4, 8, 16)
SGU_W = 512
SGU_GROUPS = 4
SGU_GC = SGU_W // SGU_GROUPS
SGU_CHUNK = 128
BRANCH_W = 512
N_BRANCH = 4
D_FF = 4 * D_MODEL
IN_W = Q_W + 2 * KV_W + 2 * CONV_W + POOL_W + 2 * SGU_W
EPS = 1e-6

kernel_name = "hybrid_diffusion_prefix_trunk_step"


def rmsnorm(x, g):
    xf = x.astype(jnp.float32)
    y = xf * lax.rsqrt(jnp.mean(xf * xf, axis=-1, keepdims=True) + EPS)
    return (y * g.astype(jnp.float32)).astype(x.dtype)


def layernorm(x, g, b):
    xf = x.astype(jnp.float32)
    mu = jnp.mean(xf, axis=-1, keepdims=True)
    var = jnp.mean(jnp.square(xf - mu), axis=-1, keepdims=True)
    y = (xf - mu) * lax.rsqrt(var + EPS)
    return (y * g.astype(jnp.float32) + b.astype(jnp.float32)).astype(x.dtype)


def axial_rope(n):
    rows = n // GRID_W
    row = jnp.repeat(jnp.arange(rows, dtype=jnp.float32), GRID_W)
    col = jnp.tile(jnp.arange(GRID_W, dtype=jnp.float32), rows)
    inv = ROPE_THETA ** (-jnp.arange(0, AXIS_DIM, 2, dtype=jnp.float32) / AXIS_DIM)
    ang = jnp.concatenate([row[:, None] * inv, col[:, None] * inv], axis=-1)
    return jnp.cos(ang), jnp.sin(ang)


def apply_rope(x, cos, sin):
    xf = x.astype(jnp.float32).reshape(x.shape[:-1] + (HEAD_DIM // 2, 2))
    x0, x1 = xf[..., 0], xf[..., 1]
    c = cos[None, :, None, :]
    s = sin[None, :, None, :]
    out = jnp.stack([x0 * c - x1 * s, x0 * s + x1 * c], axis=-1).reshape(x.shape)
    return out.astype(x.dtype)


def block_attention(q, k, v):
    b, n = q.shape[0], q.shape[1]
    nb = n // Q_BLOCK
    g = N_HEADS // N_KV_HEADS
    qb = q.reshape(b, nb, Q_BLOCK, N_KV_HEADS, g, HEAD_DIM).transpose(1, 0, 2, 3, 4, 5)
    scale = HEAD_DIM ** -0.5

    def one_block(qi):
        s = jnp.einsum('bqkgd,btkd->bkgqt', qi, k).astype(jnp.float32) * scale
        p = jax.nn.softmax(s, axis=-1).astype(v.dtype)
        return jnp.einsum('bkgqt,btkd->bqkgd', p, v)

    o = lax.map(one_block, qb)
    return o.transpose(1, 0, 2, 3, 4, 5).reshape(b, n, Q_W)


def conformer_conv(a, conv_w, conv_b, ln_g, ln_b):
    a = a[..., :CONV_W] * jax.nn.sigmoid(a[..., CONV_W:])
    y = lax.conv_general_dilated(a, conv_w[:, None, :].astype(a.dtype), (1,),
                                 [(CONV_K // 2, CONV_K // 2)],
                                 dimension_numbers=('NWC', 'WIO', 'NWC'),
                                 feature_group_count=CONV_W) + conv_b
    return jax.nn.silu(layernorm(y, ln_g, ln_b))


def multiscale_pool(x, pool_w, pool_scale):
    b, n = x.shape[0], x.shape[1]
    xg = x.reshape(b, n, POOL_GROUPS, POOL_GC)
    cs = jnp.cumsum(xg.astype(jnp.float32), axis=1)
    cs = jnp.concatenate([jnp.zeros_like(cs[:, :1]), cs], axis=1)
    t = jnp.arange(n)[:, None]
    w = jnp.array(POOL_WINDOWS, dtype=jnp.int32)[None, :]
    lo = jnp.clip(t - w // 2, 0, n)
    hi = jnp.clip(t - w // 2 + w, 0, n)
    gidx = jnp.arange(POOL_GROUPS)[None, :]
    s = cs[:, hi, gidx, :] - cs[:, lo, gidx, :]
    mean = s / (hi - lo).astype(jnp.float32)[None, :, :, None]
    pooled = (mean - xg.astype(jnp.float32)).astype(x.dtype)
    y = jnp.einsum('bsgc,gcd->bsgd', pooled, pool_w)
    return y.reshape(b, n, POOL_W) * pool_scale


def spatial_gating(z, norm_g, sgu_w, sgu_b):
    b, n = z.shape[0], z.shape[1]
    z = jax.nn.gelu(z)
    u, v = z[..., :SGU_W], z[..., SGU_W:]
    v = rmsnorm(v, norm_g).reshape(b, n // SGU_CHUNK, SGU_CHUNK, SGU_GROUPS, SGU_GC)
    sv = jnp.einsum('gij,bnjgc->bnigc', sgu_w, v) + sgu_b.T[None, None, :, :, None]
    return u * sv.reshape(b, n, SGU_W)


def mixer(h, lp, rope, ctx_k, ctx_v):
    b, n = h.shape[0], h.shape[1]
    z = h @ lp['w_in']
    o1 = Q_W
    o2 = o1 + KV_W
    o3 = o2 + KV_W
    o4 = o3 + 2 * CONV_W
    o5 = o4 + POOL_W
    q, k, v, a, pl, sg = jnp.split(z, [o1, o2, o3, o4, o5], axis=-1)
    q = rmsnorm(q.reshape(b, n, N_HEADS, HEAD_DIM), lp['q_norm_g'])
    k = rmsnorm(k.reshape(b, n, N_KV_HEADS, HEAD_DIM), lp['k_norm_g'])
    v = v.reshape(b, n, N_KV_HEADS, HEAD_DIM)
    if rope is None:
        keys, vals = k, v
    else:
        cos, sin = rope
        q = apply_rope(q, cos, sin)
        k = apply_rope(k, cos, sin)
        keys = jnp.concatenate([ctx_k.astype(k.dtype), k], axis=1)
        vals = jnp.concatenate([ctx_v.astype(v.dtype), v], axis=1)
    attn = block_attention(q, keys, vals)
    conv = conformer_conv(a, lp['conv_w'], lp['conv_b'], lp['conv_ln_g'], lp['conv_ln_b'])
    pool = multiscale_pool(pl, lp['pool_w'], lp['pool_scale'])
    sgu = spatial_gating(sg, lp['sgu_norm_g'], lp['sgu_w'], lp['sgu_b'])
    branches = (attn, conv, pool, sgu)
    merged = None
    for i in range(N_BRANCH):
        gate = jax.nn.sigmoid(h @ lp['w_gate'][:, i * D_MODEL:(i + 1) * D_MODEL]
                              + lp['b_gate'][i * D_MODEL:(i + 1) * D_MODEL])
        term = gate * (branches[i] @ lp['w_branch'][i])
        merged = term if merged is None else merged + term
    return merged @ lp['w_out'], k, v


def layer(x, mod, lp, rope, ctx_k, ctx_v):
    sh1, sc1, g1, sh2, sc2, g2 = jnp.split(mod, 6, axis=-1)
    h = rmsnorm(x, lp['norm1_g']) * (1.0 + sc1) + sh1
    m, k, v = mixer(h, lp, rope, ctx_k, ctx_v)
    x = x + g1 * m
    h = rmsnorm(x, lp['norm2_g']) * (1.0 + sc2) + sh2
    f = jnp.square(jax.nn.relu(h @ lp['w_mlp_in'])) @ lp['w_mlp_out']
    return x + g2 * f, k, v


def setup_inputs(seed: int = 0) -> dict:
    key = jax.random.key(seed)
    ks = jax.random.split(key, 32)
    f32 = jnp.float32

    def nrm(k, shape, scale):
        return jax.random.normal(k, shape, f32) * scale

    D = D_MODEL
    return {
        'x_prompt': nrm(ks[0], (BATCH, SEQ, D), 1.0),
        'x_sample': nrm(ks[1], (DEC_BATCH, DEC_SEQ, D), 1.0),
        'cache_k': nrm(ks[2], (DEC_BATCH, DEPTH, PAST_LEN, N_KV_HEADS, HEAD_DIM), 1.0),
        'cache_v': nrm(ks[3], (DEC_BATCH, DEPTH, PAST_LEN, N_KV_HEADS, HEAD_DIM), 1.0),
        'c': nrm(ks[4], (DEC_BATCH, D), 1.0),
        'c_ctx': nrm(ks[5], (D,), 1.0),
        'w_mod': nrm(ks[6], (DEPTH, D, 6 * D), 0.5 * D ** -0.5),
        'b_mod': nrm(ks[7], (DEPTH, 6 * D), 0.01),
        'norm1_g': 1.0 + nrm(ks[8], (DEPTH, D), 0.05),
        'w_in': nrm(ks[9], (DEPTH, D, IN_W), D ** -0.5),
        'q_norm_g': 1.0 + nrm(ks[10], (DEPTH, HEAD_DIM), 0.05),
        'k_norm_g': 1.0 + nrm(ks[11], (DEPTH, HEAD_DIM), 0.05),
        'conv_w': nrm(ks[12], (DEPTH, CONV_K, CONV_W), CONV_K ** -0.5),
        'conv_b': nrm(ks[13], (DEPTH, CONV_W), 0.01),
        'conv_ln_g': 1.0 + nrm(ks[14], (DEPTH, CONV_W), 0.05),
        'conv_ln_b': nrm(ks[15], (DEPTH, CONV_W), 0.01),
        'pool_w': nrm(ks[16], (DEPTH, POOL_GROUPS, POOL_GC, POOL_GC), POOL_GC ** -0.5),
        'pool_scale': 1.0 + nrm(ks[17], (DEPTH, POOL_W), 0.05),
        'sgu_norm_g': 1.0 + nrm(ks[18], (DEPTH, SGU_W), 0.05),
        'sgu_w': nrm(ks[19], (DEPTH, SGU_GROUPS, SGU_CHUNK, SGU_CHUNK), SGU_CHUNK ** -0.5),
        'sgu_b': 1.0 + nrm(ks[20], (DEPTH, SGU_GROUPS, SGU_CHUNK), 0.01),
        'w_branch': nrm(ks[21], (DEPTH, N_BRANCH, BRANCH_W, D), BRANCH_W ** -0.5),
        'w_gate': nrm(ks[22], (DEPTH, D, N_BRANCH * D), D ** -0.5),
        'b_gate': nrm(ks[23], (DEPTH, N_BRANCH * D), 0.01),
        'w_out': nrm(ks[24], (DEPTH, D, D), D ** -0.5),
        'norm2_g': 1.0 + nrm(ks[25], (DEPTH, D), 0.05),
        'w_mlp_in': nrm(ks[26], (DEPTH, D, D_FF), D ** -0.5),
        'w_mlp_out': nrm(ks[27], (DEPTH, D_FF, D), D_FF ** -0.5),
        'final_norm_g': 1.0 + nrm(ks[28], (D,), 0.05),
    }


def reference(x_prompt, x_sample, cache_k, cache_v, c, c_ctx, w_mod, b_mod, norm1_g, w_in,
              q_norm_g, k_norm_g, conv_w, conv_b, conv_ln_g, conv_ln_b, pool_w, pool_scale,
              sgu_norm_g, sgu_w, sgu_b, w_branch, w_gate, b_gate, w_out, norm2_g,
              w_mlp_in, w_mlp_out, final_norm_g):
    rope = axial_rope(x_sample.shape[1])
    xp = x_prompt
    xs = x_sample
    new_k = []
    new_v = []
    for l in range(DEPTH):
        lp = {
            'norm1_g': norm1_g[l], 'w_in': w_in[l], 'q_norm_g': q_norm_g[l], 'k_norm_g': k_norm_g[l],
            'conv_w': conv_w[l], 'conv_b': conv_b[l], 'conv_ln_g': conv_ln_g[l], 'conv_ln_b': conv_ln_b[l],
            'pool_w': pool_w[l], 'pool_scale': pool_scale[l], 'sgu_norm_g': sgu_norm_g[l],
            'sgu_w': sgu_w[l], 'sgu_b': sgu_b[l], 'w_branch': w_branch[l], 'w_gate': w_gate[l],
            'b_gate': b_gate[l], 'w_out': w_out[l], 'norm2_g': norm2_g[l],
            'w_mlp_in': w_mlp_in[l], 'w_mlp_out': w_mlp_out[l],
        }
        mod_ctx = (jax.nn.silu(c_ctx)[None, :] @ w_mod[l] + b_mod[l])[:, None, :]
        mod_lat = (jax.nn.silu(c) @ w_mod[l] + b_mod[l])[:, None, :]
        xp, kp, vp = layer(xp, mod_ctx, lp, None, None, None)
        new_k.append(kp)
        new_v.append(vp)
        xs, _, _ = layer(xs, mod_lat, lp, rope, cache_k[:, l], cache_v[:, l])
    y_prompt = rmsnorm(xp, final_norm_g)
    y_sample = rmsnorm(xs, final_norm_g)
    new_cache_k = jnp.stack(new_k, axis=1)
    new_cache_v = jnp.stack(new_v, axis=1)
    return (y_prompt, y_sample, new_cache_k, new_cache_v)
```

```python
import numpy as np
from contextlib import ExitStack
import concourse.bass as bass
import concourse.mybir as mybir
from concourse.bass_utils import run_bass_kernel_spmd

F32 = mybir.dt.float32
BF16 = mybir.dt.bfloat16
AF = mybir.ActivationFunctionType
ALU = mybir.AluOpType
AX = mybir.AxisListType

D = 1024
KC = 8
EPS = 1e-6
NLAYER = 2
SEQ_P = 256
PAST = 512
NSLOT = 4
COMPUTE = ('pe', 'act', 'dve', 'pool')


class Buf:
    __slots__ = ('name', 'w', 'r')

    def __init__(self, name):
        self.name = name
        self.w = None
        self.r = []


class Ins:
    __slots__ = ('eng', 'fn', 'deps', 'sig', 'ticket', 'dma', 'chan', 'dmaval')

    def __init__(self, eng, fn, dma=False, chan=None):
        self.eng = eng
        self.fn = fn
        self.deps = []
        self.sig = False
        self.ticket = None
        self.dma = dma
        self.chan = chan
        self.dmaval = None


def _flat(x):
    out = []
    for b in x:
        if isinstance(b, (list, tuple)):
            out.extend(_flat(b))
        else:
            out.append(b)
    return out


class Prog:
    def __init__(self):
        self.q = {e: [] for e in ('pe', 'act', 'dve', 'pool', 'sp')}
        self.chan_cnt = {}
        self.dry = False

    def op(self, eng, fn, reads=(), writes=(), dma=False, chan=None):
        if self.dry:
            return None
        ins = Ins(eng, fn, dma=dma, chan=chan)
        reads = _flat(reads)
        writes = _flat(writes)
        raw = []
        oth = []
        for b in reads:
            if b.w is not None:
                raw.append(b.w)
        for b in writes:
            if b.r:
                oth.extend(b.r)
            elif b.w is not None:
                oth.append(b.w)
        for b in reads:
            if not dma:
                b.r = [x for x in b.r if not (x.eng == eng and not x.dma)]
            b.r.append(ins)
        for b in writes:
            b.w = ins
            b.r = []
        seen = set()
        for d in raw:
            if d is ins or id(d) in seen:
                continue
            seen.add(id(d))
            ins.deps.append((d, self.chan_cnt[d.chan] if d.dma else None))
        for d in oth:
            if d is ins or id(d) in seen:
                continue
            seen.add(id(d))
            if (not dma) and (not d.dma) and d.eng == eng and eng == 'pe':
                continue
            ins.deps.append((d, self.chan_cnt[d.chan] if d.dma else None))
        if dma:
            c = self.chan_cnt.get(chan, 0) + 16
            self.chan_cnt[chan] = c
            ins.dmaval = c
        self.q[eng].append(ins)
        return ins


def finalize_and_emit(prog, block, sems, chan_sems, final_waits):
    for e in prog.q:
        for ins in prog.q[e]:
            for d, _v in ins.deps:
                d.sig = True
    for e in COMPUTE:
        t = 0
        for ins in prog.q[e]:
            if ins.dma:
                continue
            if ins.sig:
                t += 1
                ins.ticket = t

    def emit_engine(eng_name, eng):
        waited = {}
        for ins in prog.q[eng_name]:
            need = {}
            for d, v in ins.deps:
                if d.dma:
                    key = ('c', d.chan)
                    val = v
                else:
                    key = ('e', d.eng)
                    val = d.ticket
                if val > need.get(key, 0):
                    need[key] = val
            for key, val in need.items():
                if waited.get(key, 0) >= val:
                    continue
                waited[key] = val
                sem = chan_sems[key[1]] if key[0] == 'c' else sems[key[1]]
                eng.wait_ge(sem, val)
            r = ins.fn(eng)
            if ins.dma:
                r.then_inc(chan_sems[ins.chan], 16)
            elif ins.sig:
                r.then_inc(sems[ins.eng], 1)
        for (kind, name, val) in final_waits.get(eng_name, []):
            sem = chan_sems[name] if kind == 'c' else sems[name]
            eng.wait_ge(sem, val)

    @block.tensor
    def _(e):
        emit_engine('pe', e)

    @block.scalar
    def _(e):
        emit_engine('act', e)

    @block.vector
    def _(e):
        emit_engine('dve', e)

    @block.gpsimd
    def _(e):
        emit_engine('pool', e)

    @block.sync
    def _(e):
        emit_engine('sp', e)


BLK_NAMES = (['in_kv', 'in_a0', 'in_a1', 'in_pl', 'in_q', 'in_u', 'in_v']
             + [f'gate{i}_{h}' for i in range(4) for h in range(2)]
             + [f'br{i}_{h}' for i in range(4) for h in range(2)]
             + ['out_0', 'out_1']
             + [f'mi_{j}' for j in range(8)]
             + [f'mo_{cb}_{g}' for cb in range(2) for g in range(4)])
BLK_ID = {n: i for i, n in enumerate(BLK_NAMES)}
NBLK = len(BLK_NAMES)
BLK_COLS = {n: 4096 for n in BLK_NAMES}
BLK_COLS['in_kv'] = 2048
for _i in range(4):
    for _h in range(2):
        BLK_COLS[f'br{_i}_{_h}'] = 2048

V_N1G, V_N2G, V_BG, V_CW, V_CB, V_LG, V_LB, V_PS = 0, 8, 16, 48, 172, 176, 180, 184
NVEC = 188
B_QG, B_KG, B_SG, B_SB = 0, 64, 128, 640
NVBC = 1152


def _blockify(W, c0, ncols):
    K = W.shape[0]
    kc = K // 128
    b = W[:, c0:c0 + ncols].reshape(kc, 128, ncols).transpose(1, 0, 2).reshape(128, kc * ncols)
    out = np.zeros((128, 4096), np.float32)
    out[:, :kc * ncols] = b
    return out


def host_weights(inp):
    wts = np.zeros((NLAYER, NBLK, 128, 4096), np.float32)
    qperm = np.array([(j + 4 * hf) * 64 + d for j in range(4) for hf in range(2) for d in range(64)])
    for l in range(NLAYER):
        w_in = np.asarray(inp['w_in'][l])
        wq = w_in[:, 0:512][:, qperm]
        wts[l, BLK_ID['in_q']] = _blockify(wq, 0, 512)
        wts[l, BLK_ID['in_kv']] = _blockify(w_in, 512, 256)
        wts[l, BLK_ID['in_a0']] = _blockify(w_in, 768, 512)
        wts[l, BLK_ID['in_a1']] = _blockify(w_in, 1280, 512)
        wts[l, BLK_ID['in_pl']] = _blockify(w_in, 1792, 512)
        wts[l, BLK_ID['in_u']] = _blockify(w_in, 2304, 512)
        wts[l, BLK_ID['in_v']] = _blockify(w_in, 2816, 512)
        wg = np.asarray(inp['w_gate'][l])
        wb = np.asarray(inp['w_branch'][l])
        for i in range(4):
            for h in range(2):
                wts[l, BLK_ID[f'gate{i}_{h}']] = _blockify(wg, i * 1024 + h * 512, 512)
                wts[l, BLK_ID[f'br{i}_{h}']] = _blockify(wb[i], h * 512, 512)
        wo = np.asarray(inp['w_out'][l])
        for cb in range(2):
            wts[l, BLK_ID[f'out_{cb}']] = _blockify(wo, cb * 512, 512)
        wmi = np.asarray(inp['w_mlp_in'][l])
        for j in range(8):
            wts[l, BLK_ID[f'mi_{j}']] = _blockify(wmi, j * 512, 512)
        wmo = np.asarray(inp['w_mlp_out'][l])
        for cb in range(2):
            for g in range(4):
                wts[l, BLK_ID[f'mo_{cb}_{g}']] = _blockify(wmo[g * 1024:(g + 1) * 1024], cb * 512, 512)
    wmod = np.zeros((NLAYER, 12, 128, 4096), np.float32)
    for l in range(NLAYER):
        wm = np.asarray(inp['w_mod'][l])
        for j in range(12):
            wmod[l, j] = _blockify(wm, j * 512, 512)
    smallw = np.zeros((NLAYER, 128, 1024), np.float32)
    for l in range(NLAYER):
        smallw[l, :, 0:512] = np.asarray(inp['pool_w'][l]).transpose(1, 0, 2).reshape(128, 512)
        smallw[l, :, 512:1024] = np.asarray(inp['sgu_w'][l]).transpose(2, 0, 1).reshape(128, 512)
    vecT = np.zeros((128, NLAYER, NVEC), np.float32)
    vbc = np.zeros((128, NLAYER, NVBC), np.float32)
    bmodT = np.zeros((128, NLAYER, 48), np.float32)
    for l in range(NLAYER):
        vecT[:, l, V_N1G:V_N1G + 8] = np.asarray(inp['norm1_g'][l]).reshape(8, 128).T
        vecT[:, l, V_N2G:V_N2G + 8] = np.asarray(inp['norm2_g'][l]).reshape(8, 128).T
        vecT[:, l, V_BG:V_BG + 32] = np.asarray(inp['b_gate'][l]).reshape(32, 128).T
        vecT[:, l, V_CW:V_CW + 124] = np.asarray(inp['conv_w'][l]).reshape(31, 4, 128).transpose(2, 1, 0).reshape(128, 124)
        vecT[:, l, V_CB:V_CB + 4] = np.asarray(inp['conv_b'][l]).reshape(4, 128).T
        vecT[:, l, V_LG:V_LG + 4] = np.asarray(inp['conv_ln_g'][l]).reshape(4, 128).T
        vecT[:, l, V_LB:V_LB + 4] = np.asarray(inp['conv_ln_b'][l]).reshape(4, 128).T
        vecT[:, l, V_PS:V_PS + 4] = np.asarray(inp['pool_scale'][l]).reshape(4, 128).T
        vbc[:, l, B_QG:B_QG + 64] = np.asarray(inp['q_norm_g'][l])[None, :]
        vbc[:, l, B_KG:B_KG + 64] = np.asarray(inp['k_norm_g'][l])[None, :]
        vbc[:, l, B_SG:B_SG + 512] = np.asarray(inp['sgu_norm_g'][l])[None, :]
        vbc[:, l, B_SB:B_SB + 512] = np.asarray(inp['sgu_b'][l]).reshape(512)[None, :]
        bmodT[:, l, :] = np.asarray(inp['b_mod'][l]).reshape(48, 128).T
    fng = np.broadcast_to(np.asarray(inp['final_norm_g'])[None, :], (128, D)).astype(np.float32).copy()
    return dict(wts=wts.reshape(NLAYER * NBLK, 128, 4096), wmod=wmod.reshape(NLAYER * 12, 128, 4096),
                smallw=smallw, vecT=vecT, vbc=vbc, bmodT=bmodT, fng=fng)


def host_consts(NS):
    nt = NS // 128
    t = np.arange(NS)
    row = (t // 64).astype(np.float32)
    col = (t % 64).astype(np.float32)
    inv = (10000.0 ** (-np.arange(0, 32, 2, dtype=np.float32) / 32)).astype(np.float32)
    ang = np.concatenate([row[:, None] * inv, col[:, None] * inv], axis=-1).astype(np.float32)
    cos = np.cos(ang).astype(np.float32).reshape(nt, 128, 32).transpose(1, 0, 2).copy()
    sin = np.sin(ang).astype(np.float32).reshape(nt, 128, 32).transpose(1, 0, 2).copy()
    corr = np.ones((128, 4, 2, 8), np.float32)
    for g in range(4):
        w = 2 ** (g + 1)
        for i in range(8):
            if i < w // 2:
                corr[:, g, 0, i] = w / (i + w // 2)
            e = 7 - i
            if e < w // 2 - 1:
                corr[:, g, 1, i] = w / (e + 1 + w // 2)
    ident = np.eye(128, dtype=np.float32)
    return dict(ropec=cos, ropes=sin, pcorr=corr, ident=ident)


def build_program(NS):
    nc = bass.Bass("TRN2", target_bir_lowering=False)
    NTS = NS // 128
    NK = NS + PAST
    NKT = NK // 128
    NPT = 2 * SEQ_P

    def din(name, shape, dt=F32):
        return nc.dram_tensor(name, shape, dt, kind="ExternalInput").ap()

    def dout(name, shape):
        return nc.dram_tensor(name, shape, F32, kind="ExternalOutput").ap()

    xp_d = din("xp", [NPT, D])
    xs_d = din("xs", [NS, D])
    ck_d = din("ck", [NLAYER, PAST, 128])
    cv_d = din("cv", [NLAYER, PAST, 128])
    cvec_d = din("cvec", [128, 8, 2])
    wts_d = din("wts", [NLAYER * NBLK, 128, 4096])
    wmod_d = din("wmod", [NLAYER * 12, 128, 4096])
    smallw_d = din("smallw", [NLAYER, 128, 1024])
    vecT_d = din("vecT", [128, NLAYER, NVEC])
    vbc_d = din("vbc", [128, NLAYER, NVBC])
    bmodT_d = din("bmodT", [128, NLAYER, 48])
    fng_d = din("fng", [128, D])
    ropec_d = din("ropec", [128, NTS, 32])
    ropes_d = din("ropes", [128, NTS, 32])
    pcorr_d = din("pcorr", [128, 4, 2, 8])
    ident_d = din("ident", [128, 128])
    yp_d = dout("yp", [NPT, D])
    ys_d = dout("ys", [NS, D])
    nk_d = dout("nk", [2, NLAYER, SEQ_P, 128])
    nv_d = dout("nv", [2, NLAYER, SEQ_P, 128])
    x1p_d = nc.dram_tensor("x1p", [NPT, D], F32).ap()
    x1s_d = nc.dram_tensor("x1s", [NS, D], F32).ap()
    aT_d = nc.dram_tensor("aT_s", [4, 128, NS + 32], BF16).ap()
    plT_d = nc.dram_tensor("plT_s", [4, 128, NS + 16], BF16).ap()

    es = ExitStack()
    with es:
        def sb(name, shape, dt):
            return es.enter_context(nc.sbuf_tensor(name, shape, dt))

        wslot = [sb(f"wslot{i}", [128, 4096], BF16) for i in range(NSLOT)]
        x_sb = sb("x_sb", [128, 4, D], F32)
        hT = sb("hT", [128, KC, 512], BF16)
        KT = sb("KT", [128, NK], BF16)
        V1 = sb("V1", [128, NKT, 2, 128], BF16)
        QT = sb("QT", [128, 8, 512], BF16)
        merged = sb("merged", [128, 8, 512], F32)
        merged_bf = sb("merged_bf", [128, 8, 512], BF16)
        gbc = sb("gbc", [128, 2, D], F32)
        fng = sb("fng_sb", [128, D], F32)
        ropec = sb("ropec_sb", [128, NTS, 32], F32)
        ropes = sb("ropes_sb", [128, NTS, 32], F32)
        vbc = sb("vbc_sb", [128, NVBC], F32)
        vecT = sb("vecT_sb", [128, NLAYER, NVEC], F32)
        bmodT = sb("bmodT_sb", [128, NLAYER, 48], F32)
        modT = sb("modT", [128, NLAYER, 48, 2], F32)
        gmT = sb("gmT", [128, NLAYER, 2, 8, 2], F32)
        cvec = sb("cvec_sb", [128, 8, 2], F32)
        csilu = sb("csilu", [128, 8, 2], F32)
        pcorr = sb("pcorr_sb", [128, 4, 2, 8], F32)
        identf = sb("identf", [128, 128], F32)
        identb = sb("identb", [128, 128], BF16)
        onesf = sb("onesf", [128, 128], F32)
        epsT = sb("epsT", [128, 1], F32)
        zeroT = sb("zeroT", [128, 64], BF16)
        smallw = sb("smallw_sb", [128, 1024], BF16)
        stat = sb("stat", [128, 64], F32)
        FA = sb("FA", [128, 8, 512], F32)
        FB = sb("FB", [128, 6, 512], F32)
        G = sb("Gt", [128, 3, 4, 512], BF16)
        q_bf = sb("q_bf", [128, 512], BF16)
        pT = sb("pT", [128, 3, 512], BF16)
        awin = sb("awin", [128, 4, 576], BF16)
        plwin = sb("plwin", [128, 4, 544], BF16)
        kvout = awin[:, :, :].rearrange("p a b -> p (a b)").bitcast(F32)[:, 0:1024].rearrange("p (s t c) -> p s t c", s=4, t=2)
        pooled_bf = sb("pooled_bf", [128, 512], BF16)
        psum = [es.enter_context(nc.psum_tensor(f"ps{i}", [128, 512], F32)) for i in range(8)]

        sems = {e: es.enter_context(nc.semaphore(f"s_{e}")) for e in COMPUTE}
        chan_names = ([f'w{i}' for i in range(NSLOT)] + ['x0', 'x1', 'x2', 'x3', 'xo0', 'xo1', 'xo2', 'xo3', 'const', 'kvc', 'kvo', 'ast', 'pst', 'awin', 'plwin',
                                                          'wmod0', 'wmod1', 'small', 'pad', 'vbc', 'xf'])
        csems = {c: es.enter_context(nc.semaphore(f"c_{c}")) for c in chan_names}
        block = es.enter_context(nc.Block())

        P = Prog()
        B = {}
        cur = {'l': 0}

        def buf(name):
            if name not in B:
                B[name] = Buf(name)
            return B[name]

        bFAh = [buf(f'FAh{i}') for i in range(16)]
        bFA = [(bFAh[2 * i], bFAh[2 * i + 1]) for i in range(8)]
        bFB = [buf(f'FB{i}') for i in range(6)]
        bG = [buf('G0'), buf('G1'), buf('G2')]
        bPS = [buf(f'PS{i}') for i in range(8)]
        bX = [buf(f'X{i}') for i in range(4)]
        bHT = [buf(f'HT{i}') for i in range(8)]
        bW = [buf(f'W{i}') for i in range(NSLOT)]
        bPT = [buf(f'pT{i}') for i in range(3)]
        bST = [buf(f'st{i}') for i in range(64)]

        def st(c0, n=1):
            return bST[c0:c0 + n]

        def mm(out, lhsT, rhs, start, stop, reads, writes):
            P.op('pe', lambda e: e.matmul(out, lhsT=lhsT, rhs=rhs, start=start, stop=stop), reads, writes)

        def tr(out, in_, ident, reads, writes):
            P.op('pe', lambda e: e.transpose(out, in_, ident), reads, writes)

        def act(out, in_, func, reads, writes, bias=None, scale=None):
            kw = {}
            if bias is not None:
                kw['bias'] = bias
            if scale is not None:
                kw['scale'] = scale
            P.op('act', lambda e: e.activation(out=out, in_=in_, func=func, **kw), reads, writes)

        def tt(out, in0, in1, op, reads, writes, eng='dve'):
            P.op(eng, lambda e: e.tensor_tensor(out=out, in0=in0, in1=in1, op=op), reads, writes)

        def ts(out, in0, s1, s2, op0, op1, reads, writes, eng='dve'):
            if op1 is None:
                P.op(eng, lambda e: e.tensor_scalar(out=out, in0=in0, scalar1=s1, scalar2=None, op0=op0), reads, writes)
            else:
                P.op(eng, lambda e: e.tensor_scalar(out=out, in0=in0, scalar1=s1, scalar2=s2, op0=op0, op1=op1), reads, writes)

        def stt(out, in0, scalar, in1, op0, op1, reads, writes):
            P.op('dve', lambda e: e.scalar_tensor_tensor(out=out, in0=in0, scalar=scalar, in1=in1, op0=op0, op1=op1), reads, writes)

        def cp(out, in_, reads, writes, eng='dve'):
            if eng == 'act_copy':
                P.op('act', lambda e: e.activation(out=out, in_=in_, func=AF.Copy), reads, writes)
            else:
                P.op(eng, lambda e: e.tensor_copy(out=out, in_=in_), reads, writes)

        def sumsq(junk, in_, acc, reads, writes):
            P.op('act', lambda e: e.activation(out=junk, in_=in_, func=AF.Square, accum_out=acc), reads, writes)

        def recip(out, in_, reads, writes):
            P.op('dve', lambda e: e.reciprocal(out=out, in_=in_), reads, writes)

        def dma(q, out, in_, reads, writes, chan):
            P.op(q, lambda e: e.dma_start(out=out, in_=in_), reads, writes, dma=True, chan=chan)

        class PSPool:
            def __init__(self):
                self.free = list(range(8))

            def get(self):
                i = self.free.pop(0)
                return i

            def put(self, i):
                self.free.append(i)
        PSP = PSPool()

        class WStream:
            def __init__(self):
                self.sched = []
                self.pos = 0
                self.issued = 0

            def _issue(self, k):
                l, name = self.sched[k]
                s = k % NSLOT
                ncol = BLK_COLS[name]
                dma('pool', wslot[s][:, 0:ncol], wts_d[l * NBLK + BLK_ID[name], :, 0:ncol], [], [bW[s]], f'w{s}')

            def need(self, l, name):
                if P.dry:
                    self.sched.append((l, name))
                    return 0
                k = self.pos
                assert self.sched[k] == (l, name), (self.sched[k], l, name)
                lim = min(len(self.sched), k + NSLOT - 1)
                while self.issued < lim:
                    self._issue(self.issued)
                    self.issued += 1
                self.pos += 1
                return k % NSLOT
        WS = WStream()

        def wview(s, kc, ncols):
            return wslot[s][:, 0:kc * ncols].rearrange("p (k c) -> p k c", k=kc)

        def setup():
            dma('sp', identf[:, :], ident_d[:, :], [], [buf('identf')], 'const')
            dma('sp', cvec[:, :, :], cvec_d[:, :, :], [], [buf('cvec')], 'const')
            dma('sp', bmodT[:, :, :], bmodT_d[:, :, :], [], [buf('bmodT')], 'const')
            dma('sp', vecT[:, :, :], vecT_d[:, :, :], [], [buf('vecT')], 'const')
            dma('sp', ropec[:, :, :], ropec_d[:, :, :], [], [buf('ropec')], 'const')
            dma('sp', ropes[:, :, :], ropes_d[:, :, :], [], [buf('ropes')], 'const')
            dma('sp', pcorr[:, :, :, :], pcorr_d[:, :, :, :], [], [buf('pcorr')], 'const')
            dma('sp', fng[:, :], fng_d[:, :], [], [buf('fng')], 'const')
            cp(identb[:, :], identf[:, :], [buf('identf')], [buf('identb')])
            P.op('dve', lambda e: e.memset(onesf[:, :], 1.0), [], [buf('onesf')])
            P.op('dve', lambda e: e.memset(epsT[:, :], EPS), [], [buf('epsT')])
            P.op('dve', lambda e: e.memset(zeroT[:, :], 0.0), [], [buf('zeroT')])
            P.op('dve', lambda e: e.memset(V1[:, :, :, 64:128], 1.0), [], [buf('V1')])
            P.op('dve', lambda e: e.memset(QT[:, :, :], 0.0), [], [buf('QT')])
            act(csilu[:, :, :], cvec[:, :, :], AF.Silu, [buf('cvec')], [buf('csilu')])
            stages = [(FA[:, :, :].rearrange("p a b -> p (a b)"), bFA), (x_sb[:, :, :].rearrange("p a b -> p (a b)"), bX)]
            nblk = 0
            for l in range(NLAYER):
                pb = PSP.get()
                for j12 in range(12):
                    wst, wbufs = stages[nblk % 2]
                    dma('sp', wst, wmod_d[l * 12 + j12, :, :], [], wbufs, f'wmod{nblk % 2}')
                    nblk += 1
                    wv = wst.rearrange("p (k c) -> p k c", k=8)
                    for jj in range(4):
                        j = j12 * 4 + jj
                        for kc in range(KC):
                            mm(psum[pb][:, 2 * j:2 * j + 2], wv[:, kc, jj * 128:(jj + 1) * 128], csilu[:, kc, :],
                               kc == 0, kc == KC - 1, wbufs + [buf('csilu')], [bPS[pb]])
                tt(modT[:, l, :, :], psum[pb][:, 0:96].rearrange("p (j t) -> p j t", t=2),
                   bmodT[:, l, :].unsqueeze(2).to_broadcast([128, 48, 2]), ALU.add,
                   [bPS[pb], buf('bmodT')], [buf('modT')])
                PSP.put(pb)
                for n, (sc0, g0) in enumerate([(8, V_N1G), (32, V_N2G)]):
                    stt(gmT[:, l, n, :, :], modT[:, l, sc0:sc0 + 8, :], 1.0,
                        vecT[:, l, g0:g0 + 8].unsqueeze(2).to_broadcast([128, 8, 2]), ALU.add, ALU.mult,
                        [buf('modT'), buf('vecT')], [buf('gmT')])

        def layer_setup(l):
            dma('sp', vbc[:, :], vbc_d[:, l, :], [], [buf('vbc')], 'vbc')
            dma('pool', smallw[:, :], smallw_d[l, :, :], [], [buf('smallw')], 'small')

        def type_setup(l, t):
            for n, j0 in enumerate([16, 40]):
                for kc in range(8):
                    d = n * 8 + kc
                    ts(FB[:, d // 4, (d % 4) * 128:(d % 4 + 1) * 128], identf[:, :], modT[:, l, j0 + kc, t:t + 1], None, ALU.mult, None,
                       [buf('identf'), buf('modT')], [bFB[d // 4]])
            for n in range(2):
                for half in range(2):
                    pb = PSP.get()
                    for q4 in range(4):
                        d = n * 8 + half * 4 + q4
                        mm(psum[pb][:, q4 * 128:(q4 + 1) * 128], onesf[:, :], FB[:, d // 4, (d % 4) * 128:(d % 4 + 1) * 128], True, True,
                           [buf('onesf'), bFB[d // 4]], [bPS[pb]])
                    cp(gbc[:, n, half * 512:(half + 1) * 512], psum[pb][:, :], [bPS[pb]], [buf('gbc')])
                    PSP.put(pb)

        def load_x(src_d, tok0, nsub, kind):
            rd = [buf('x1' + kind)] if cur['l'] > 0 else []
            for s in range(nsub):
                dma('sp', x_sb[:, s, :], src_d[tok0 + s * 128:tok0 + (s + 1) * 128, :], rd, [bX[s]], f'x{s}')

        def rstd_from_ssq(nsub, scale):
            act(stat[:, 4:4 + nsub], stat[:, 0:nsub], AF.Sqrt, st(0, nsub), st(4, nsub), bias=epsT[:, 0:1], scale=scale)
            recip(stat[:, 8:8 + nsub], stat[:, 4:4 + nsub], st(4, nsub), st(8, nsub))

        xf = {'have': False}
        FBj = FB[:, 4:6, :].rearrange("p a b -> p (a b)")

        xn = {'ready': False}

        def norm_elem_from_x(nsub):
            for s in range(nsub):
                sumsq(FA[:, 2 * s:2 * s + 2, :].rearrange("p a b -> p (a b)"), x_sb[:, s, :], stat[:, s:s + 1],
                      [bX[s]], [bFA[2 * s], bFA[2 * s + 1]] + st(s))
            rstd_from_ssq(nsub, 1.0 / D)
            for s in range(nsub):
                act(FA[:, 2 * s:2 * s + 2, :].rearrange("p a b -> p (a b)"), x_sb[:, s, :], AF.Copy,
                    [bX[s]] + st(8 + s), [bFA[2 * s], bFA[2 * s + 1]], scale=stat[:, 8 + s:9 + s])

        def norm_elem_from_fa(nsub):
            for s in range(nsub):
                fa = FA[:, 2 * s:2 * s + 2, :].rearrange("p a b -> p (a b)")
                sumsq(FBj, fa, stat[:, s:s + 1], [bFA[2 * s], bFA[2 * s + 1]], [bFB[4], bFB[5]] + st(s))
            rstd_from_ssq(nsub, 1.0 / D)
            for s in range(nsub):
                fa = FA[:, 2 * s:2 * s + 2, :].rearrange("p a b -> p (a b)")
                act(fa, fa, AF.Copy, [bFA[2 * s], bFA[2 * s + 1]] + st(8 + s), [bFA[2 * s], bFA[2 * s + 1]], scale=stat[:, 8 + s:9 + s])

        def norm_to_hT(l, t, n, nsub, skip_elem=False):
            Tm = nsub * 128
            if skip_elem:
                pass
            elif n == 0 and xn['ready']:
                xn['ready'] = False
            elif n == 0 and xf['have']:
                xf['have'] = False
                norm_elem_from_fa(nsub)
            else:
                norm_elem_from_x(nsub)
            for kc in range(KC):
                pb = PSP.get()
                for s in range(nsub):
                    xin = FA[:, 2 * s + kc // 4, (kc % 4) * 128:(kc % 4 + 1) * 128]
                    tr(psum[pb][:, s * 128:(s + 1) * 128], xin, identf[:, :], [bFA[2 * s + kc // 4], buf('identf')], [bPS[pb]])
                sh = modT[:, l, (0 if n == 0 else 24) + kc, t:t + 1]
                gm = gmT[:, l, n, kc, t:t + 1]
                if kc % 2 == 0:
                    act(hT[:, kc, 0:Tm], psum[pb][:, 0:Tm], AF.Identity, [bPS[pb], buf('gmT'), buf('modT')], [bHT[kc]],
                        bias=sh, scale=gm)
                else:
                    ts(hT[:, kc, 0:Tm], psum[pb][:, 0:Tm], gm, sh, ALU.mult, ALU.add,
                       [bPS[pb], buf('gmT'), buf('modT')], [bHT[kc]])
                PSP.put(pb)

        def headnorm_rope(src_ps, pbuf, nh, g_off, gscale, rope_tile, out_ap, out_bufs, tmpi):
            W = nh * 64
            f_sq, f_n, f_r0, f_r1 = tmpi
            act(FB[:, f_sq, 0:W], src_ps, AF.Square, [pbuf], [bFB[f_sq]])
            P.op('dve', lambda e: e.tensor_reduce(out=stat[:, 16:16 + nh], in_=FB[:, f_sq, 0:W].rearrange("p (h d) -> p h d", h=nh),
                                                  axis=AX.X, op=ALU.add), [bFB[f_sq]], st(16, nh))
            act(stat[:, 24:24 + nh], stat[:, 16:16 + nh], AF.Sqrt, st(16, nh), st(24, nh), bias=epsT[:, 0:1], scale=1.0 / 64)
            recip(stat[:, 32:32 + nh], stat[:, 24:24 + nh], st(24, nh), st(32, nh))
            qn = FB[:, f_n, 0:W].rearrange("p (h d) -> p h d", h=nh)
            tt(qn, src_ps.rearrange("p (h d) -> p h d", h=nh), stat[:, 32:32 + nh].unsqueeze(2).to_broadcast([128, nh, 64]),
               ALU.mult, [pbuf] + st(32, nh), [bFB[f_n]])
            gb = vbc[:, g_off:g_off + 64].unsqueeze(1).to_broadcast([128, nh, 64])
            if rope_tile is None:
                stt(out_ap.rearrange("p (h d) -> p h d", h=nh), qn, gscale, gb, ALU.mult, ALU.mult,
                    [bFB[f_n], buf('vbc')], out_bufs)
                return
            stt(qn, qn, gscale, gb, ALU.mult, ALU.mult, [bFB[f_n], buf('vbc')], [bFB[f_n]])
            q4 = FB[:, f_n, 0:W].rearrange("p (h j t) -> p h j t", h=nh, t=2)
            x0 = q4[:, :, :, 0]
            x1 = q4[:, :, :, 1]
            cb_ = ropec[:, rope_tile, :].unsqueeze(1).to_broadcast([128, nh, 32])
            sb_ = ropes[:, rope_tile, :].unsqueeze(1).to_broadcast([128, nh, 32])
            H = nh * 32
            t1 = FB[:, f_r0, 0:H].rearrange("p (h j) -> p h j", h=nh)
            t2 = FB[:, f_r0, 256:256 + H].rearrange("p (h j) -> p h j", h=nh)
            t3 = FB[:, f_r1, 0:H].rearrange("p (h j) -> p h j", h=nh)
            t4 = FB[:, f_r1, 256:256 + H].rearrange("p (h j) -> p h j", h=nh)
            o4 = out_ap.rearrange("p (h j t) -> p h j t", h=nh, t=2)
            rd = [bFB[f_n], buf('ropec'), buf('ropes')]
            tt(t1, x0, cb_, ALU.mult, rd, [bFB[f_r0]])
            tt(t2, x1, sb_, ALU.mult, rd, [bFB[f_r0]])
            tt(t3, x0, sb_, ALU.mult, rd, [bFB[f_r1]])
            tt(t4, x1, cb_, ALU.mult, rd, [bFB[f_r1]])
            tt(o4[:, :, :, 0], t1, t2, ALU.subtract, [bFB[f_r0]], out_bufs)
            tt(o4[:, :, :, 1], t3, t4, ALU.add, [bFB[f_r1]], out_bufs)

        def seq_setup(l, kind, N):
            aTv = aT_d.rearrange("j p n -> p j n")
            plv = plT_d.rearrange("j p n -> p j n")
            z16 = zeroT[:, 0:64].rearrange("p (j n) -> p j n", j=4)
            z8 = zeroT[:, 0:32].rearrange("p (j n) -> p j n", j=4)
            nseg = 2 if kind == 'P' else 1
            L = SEQ_P if kind == 'P' else N
            for g in range(nseg):
                a0 = g * (L + 32)
                p0 = g * (L + 16)
                dma('sp', aTv[:, :, a0:a0 + 16], z16, [buf('zeroT')], [buf('aT_d')], 'pad')
                dma('sp', aTv[:, :, a0 + 16 + L:a0 + 32 + L], z16, [buf('zeroT')], [buf('aT_d')], 'pad')
                dma('sp', plv[:, :, p0:p0 + 8], z8, [buf('zeroT')], [buf('plT_d')], 'pad')
                dma('sp', plv[:, :, p0 + 8 + L:p0 + 16 + L], z8, [buf('zeroT')], [buf('plT_d')], 'pad')
            if kind == 'S':
                dma('sp', FA[:, 0, :].rearrange("p (k c) -> p k c", k=4), ck_d[l].rearrange("(k p) c -> p k c", p=128),
                    [], [bFA[0]], 'kvc')
                dma('sp', FA[:, 1, :].rearrange("p (k c) -> p k c", k=4), cv_d[l].rearrange("(k p) c -> p k c", p=128),
                    [], [bFA[1]], 'kvc')
                cp(q_bf[:, :], FA[:, 0, :], [bFA[0]], [buf('q_bf')])
                pb = PSP.get()
                pbv = psum[pb][:, :].bitcast(BF16)
                for k in range(4):
                    tr(pbv[:, k * 128:(k + 1) * 128], q_bf[:, k * 128:(k + 1) * 128], identb[:, :], [buf('q_bf'), buf('identb')], [bPS[pb]])
                cp(KT[:, 0:512], pbv[:, 0:512], [bPS[pb]], [buf('KT')])
                PSP.put(pb)
                cp(V1[:, 0:4, :, 0:64], FA[:, 1, :].rearrange("p (k v d) -> p k v d", k=4, v=2), [bFA[1]], [buf('V1')])

        xpre = {'have': False}

        def load_x_once(src_d, tok0, nsub, kind):
            if xpre['have']:
                xpre['have'] = False
            else:
                load_x(src_d, tok0, nsub, kind)

        def phaseA(l, kind, t, src_d, tok0, nsub, key0, segs, nxt):
            Tm = nsub * 128
            load_x_once(src_d, tok0, nsub, kind)
            norm_to_hT(l, t, 0, nsub)
            if nxt is not None:
                load_x(src_d, nxt[0], nxt[1], kind)
                xpre['have'] = True
            s_kv = WS.need(l, 'in_kv')
            wkv = wview(s_kv, 8, 256)
            kvb = []
            for s in range(nsub):
                pb = PSP.get()
                kvb.append(pb)
                for kc in range(KC):
                    mm(psum[pb][:, 0:256], hT[:, kc, s * 128:(s + 1) * 128], wkv[:, kc, :], kc == 0, kc == KC - 1,
                       bHT + [bW[s_kv]], [bPS[pb]])
            kt0 = (key0 + tok0) // 128
            for s in range(nsub):
                pb = kvb[s]
                gt = (tok0 // 128 + s) if kind == 'S' else None
                headnorm_rope(psum[pb][:, 0:128], bPS[pb], 2, B_KG, 1.0, gt, kvout[:, s, 0, :], [buf('awin')], (0, 1, 2, 3))
                cp(kvout[:, s, 1, :], psum[pb][:, 128:256], [bPS[pb]], [buf('awin')], eng='act_copy')
                PSP.put(pb)
                cp(V1[:, kt0 + s, :, 0:64], kvout[:, s, 1, :].rearrange("p (v d) -> p v d", v=2), [buf('awin')], [buf('V1')])
                cp(q_bf[:, s * 128:(s + 1) * 128], kvout[:, s, 0, :], [buf('awin')], [buf('q_bf')])
            if kind == 'P':
                for g in range(2):
                    dma('sp', nk_d[g, l, :, :].rearrange("(s p) c -> p s c", p=128), kvout[:, 2 * g:2 * g + 2, 0, :],
                        [buf('awin')], [], 'kvo')
                    dma('sp', nv_d[g, l, :, :].rearrange("(s p) c -> p s c", p=128), kvout[:, 2 * g:2 * g + 2, 1, :],
                        [buf('awin')], [], 'kvo')
            s_a0 = WS.need(l, 'in_a0')
            wa0 = wview(s_a0, 8, 512)
            s_a1 = WS.need(l, 'in_a1')
            wa1 = wview(s_a1, 8, 512)
            for j in range(4):
                pa = PSP.get()
                for kc in range(KC):
                    mm(psum[pa][:, 0:Tm], wa0[:, kc, j * 128:(j + 1) * 128], hT[:, kc, 0:Tm], kc == 0, kc == KC - 1,
                       bHT + [bW[s_a0]], [bPS[pa]])
                pb = PSP.get()
                for kc in range(KC):
                    mm(psum[pb][:, 0:Tm], wa1[:, kc, j * 128:(j + 1) * 128], hT[:, kc, 0:Tm], kc == 0, kc == KC - 1,
                       bHT + [bW[s_a1]], [bPS[pb]])
                fb = 4 + (j % 2)
                act(FB[:, fb, 0:Tm], psum[pb][:, 0:Tm], AF.Sigmoid, [bPS[pb]], [bFB[fb]])
                PSP.put(pb)
                tt(G[:, 0, j, 0:Tm], psum[pa][:, 0:Tm], FB[:, fb, 0:Tm], ALU.mult, [bPS[pa], bFB[fb]], [bG[0]])
                PSP.put(pa)
            for sg in segs:
                dma('sp', aT_d.rearrange("j p n -> p j n")[:, :, sg['apos']:sg['apos'] + sg['n']], G[:, 0, :, sg['col0']:sg['col0'] + sg['n']],
                    [bG[0]], [buf('aT_d')], 'ast')
            if nxt is not None:
                norm_elem_from_x(nxt[1])
                xn['ready'] = True
            s_pl = WS.need(l, 'in_pl')
            wpl = wview(s_pl, 8, 512)
            for j in range(4):
                pb = PSP.get()
                for kc in range(KC):
                    mm(psum[pb][:, 0:Tm], wpl[:, kc, j * 128:(j + 1) * 128], hT[:, kc, 0:Tm], kc == 0, kc == KC - 1,
                       bHT + [bW[s_pl]], [bPS[pb]])
                if j % 2 == 0:
                    cp(G[:, 1, j, 0:Tm], psum[pb][:, 0:Tm], [bPS[pb]], [bG[1]])
                else:
                    cp(G[:, 1, j, 0:Tm], psum[pb][:, 0:Tm], [bPS[pb]], [bG[1]], eng='act_copy')
                PSP.put(pb)
            for sg in segs:
                dma('sp', plT_d.rearrange("j p n -> p j n")[:, :, sg['ppos']:sg['ppos'] + sg['n']], G[:, 1, :, sg['col0']:sg['col0'] + sg['n']],
                    [bG[1]], [buf('plT_d')], 'pst')
            pb2 = PSP.get()
            pbv = psum[pb2][:, :].bitcast(BF16)
            for s in range(nsub):
                tr(pbv[:, s * 128:(s + 1) * 128], q_bf[:, s * 128:(s + 1) * 128], identb[:, :], [buf('q_bf'), buf('identb')], [bPS[pb2]])
            cp(KT[:, kt0 * 128:(kt0 + nsub) * 128], pbv[:, 0:nsub * 128], [bPS[pb2]], [buf('KT')])
            PSP.put(pb2)

        def gate_and_project(l, i, gi, Tm, pos, pump=None):
            for half in range(2):
                s_b = WS.need(l, f'br{i}_{half}')
                wb = wview(s_b, 4, 512)
                s_g = WS.need(l, f'gate{i}_{half}')
                wg = wview(s_g, 8, 512)
                for dq in range(4):
                    dc = half * 4 + dq
                    if pump is not None:
                        pump()
                    pg = PSP.get()
                    for kc in range(KC):
                        mm(psum[pg][:, 0:Tm], wg[:, kc, dq * 128:(dq + 1) * 128], hT[:, kc, 0:Tm], kc == 0, kc == KC - 1,
                           bHT + [bW[s_g]], [bPS[pg]])
                    fg = dc % 2
                    act(FB[:, fg, 0:Tm], psum[pg][:, 0:Tm], AF.Sigmoid, [bPS[pg], buf('vecT')], [bFB[fg]],
                        bias=vecT[:, l, V_BG + i * 8 + dc:V_BG + i * 8 + dc + 1])
                    PSP.put(pg)
                    pb = PSP.get()
                    for kc in range(4):
                        mm(psum[pb][:, 0:Tm], wb[:, kc, dq * 128:(dq + 1) * 128], G[:, gi, kc, 0:Tm], kc == 0, kc == 3,
                           [bG[gi], bW[s_b]], [bPS[pb]])
                    bm = buf(f'merged{dc}')
                    if pos == 'first':
                        tt(merged[:, dc, 0:Tm], FB[:, fg, 0:Tm], psum[pb][:, 0:Tm], ALU.mult, [bFB[fg], bPS[pb]], [bm])
                    else:
                        ft = 2 + dc % 2
                        tt(FB[:, ft, 0:Tm], FB[:, fg, 0:Tm], psum[pb][:, 0:Tm], ALU.mult, [bFB[fg], bPS[pb]], [bFB[ft]])
                        if pos != 'last':
                            tt(merged[:, dc, 0:Tm], merged[:, dc, 0:Tm], FB[:, ft, 0:Tm], ALU.add, [bm, bFB[ft]], [bm])
                        else:
                            tt(merged_bf[:, dc, 0:Tm], merged[:, dc, 0:Tm], FB[:, ft, 0:Tm], ALU.add, [bm, bFB[ft]],
                               [buf(f'mbf{dc}')])
                    PSP.put(pb)

        def phaseB(l, kind, t, src_d, dst_d, tok0, nsub, segs, final, nxt):
            Tm = nsub * 128
            nsg = len(segs)
            wa0_ = segs[0]['apos'] - 16
            wp0_ = segs[0]['ppos'] - 8
            load_x_once(src_d, tok0, nsub, kind)
            dma('sp', awin[:, :, 0:Tm + 32 * nsg], aT_d.rearrange("j p n -> p j n")[:, :, wa0_:wa0_ + Tm + 32 * nsg], [buf('aT_d')], [buf('awin')], 'awin')
            dma('sp', plwin[:, :, 0:Tm + 16 * nsg], plT_d.rearrange("j p n -> p j n")[:, :, wp0_:wp0_ + Tm + 16 * nsg], [buf('plT_d')], [buf('plwin')], 'plwin')
            norm_to_hT(l, t, 0, nsub)
            s_q = WS.need(l, 'in_q')
            wq = wview(s_q, 8, 512)
            qb = []
            for s in range(nsub):
                pb = PSP.get()
                qb.append(pb)
                for kc in range(KC):
                    mm(psum[pb][:, :], hT[:, kc, s * 128:(s + 1) * 128], wq[:, kc, :], kc == 0, kc == KC - 1,
                       bHT + [bW[s_q]], [bPS[pb]])
            for s in range(nsub):
                gt = (tok0 // 128 + s) if kind == 'S' else None
                headnorm_rope(psum[qb[s]][:, :], bPS[qb[s]], 8, B_QG, 0.125, gt, G[:, 2, s, :], [bG[2]], (2, 3, 4, 5))
                PSP.put(qb[s])
            s_u = WS.need(l, 'in_u')
            wu = wview(s_u, 8, 512)
            for j in range(4):
                pb = PSP.get()
                for kc in range(KC):
                    mm(psum[pb][:, 0:Tm], wu[:, kc, j * 128:(j + 1) * 128], hT[:, kc, 0:Tm], kc == 0, kc == KC - 1,
                       bHT + [bW[s_u]], [bPS[pb]])
                act(FA[:, j, 0:Tm], psum[pb][:, 0:Tm], AF.Gelu_apprx_tanh, [bPS[pb]], [bFA[j]])
                PSP.put(pb)
            s_v = WS.need(l, 'in_v')
            wv = wview(s_v, 8, 512)
            vn = FA[:, 6:8, :].rearrange("p a b -> p (a b)").bitcast(BF16).rearrange("p (s c) -> p s c", s=4)
            vb = []
            for s in range(nsub):
                pb = PSP.get()
                vb.append(pb)
                for kc in range(KC):
                    mm(psum[pb][:, :], hT[:, kc, s * 128:(s + 1) * 128], wv[:, kc, :], kc == 0, kc == KC - 1,
                       bHT + [bW[s_v]], [bPS[pb]])
            gel = [(FA[:, 4, :], bFA[4]), (FA[:, 5, :], bFA[5]), (FB[:, 4, :], bFB[4]), (FB[:, 5, :], bFB[5])]
            for s in range(nsub):
                act(gel[s][0], psum[vb[s]][:, :], AF.Gelu_apprx_tanh, [bPS[vb[s]]], [gel[s][1]])
                PSP.put(vb[s])
            for s in range(nsub):
                pb2 = PSP.get()
                pbv = psum[pb2][:, :].bitcast(BF16)
                for j in range(4):
                    tr(pbv[:, j * 128:(j + 1) * 128], G[:, 2, s, j * 128:(j + 1) * 128], identb[:, :], [bG[2], buf('identb')], [bPS[pb2]])
                cp(QT[0:64, 0:4, s * 128:(s + 1) * 128], pbv[0:64, 0:512].rearrange("p (j c) -> p j c", j=4), [bPS[pb2]], [buf('QT')])
                cp(QT[64:128, 4:8, s * 128:(s + 1) * 128], pbv[64:128, 0:512].rearrange("p (j c) -> p j c", j=4), [bPS[pb2]], [buf('QT')],
                   eng='act_copy')
                PSP.put(pb2)
            v_ops = []

            def v_stage_a():
                for s in range(nsub):
                    ft = s % 2
                    tt(FB[:, ft, :], gel[s][0], gel[s][0], ALU.mult, [gel[s][1]], [bFB[ft]])
                    P.op('dve', lambda e, s=s, ft=ft: e.tensor_reduce(out=stat[:, 40 + s:41 + s], in_=FB[:, ft, :], axis=AX.X, op=ALU.add),
                         [bFB[ft]], st(40 + s))

            def v_stage_b():
                act(stat[:, 44:44 + nsub], stat[:, 40:40 + nsub], AF.Sqrt, st(40, nsub), st(44, nsub), bias=epsT[:, 0:1], scale=1.0 / 512)
                recip(stat[:, 48:48 + nsub], stat[:, 44:44 + nsub], st(44, nsub), st(48, nsub))
                for s in range(nsub):
                    stt(vn[:, s, :], gel[s][0], stat[:, 48 + s:49 + s], vbc[:, B_SG:B_SG + 512], ALU.mult, ALU.mult,
                        [gel[s][1], buf('vbc')] + st(48 + s), [bFA[6], bFA[7]])
            v_ops.append(v_stage_a)
            v_ops.append(v_stage_b)
            def sgu_spatial():
                swT = smallw[:, 512:1024].rearrange("p (g i) -> p g i", g=4)
                sgb = vbc[:, B_SB:B_SB + 512].rearrange("p (g i) -> p g i", g=4)
                for g in range(4):
                    pb = PSP.get()
                    for s in range(nsub):
                        mm(psum[pb][:, s * 128:(s + 1) * 128], vn[:, s, g * 128:(g + 1) * 128], swT[:, g, :], True, True,
                           [bFA[6], bFA[7], buf('smallw')], [bPS[pb]])
                    ft = g % 2
                    tt(FB[:, ft, 0:Tm].rearrange("p (s i) -> p s i", s=nsub), psum[pb][:, 0:Tm].rearrange("p (s i) -> p s i", s=nsub),
                       sgb[:, g, :].unsqueeze(1).to_broadcast([128, nsub, 128]), ALU.add, [bPS[pb], buf('vbc')], [bFB[ft]])
                    PSP.put(pb)
                    tt(G[:, 2, g, 0:Tm], FB[:, ft, 0:Tm], FA[:, g, 0:Tm], ALU.mult, [bFB[ft], bFA[g]], [bG[2]])
            cw = vecT[:, l, V_CW:V_CW + 124].rearrange("p (c j) -> p c j", c=4)
            conv_ops = []
            if nsg == 2:
                n2 = segs[0]['n']
                for j in range(31):
                    for c in range(4):
                        outv = FA[:, c, 0:2 * n2].rearrange("p (g n) -> p g n", g=2)
                        inv = awin[:, c, 0:2 * (n2 + 32)].rearrange("p (g n) -> p g n", g=2)[:, :, j + 1:j + 1 + n2]
                        if j == 0:
                            conv_ops.append(lambda c=c, outv=outv, inv=inv: ts(
                                outv, inv, cw[:, c, 0:1], vecT[:, l, V_CB + c:V_CB + c + 1],
                                ALU.mult, ALU.add, [buf('awin'), buf('vecT')], [bFA[c]]))
                        else:
                            conv_ops.append(lambda c=c, j=j, outv=outv, inv=inv: stt(
                                outv, inv, cw[:, c, j:j + 1], outv,
                                ALU.mult, ALU.add, [buf('awin'), buf('vecT'), bFA[c]], [bFA[c]]))
            else:
                for j in range(31):
                    for c in range(4):
                        for gi_, sg in enumerate(segs):
                            c0, n_, wo_ = sg['col0'], sg['n'], gi_ * (sg['n'] + 32)
                            if j == 0:
                                conv_ops.append(lambda c=c, c0=c0, n_=n_, wo_=wo_: ts(
                                    FA[:, c, c0:c0 + n_], awin[:, c, wo_ + 1:wo_ + 1 + n_], cw[:, c, 0:1], vecT[:, l, V_CB + c:V_CB + c + 1],
                                    ALU.mult, ALU.add, [buf('awin'), buf('vecT')], [bFA[c]]))
                            else:
                                conv_ops.append(lambda c=c, j=j, c0=c0, n_=n_, wo_=wo_: stt(
                                    FA[:, c, c0:c0 + n_], awin[:, c, wo_ + j + 1:wo_ + j + 1 + n_], cw[:, c, j:j + 1], FA[:, c, c0:c0 + n_],
                                    ALU.mult, ALU.add, [buf('awin'), buf('vecT'), bFA[c]], [bFA[c]]))

            def pump(ops, n):
                for _ in range(min(n, len(ops))):
                    ops.pop(0)()
            sqt = [0, 1, 4, 5]
            presq = {'done': False}
            for h in range(8):
                kcA, hh = h // 2, h % 2
                hf = h // 4
                if h == 1:
                    while v_ops:
                        v_ops.pop(0)()
                if h == 2:
                    sgu_spatial()
                if h >= 2:
                    pump(conv_ops, 21 if nsg == 1 else 10)
                if h == 7 and nsg == 1:
                    pump(conv_ops, 1000)
                    for c in range(4):
                        tt(FB[:, sqt[c], 0:Tm], FA[:, c, 0:Tm], FA[:, c, 0:Tm], ALU.mult, [bFA[c]], [bFB[sqt[c]]])
                    presq['done'] = True
                po = PSP.get()
                pss = []
                work = [(sg, kt) for sg in segs for kt in range(sg['kt0'], sg['kt0'] + sg['nkt'])]

                def s_mm(i):
                    sg, kt = work[i]
                    pb = PSP.get()
                    mm(psum[pb][:, 0:sg['n']], KT[:, kt * 128:(kt + 1) * 128], QT[:, h, sg['col0']:sg['col0'] + sg['n']],
                       True, True, [buf('KT'), buf('QT')], [bPS[pb]])
                    pss.append(pb)
                s_mm(0)
                if len(work) > 1:
                    s_mm(1)
                for i, (sg, kt) in enumerate(work):
                    pb = pss[i]
                    pi = i % 3
                    c0, n_ = sg['col0'], sg['n']
                    act(pT[:, pi, 0:n_], psum[pb][:, 0:n_], AF.Exp, [bPS[pb]], [bPT[pi]])
                    PSP.put(pb)
                    if h == 0 and v_ops and i % 6 == 5:
                        v_ops.pop(0)()
                    if i + 2 < len(work):
                        s_mm(i + 2)
                    mm(psum[po][:, c0:c0 + n_], V1[:, kt, hf, :], pT[:, pi, 0:n_], kt == sg['kt0'], kt == sg['kt0'] + sg['nkt'] - 1,
                       [buf('V1'), bPT[pi]], [bPS[po]])
                fr = 2 + h % 2
                recip(FB[0:64, fr, 0:Tm], psum[po][64:128, 0:Tm], [bPS[po]], [bFB[fr]])
                tt(G[hh * 64:(hh + 1) * 64, 0, kcA, 0:Tm], psum[po][0:64, 0:Tm], FB[0:64, fr, 0:Tm], ALU.mult, [bPS[po], bFB[fr]], [bG[0]])
                PSP.put(po)
            def conv_ln():
                p1 = PSP.get()
                p2 = PSP.get()
                for c in range(4):
                    mm(psum[p1][:, 0:Tm], onesf[:, :], FA[:, c, 0:Tm], c == 0, c == 3, [buf('onesf'), bFA[c]], [bPS[p1]])
                for c in range(4):
                    if presq['done']:
                        fs = sqt[c]
                    else:
                        fs = 4 + c % 2
                        act(FB[:, fs, 0:Tm], FA[:, c, 0:Tm], AF.Square, [bFA[c]], [bFB[fs]])
                    mm(psum[p2][:, 0:Tm], onesf[:, :], FB[:, fs, 0:Tm], c == 0, c == 3, [buf('onesf'), bFB[fs]], [bPS[p2]])
                act(FA[:, 4, 0:Tm], psum[p1][:, 0:Tm], AF.Copy, [bPS[p1]], [bFA[4]], scale=1.0 / 512)
                act(FA[:, 5, 0:Tm], psum[p1][:, 0:Tm], AF.Square, [bPS[p1]], [bFA[5]], scale=1.0 / 512)
                stt(FA[:, 6, 0:Tm], psum[p2][:, 0:Tm], 1.0 / 512, FA[:, 5, 0:Tm], ALU.mult, ALU.subtract, [bPS[p2], bFA[5]], [bFA[6]])
                PSP.put(p1)
                PSP.put(p2)
                act(FA[:, 5, 0:Tm], FA[:, 6, 0:Tm], AF.Sqrt, [bFA[6]], [bFA[5]], bias=epsT[:, 0:1], scale=1.0)
                recip(FA[:, 6, 0:Tm], FA[:, 5, 0:Tm], [bFA[5]], [bFA[6]])
                for c in range(4):
                    fy = 4 + c % 2
                    tt(FB[:, fy, 0:Tm], FA[:, c, 0:Tm], FA[:, 4, 0:Tm], ALU.subtract, [bFA[c], bFA[4]], [bFB[fy]])
                    tt(FB[:, fy, 0:Tm], FB[:, fy, 0:Tm], FA[:, 6, 0:Tm], ALU.mult, [bFB[fy], bFA[6]], [bFB[fy]])
                    act(G[:, 1, c, 0:Tm], FB[:, fy, 0:Tm], AF.Silu, [bFB[fy], buf('vecT')], [bG[1]],
                        bias=vecT[:, l, V_LB + c:V_LB + c + 1], scale=vecT[:, l, V_LG + c:V_LG + c + 1])

            if nsg == 1:
                pump(conv_ops, 1000)
                conv_ln()
                gate_and_project(l, 0, 0, Tm, 'first')
                gate_and_project(l, 3, 2, Tm, 'mid')
            else:
                gate_and_project(l, 0, 0, Tm, 'first', pump=lambda: pump(conv_ops, 12))
                gate_and_project(l, 3, 2, Tm, 'mid', pump=lambda: pump(conv_ops, 12))
                pump(conv_ops, 1000)
                conv_ln()
            fa_flat = FA[:, :, :].rearrange("p a b -> p (a b)")
            pw_ = smallw[:, 0:512].rearrange("p (g d) -> p g d", g=4)
            pool_ops = []

            def pool_group(g):
                w = 2 ** (g + 1)
                hw = w // 2
                for gi_, sg in enumerate(segs):
                    c0, n_ = sg['col0'], sg['n']
                    xw = plwin[:, g, gi_ * (n_ + 16):gi_ * (n_ + 16) + n_ + 16]

                    def seg_ops(c0=c0, n_=n_, xw=xw, sg=sg):
                        if g == 0:
                            src, srcb = xw, [buf('plwin')]
                        else:
                            L1 = n_ + 15
                            pool_ops.append(lambda: tt(fa_flat[:, 0:L1], xw[:, 0:L1], xw[:, 1:L1 + 1], ALU.add, [buf('plwin')], [bFA[0], bFA[1]]))
                            src, srcb = fa_flat[:, 0:1024], [bFA[0], bFA[1]]
                            if g >= 2:
                                L2 = n_ + 13
                                pool_ops.append(lambda: tt(fa_flat[:, 1024:1024 + L2], fa_flat[:, 0:L2], fa_flat[:, 2:L2 + 2], ALU.add, [bFA[0], bFA[1]], [bFA[2], bFA[3]]))
                                src, srcb = fa_flat[:, 1024:2048], [bFA[2], bFA[3]]
                            if g >= 3:
                                L3 = n_ + 9
                                pool_ops.append(lambda: tt(fa_flat[:, 2048:2048 + L3], fa_flat[:, 1024:1024 + L3], fa_flat[:, 1028:1028 + L3], ALU.add, [bFA[2], bFA[3]], [bFA[4], bFA[5]]))
                                src, srcb = fa_flat[:, 2048:3072], [bFA[4], bFA[5]]
                        pool_ops.append(lambda: tt(FA[:, 6, c0:c0 + n_], src[:, 8 - hw:8 - hw + n_], src[:, 8:8 + n_], ALU.add, srcb, [bFA[6]]))
                        pool_ops.append(lambda: ts(FA[:, 6, c0:c0 + n_], FA[:, 6, c0:c0 + n_], 1.0 / w, None, ALU.mult, None, [bFA[6]], [bFA[6]]))
                        if sg['first']:
                            pool_ops.append(lambda: tt(FA[:, 6, c0:c0 + 8], FA[:, 6, c0:c0 + 8], pcorr[:, g, 0, :], ALU.mult, [bFA[6], buf('pcorr')], [bFA[6]]))
                        if sg['last']:
                            pool_ops.append(lambda: tt(FA[:, 6, c0 + n_ - 8:c0 + n_], FA[:, 6, c0 + n_ - 8:c0 + n_], pcorr[:, g, 1, :], ALU.mult, [bFA[6], buf('pcorr')], [bFA[6]]))
                        pool_ops.append(lambda: tt(pooled_bf[:, c0:c0 + n_], FA[:, 6, c0:c0 + n_], xw[:, 8:8 + n_], ALU.subtract, [bFA[6], buf('plwin')], [buf('pooled')]))
                    seg_ops()

                def fin():
                    pb = PSP.get()
                    mm(psum[pb][:, 0:Tm], pw_[:, g, :], pooled_bf[:, 0:Tm], True, True, [buf('smallw'), buf('pooled')], [bPS[pb]])
                    act(G[:, 0, g, 0:Tm], psum[pb][:, 0:Tm], AF.Copy, [bPS[pb], buf('vecT')], [bG[0]], scale=vecT[:, l, V_PS + g:V_PS + g + 1])
                    PSP.put(pb)
                pool_ops.append(fin)
            for g in range(4):
                pool_group(g)
            gate_and_project(l, 1, 1, Tm, 'mid', pump=lambda: pump(pool_ops, 4 * nsg))
            pump(pool_ops, 1000)
            gate_and_project(l, 2, 0, Tm, 'last')
            mbf = [buf(f'mbf{dc}') for dc in range(8)]
            s_oo = [WS.need(l, 'out_0'), WS.need(l, 'out_1')]

            def norm2_scale(s):
                recip(stat[:, 8 + s:9 + s], stat[:, 4 + s:5 + s], st(4 + s), st(8 + s))
                act(FA[:, 2 * s:2 * s + 2, :].rearrange("p a b -> p (a b)"), x_sb[:, s, :], AF.Copy,
                    [bX[s]] + st(8 + s), [bFA[2 * s], bFA[2 * s + 1]], scale=stat[:, 8 + s:9 + s])
            for s in range(nsub):
                for cb in range(2):
                    s_o = s_oo[cb]
                    wo = wview(s_o, 8, 512)
                    pb = PSP.get()
                    for kc in range(KC):
                        mm(psum[pb][:, :], merged_bf[:, kc, s * 128:(s + 1) * 128], wo[:, kc, :], kc == 0, kc == KC - 1,
                           mbf + [bW[s_o]], [bPS[pb]])
                    ft = 2 + cb
                    tt(FB[:, ft, :], psum[pb][:, :], gbc[:, 0, cb * 512:(cb + 1) * 512], ALU.mult, [bPS[pb], buf('gbc')], [bFB[ft]])
                    PSP.put(pb)
                    tt(x_sb[:, s, cb * 512:(cb + 1) * 512], x_sb[:, s, cb * 512:(cb + 1) * 512], FB[:, ft, :], ALU.add, [bX[s], bFB[ft]], [bX[s]])
                sumsq(FA[:, 2 * s:2 * s + 2, :].rearrange("p a b -> p (a b)"), x_sb[:, s, :], stat[:, s:s + 1],
                      [bX[s]], [bFA[2 * s], bFA[2 * s + 1]] + st(s))
                act(stat[:, 4 + s:5 + s], stat[:, s:s + 1], AF.Sqrt, st(s), st(4 + s), bias=epsT[:, 0:1], scale=1.0 / D)
                if s > 0:
                    norm2_scale(s - 1)
            norm2_scale(nsub - 1)
            norm_to_hT(l, t, 1, nsub, skip_elem=True)
            f1 = [FA[:, 0:4, :].rearrange("p a b -> p (a b)").bitcast(BF16).rearrange("p (k c) -> p k c", k=8),
                  FA[:, 4:8, :].rearrange("p a b -> p (a b)").bitcast(BF16).rearrange("p (k c) -> p k c", k=8)]
            f1.append(merged_bf)
            mbfb = [buf(f'mbf{dc}') for dc in range(8)]
            f1bufs = [[bFAh[jc] for jc in range(8)], [bFAh[8 + jc] for jc in range(8)], mbfb]
            for gq in range(4):
                fi = gq % 2 if gq < 3 else 2
                if gq == 3 and nxt is not None:
                    dma('sp', FA[:, 0:2 * nxt[1], :].rearrange("p (s a) b -> p s (a b)", a=2),
                        src_d[nxt[0]:nxt[0] + nxt[1] * 128, :].rearrange("(s p) d -> p s d", p=128),
                        [buf('x1' + kind)] if cur['l'] > 0 else [], bFA[0:2 * nxt[1]], 'xf')
                    xf['have'] = True
                for half in range(2):
                    s_i = WS.need(l, f'mi_{gq * 2 + half}')
                    wi = wview(s_i, 8, 512)
                    for dq in range(4):
                        jc = half * 4 + dq
                        pb = PSP.get()
                        for kc in range(KC):
                            mm(psum[pb][:, 0:Tm], wi[:, kc, dq * 128:(dq + 1) * 128], hT[:, kc, 0:Tm], kc == 0, kc == KC - 1,
                               bHT + [bW[s_i]], [bPS[pb]])
                        fr = 4 + jc % 2
                        act(FB[:, fr, 0:Tm], psum[pb][:, 0:Tm], AF.Relu, [bPS[pb]], [bFB[fr]])
                        PSP.put(pb)
                        tt(f1[fi][:, jc, 0:Tm], FB[:, fr, 0:Tm], FB[:, fr, 0:Tm], ALU.mult, [bFB[fr]], [f1bufs[fi][jc]])
                if gq == 3 and xf['have']:
                    xf['have'] = False
                    norm_elem_from_fa(nxt[1])
                    xn['ready'] = True

                def mo_step(cb, s_o, s):
                    wo = wview(s_o, 8, 512)
                    pb = PSP.get()
                    for jc in range(8):
                        mm(psum[pb][:, :], f1[fi][:, jc, s * 128:(s + 1) * 128], wo[:, jc, :], jc == 0, jc == 7,
                           [f1bufs[fi][jc], bW[s_o]], [bPS[pb]])
                    ft = 2 + s % 2
                    tt(FB[:, ft, :], psum[pb][:, :], gbc[:, 1, cb * 512:(cb + 1) * 512], ALU.mult, [bPS[pb], buf('gbc')], [bFB[ft]])
                    PSP.put(pb)
                    tt(x_sb[:, s, cb * 512:(cb + 1) * 512], x_sb[:, s, cb * 512:(cb + 1) * 512], FB[:, ft, :], ALU.add, [bX[s], bFB[ft]], [bX[s]])
                if gq < 3:
                    for cb in range(2):
                        s_o = WS.need(l, f'mo_{cb}_{gq}')
                        for s in range(nsub):
                            mo_step(cb, s_o, s)
                else:
                    s_o0 = WS.need(l, f'mo_0_{gq}')
                    s_o1 = WS.need(l, f'mo_1_{gq}')
                    def finish_subtile(s):
                        if final:
                            stt(x_sb[:, s, :], x_sb[:, s, :], stat[:, 8 + s:9 + s], fng[:, :], ALU.mult, ALU.mult,
                                [bX[s], buf('fng')] + st(8 + s), [bX[s]])
                        dma('sp', dst_d[tok0 + s * 128:tok0 + (s + 1) * 128, :], x_sb[:, s, :], [bX[s]],
                            [buf(('y' if final else 'x1') + kind)], f'xo{s}')
                        if nxt is not None and s < nxt[1]:
                            rd = [buf('x1' + kind)] if cur['l'] > 0 else []
                            dma('sp', x_sb[:, s, :], src_d[nxt[0] + s * 128:nxt[0] + (s + 1) * 128, :], rd, [bX[s]], f'x{s}')
                    for s in range(nsub):
                        mo_step(0, s_o0, s)
                        mo_step(1, s_o1, s)
                        if final:
                            sumsq(FB[:, 4:6, :].rearrange("p a b -> p (a b)"), x_sb[:, s, :], stat[:, s:s + 1],
                                  [bX[s]], [bFB[4], bFB[5]] + st(s))
                            act(stat[:, 4 + s:5 + s], stat[:, s:s + 1], AF.Sqrt, st(s), st(4 + s), bias=epsT[:, 0:1], scale=1.0 / D)
                            if s > 0:
                                recip(stat[:, 8 + s - 1:9 + s - 1], stat[:, 4 + s - 1:5 + s - 1], st(4 + s - 1), st(8 + s - 1))
                                finish_subtile(s - 1)
                        else:
                            finish_subtile(s)
                    if final:
                        recip(stat[:, 8 + nsub - 1:9 + nsub - 1], stat[:, 4 + nsub - 1:5 + nsub - 1], st(4 + nsub - 1), st(8 + nsub - 1))
                        finish_subtile(nsub - 1)
                    if nxt is not None:
                        xpre['have'] = True

        def whole_program():
            if not P.dry:
                setup()
            for l in range(NLAYER):
                cur['l'] = l
                if not P.dry:
                    layer_setup(l)
                final = (l == NLAYER - 1)
                for kind in ('P', 'S'):
                    t = 0 if kind == 'P' else 1
                    if kind == 'P':
                        N = 2 * SEQ_P
                        src = xp_d if l == 0 else x1p_d
                        dst = yp_d if final else x1p_d
                        key0 = 0
                    else:
                        N = NS
                        src = xs_d if l == 0 else x1s_d
                        dst = ys_d if final else x1s_d
                        key0 = PAST
                    tiles = []
                    tk = 0
                    while tk < N:
                        ns_ = min(4, (N - tk) // 128)
                        if kind == 'P':
                            segs = [dict(col0=g * SEQ_P, n=SEQ_P, kt0=2 * g, nkt=2, apos=g * (SEQ_P + 32) + 16, ppos=g * (SEQ_P + 16) + 8,
                                         first=True, last=True) for g in range(2)]
                        else:
                            segs = [dict(col0=0, n=ns_ * 128, kt0=0, nkt=(N + key0) // 128, apos=16 + tk, ppos=8 + tk,
                                         first=(tk == 0), last=(tk + ns_ * 128 == N))]
                        tiles.append((tk, ns_, segs))
                        tk += ns_ * 128
                    if not P.dry:
                        type_setup(l, t)
                        seq_setup(l, kind, N)
                    for ti, (tok0, ns_, segs) in enumerate(tiles):
                        nxt = tiles[ti + 1][0:2] if ti + 1 < len(tiles) else tiles[0][0:2]
                        phaseA(l, kind, t, src, tok0, ns_, key0, segs, nxt)
                    for ti, (tok0, ns_, segs) in enumerate(tiles):
                        nxt = tiles[ti + 1][0:2] if ti + 1 < len(tiles) else None
                        phaseB(l, kind, t, src, dst, tok0, ns_, segs, final, nxt)

        P.dry = True
        whole_program()
        P.dry = False
        whole_program()
        assert WS.pos == len(WS.sched)
        fw = {'sp': [('c', c, P.chan_cnt[c]) for c in ('xo0', 'xo1', 'xo2', 'xo3', 'kvo') if c in P.chan_cnt]}
        finalize_and_emit(P, block, sems, csems, fw)
    return nc


_CACHE = {}


def run_cores(inputs, NS, ncores):
    hw = host_weights(inputs)
    hc = host_consts(NS)
    xp = np.asarray(inputs['x_prompt'], np.float32)
    xs = np.asarray(inputs['x_sample'], np.float32)
    ck = np.asarray(inputs['cache_k'], np.float32)
    cv = np.asarray(inputs['cache_v'], np.float32)
    c = np.asarray(inputs['c'], np.float32)
    c_ctx = np.asarray(inputs['c_ctx'], np.float32)
    in_maps = []
    for i in range(ncores):
        cvec = np.stack([c_ctx.reshape(8, 128).T, c[i].reshape(8, 128).T], axis=-1).astype(np.float32)
        m = dict(xp=np.ascontiguousarray(xp[2 * i:2 * i + 2].reshape(2 * SEQ_P, D)),
                 xs=np.ascontiguousarray(xs[i]),
                 ck=np.ascontiguousarray(ck[i].reshape(NLAYER, PAST, 128)),
                 cv=np.ascontiguousarray(cv[i].reshape(NLAYER, PAST, 128)),
                 cvec=np.ascontiguousarray(cvec))
        m.update(hw)
        m.update(hc)
        in_maps.append(m)
    if NS not in _CACHE:
        _CACHE[NS] = build_program(NS)
    nc = _CACHE[NS]
    res = run_bass_kernel_spmd(nc, in_maps, core_ids=list(range(ncores)))
    yp = np.stack([r['yp'].reshape(2, SEQ_P, D) for r in res.results]).reshape(2 * ncores, SEQ_P, D)
    ys = np.stack([r['ys'] for r in res.results])
    nk = np.stack([r['nk'] for r in res.results]).reshape(2 * ncores, NLAYER, SEQ_P, 2, 64)
    nv = np.stack([r['nv'] for r in res.results]).reshape(2 * ncores, NLAYER, SEQ_P, 2, 64)
    return (yp.astype(np.float32), ys.astype(np.float32), nk.astype(np.float32), nv.astype(np.float32))


def kernel(**inputs):
    NS = int(np.asarray(inputs['x_sample']).shape[1])
    ncores = int(np.asarray(inputs['x_sample']).shape[0])
    return run_cores(inputs, NS, ncores)
```

```python
import numpy as np
from contextlib import ExitStack
import concourse.bass as bass
import concourse.mybir as mybir
from concourse.bass_utils import run_bass_kernel_spmd

F32 = mybir.dt.float32
BF16 = mybir.dt.bfloat16
AF = mybir.ActivationFunctionType
ALU = mybir.AluOpType
AX = mybir.AxisListType

D = 1024
KC = 8
EPS = 1e-6
NLAYER = 2
SEQ_P = 256
PAST = 512
NSLOT = 4
COMPUTE = ('pe', 'act', 'dve', 'pool')


class Buf:
    __slots__ = ('name', 'w', 'r')

    def __init__(self, name):
        self.name = name
        self.w = None
        self.r = []


class Ins:
    __slots__ = ('eng', 'fn', 'deps', 'sig', 'ticket', 'dma', 'chan', 'dmaval')

    def __init__(self, eng, fn, dma=False, chan=None):
        self.eng = eng
        self.fn = fn
        self.deps = []
        self.sig = False
        self.ticket = None
        self.dma = dma
        self.chan = chan
        self.dmaval = None


def _flat(x):
    out = []
    for b in x:
        if isinstance(b, (list, tuple)):
            out.extend(_flat(b))
        else:
            out.append(b)
    return out


class Prog:
    def __init__(self):
        self.q = {e: [] for e in ('pe', 'act', 'dve', 'pool', 'sp')}
        self.chan_cnt = {}
        self.dry = False

    def op(self, eng, fn, reads=(), writes=(), dma=False, chan=None):
        if self.dry:
            return None
        ins = Ins(eng, fn, dma=dma, chan=chan)
        reads = _flat(reads)
        writes = _flat(writes)
        raw = []
        oth = []
        for b in reads:
            if b.w is not None:
                raw.append(b.w)
        for b in writes:
            if b.r:
                oth.extend(b.r)
            elif b.w is not None:
                oth.append(b.w)
        for b in reads:
            if not dma:
                b.r = [x for x in b.r if not (x.eng == eng and not x.dma)]
            b.r.append(ins)
        for b in writes:
            b.w = ins
            b.r = []
        seen = set()
        for d in raw:
            if d is ins or id(d) in seen:
                continue
            seen.add(id(d))
            ins.deps.append((d, self.chan_cnt[d.chan] if d.dma else None))
        for d in oth:
            if d is ins or id(d) in seen:
                continue
            seen.add(id(d))
            if (not dma) and (not d.dma) and d.eng == eng and eng == 'pe':
                continue
            ins.deps.append((d, self.chan_cnt[d.chan] if d.dma else None))
        if dma:
            c = self.chan_cnt.get(chan, 0) + 16
            self.chan_cnt[chan] = c
            ins.dmaval = c
        self.q[eng].append(ins)
        return ins


def finalize_and_emit(prog, block, sems, chan_sems, final_waits):
    for e in prog.q:
        for ins in prog.q[e]:
            for d, _v in ins.deps:
                d.sig = True
    for e in COMPUTE:
        t = 0
        for ins in prog.q[e]:
            if ins.dma:
                continue
            if ins.sig:
                t += 1
                ins.ticket = t

    def emit_engine(eng_name, eng):
        waited = {}
        for ins in prog.q[eng_name]:
            need = {}
            for d, v in ins.deps:
                if d.dma:
                    key = ('c', d.chan)
                    val = v
                else:
                    key = ('e', d.eng)
                    val = d.ticket
                if val > need.get(key, 0):
                    need[key] = val
            for key, val in need.items():
                if waited.get(key, 0) >= val:
                    continue
                waited[key] = val
                sem = chan_sems[key[1]] if key[0] == 'c' else sems[key[1]]
                eng.wait_ge(sem, val)
            r = ins.fn(eng)
            if ins.dma:
                r.then_inc(chan_sems[ins.chan], 16)
            elif ins.sig:
                r.then_inc(sems[ins.eng], 1)
        for (kind, name, val) in final_waits.get(eng_name, []):
            sem = chan_sems[name] if kind == 'c' else sems[name]
            eng.wait_ge(sem, val)

    @block.tensor
    def _(e):
        emit_engine('pe', e)

    @block.scalar
    def _(e):
        emit_engine('act', e)

    @block.vector
    def _(e):
        emit_engine('dve', e)

    @block.gpsimd
    def _(e):
        emit_engine('pool', e)

    @block.sync
    def _(e):
        emit_engine('sp', e)


BLK_NAMES = (['in_kv', 'in_a0', 'in_a1', 'in_pl', 'in_q', 'in_u', 'in_v']
             + [f'gate{i}_{h}' for i in range(4) for h in range(2)]
             + [f'br{i}_{h}' for i in range(4) for h in range(2)]
             + ['out_0', 'out_1']
             + [f'mi_{j}' for j in range(8)]
             + [f'mo_{cb}_{g}' for cb in range(2) for g in range(4)])
BLK_ID = {n: i for i, n in enumerate(BLK_NAMES)}
NBLK = len(BLK_NAMES)
BLK_COLS = {n: 4096 for n in BLK_NAMES}
BLK_COLS['in_kv'] = 2048
for _i in range(4):
    for _h in range(2):
        BLK_COLS[f'br{_i}_{_h}'] = 2048

V_N1G, V_N2G, V_BG, V_CW, V_CB, V_LG, V_LB, V_PS = 0, 8, 16, 48, 172, 176, 180, 184
NVEC = 188
B_QG, B_KG, B_SG, B_SB = 0, 64, 128, 640
NVBC = 1152


def _blockify(W, c0, ncols):
    K = W.shape[0]
    kc = K // 128
    b = W[:, c0:c0 + ncols].reshape(kc, 128, ncols).transpose(1, 0, 2).reshape(128, kc * ncols)
    out = np.zeros((128, 4096), np.float32)
    out[:, :kc * ncols] = b
    return out


def host_weights(inp):
    wts = np.zeros((NLAYER, NBLK, 128, 4096), np.float32)
    qperm = np.array([(j + 4 * hf) * 64 + d for j in range(4) for hf in range(2) for d in range(64)])
    for l in range(NLAYER):
        w_in = np.asarray(inp['w_in'][l])
        wq = w_in[:, 0:512][:, qperm]
        wts[l, BLK_ID['in_q']] = _blockify(wq, 0, 512)
        wts[l, BLK_ID['in_kv']] = _blockify(w_in, 512, 256)
        wts[l, BLK_ID['in_a0']] = _blockify(w_in, 768, 512)
        wts[l, BLK_ID['in_a1']] = _blockify(w_in, 1280, 512)
        wts[l, BLK_ID['in_pl']] = _blockify(w_in, 1792, 512)
        wts[l, BLK_ID['in_u']] = _blockify(w_in, 2304, 512)
        wts[l, BLK_ID['in_v']] = _blockify(w_in, 2816, 512)
        wg = np.asarray(inp['w_gate'][l])
        wb = np.asarray(inp['w_branch'][l])
        for i in range(4):
            for h in range(2):
                wts[l, BLK_ID[f'gate{i}_{h}']] = _blockify(wg, i * 1024 + h * 512, 512)
                wts[l, BLK_ID[f'br{i}_{h}']] = _blockify(wb[i], h * 512, 512)
        wo = np.asarray(inp['w_out'][l])
        for cb in range(2):
            wts[l, BLK_ID[f'out_{cb}']] = _blockify(wo, cb * 512, 512)
        wmi = np.asarray(inp['w_mlp_in'][l])
        for j in range(8):
            wts[l, BLK_ID[f'mi_{j}']] = _blockify(wmi, j * 512, 512)
        wmo = np.asarray(inp['w_mlp_out'][l])
        for cb in range(2):
            for g in range(4):
                wts[l, BLK_ID[f'mo_{cb}_{g}']] = _blockify(wmo[g * 1024:(g + 1) * 1024], cb * 512, 512)
    wmod = np.zeros((NLAYER, 12, 128, 4096), np.float32)
    for l in range(NLAYER):
        wm = np.asarray(inp['w_mod'][l])
        for j in range(12):
            wmod[l, j] = _blockify(wm, j * 512, 512)
    smallw = np.zeros((NLAYER, 128, 1024), np.float32)
    for l in range(NLAYER):
        smallw[l, :, 0:512] = np.asarray(inp['pool_w'][l]).transpose(1, 0, 2).reshape(128, 512)
        smallw[l, :, 512:1024] = np.asarray(inp['sgu_w'][l]).transpose(2, 0, 1).reshape(128, 512)
    vecT = np.zeros((128, NLAYER, NVEC), np.float32)
    vbc = np.zeros((128, NLAYER, NVBC), np.float32)
    bmodT = np.zeros((128, NLAYER, 48), np.float32)
    for l in range(NLAYER):
        vecT[:, l, V_N1G:V_N1G + 8] = np.asarray(inp['norm1_g'][l]).reshape(8, 128).T
        vecT[:, l, V_N2G:V_N2G + 8] = np.asarray(inp['norm2_g'][l]).reshape(8, 128).T
        vecT[:, l, V_BG:V_BG + 32] = np.asarray(inp['b_gate'][l]).reshape(32, 128).T
        vecT[:, l, V_CW:V_CW + 124] = np.asarray(inp['conv_w'][l]).reshape(31, 4, 128).transpose(2, 1, 0).reshape(128, 124)
        vecT[:, l, V_CB:V_CB + 4] = np.asarray(inp['conv_b'][l]).reshape(4, 128).T
        vecT[:, l, V_LG:V_LG + 4] = np.asarray(inp['conv_ln_g'][l]).reshape(4, 128).T
        vecT[:, l, V_LB:V_LB + 4] = np.asarray(inp['conv_ln_b'][l]).reshape(4, 128).T
        vecT[:, l, V_PS:V_PS + 4] = np.asarray(inp['pool_scale'][l]).reshape(4, 128).T
        vbc[:, l, B_QG:B_QG + 64] = np.asarray(inp['q_norm_g'][l])[None, :]
        vbc[:, l, B_KG:B_KG + 64] = np.asarray(inp['k_norm_g'][l])[None, :]
        vbc[:, l, B_SG:B_SG + 512] = np.asarray(inp['sgu_norm_g'][l])[None, :]
        vbc[:, l, B_SB:B_SB + 512] = np.asarray(inp['sgu_b'][l]).reshape(512)[None, :]
        bmodT[:, l, :] = np.asarray(inp['b_mod'][l]).reshape(48, 128).T
    fng = np.broadcast_to(np.asarray(inp['final_norm_g'])[None, :], (128, D)).astype(np.float32).copy()
    return dict(wts=wts.reshape(NLAYER * NBLK, 128, 4096), wmod=wmod.reshape(NLAYER * 12, 128, 4096),
                smallw=smallw, vecT=vecT, vbc=vbc, bmodT=bmodT, fng=fng)


def host_consts(NS):
    nt = NS // 128
    t = np.arange(NS)
    row = (t // 64).astype(np.float32)
    col = (t % 64).astype(np.float32)
    inv = (10000.0 ** (-np.arange(0, 32, 2, dtype=np.float32) / 32)).astype(np.float32)
    ang = np.concatenate([row[:, None] * inv, col[:, None] * inv], axis=-1).astype(np.float32)
    cos = np.cos(ang).astype(np.float32).reshape(nt, 128, 32).transpose(1, 0, 2).copy()
    sin = np.sin(ang).astype(np.float32).reshape(nt, 128, 32).transpose(1, 0, 2).copy()
    corr = np.ones((128, 4, 2, 8), np.float32)
    for g in range(4):
        w = 2 ** (g + 1)
        for i in range(8):
            if i < w // 2:
                corr[:, g, 0, i] = w / (i + w // 2)
            e = 7 - i
            if e < w // 2 - 1:
                corr[:, g, 1, i] = w / (e + 1 + w // 2)
    ident = np.eye(128, dtype=np.float32)
    return dict(ropec=cos, ropes=sin, pcorr=corr, ident=ident)


def build_program(NS):
    nc = bass.Bass("TRN2", target_bir_lowering=False)
    NTS = NS // 128
    NK = NS + PAST
    NKT = NK // 128
    NPT = 2 * SEQ_P

    def din(name, shape, dt=F32):
        return nc.dram_tensor(name, shape, dt, kind="ExternalInput").ap()

    def dout(name, shape):
        return nc.dram_tensor(name, shape, F32, kind="ExternalOutput").ap()

    xp_d = din("xp", [NPT, D])
    xs_d = din("xs", [NS, D])
    ck_d = din("ck", [NLAYER, PAST, 128])
    cv_d = din("cv", [NLAYER, PAST, 128])
    cvec_d = din("cvec", [128, 8, 2])
    wts_d = din("wts", [NLAYER * NBLK, 128, 4096])
    wmod_d = din("wmod", [NLAYER * 12, 128, 4096])
    smallw_d = din("smallw", [NLAYER, 128, 1024])
    vecT_d = din("vecT", [128, NLAYER, NVEC])
    vbc_d = din("vbc", [128, NLAYER, NVBC])
    bmodT_d = din("bmodT", [128, NLAYER, 48])
    fng_d = din("fng", [128, D])
    ropec_d = din("ropec", [128, NTS, 32])
    ropes_d = din("ropes", [128, NTS, 32])
    pcorr_d = din("pcorr", [128, 4, 2, 8])
    ident_d = din("ident", [128, 128])
    yp_d = dout("yp", [NPT, D])
    ys_d = dout("ys", [NS, D])
    nk_d = dout("nk", [2, NLAYER, SEQ_P, 128])
    nv_d = dout("nv", [2, NLAYER, SEQ_P, 128])
    x1p_d = nc.dram_tensor("x1p", [NPT, D], F32).ap()
    x1s_d = nc.dram_tensor("x1s", [NS, D], F32).ap()
    aT_d = nc.dram_tensor("aT_s", [4, 128, NS + 32], BF16).ap()
    plT_d = nc.dram_tensor("plT_s", [4, 128, NS + 16], BF16).ap()

    es = ExitStack()
    with es:
        def sb(name, shape, dt):
            return es.enter_context(nc.sbuf_tensor(name, shape, dt))

        wslot = [sb(f"wslot{i}", [128, 4096], BF16) for i in range(NSLOT)]
        x_sb = sb("x_sb", [128, 4, D], F32)
        hT = sb("hT", [128, KC, 512], BF16)
        KT = sb("KT", [128, NK], BF16)
        V1 = sb("V1", [128, NKT, 2, 128], BF16)
        QT = sb("QT", [128, 8, 512], BF16)
        merged = sb("merged", [128, 8, 512], F32)
        merged_bf = sb("merged_bf", [128, 8, 512], BF16)
        gbc = sb("gbc", [128, 2, D], F32)
        fng = sb("fng_sb", [128, D], F32)
        ropec = sb("ropec_sb", [128, NTS, 32], F32)
        ropes = sb("ropes_sb", [128, NTS, 32], F32)
        vbc = sb("vbc_sb", [128, NVBC], F32)
        vecT = sb("vecT_sb", [128, NLAYER, NVEC], F32)
        bmodT = sb("bmodT_sb", [128, NLAYER, 48], F32)
        modT = sb("modT", [128, NLAYER, 48, 2], F32)
        gmT = sb("gmT", [128, NLAYER, 2, 8, 2], F32)
        cvec = sb("cvec_sb", [128, 8, 2], F32)
        csilu = sb("csilu", [128, 8, 2], F32)
        pcorr = sb("pcorr_sb", [128, 4, 2, 8], F32)
        identf = sb("identf", [128, 128], F32)
        identb = sb("identb", [128, 128], BF16)
        onesf = sb("onesf", [128, 128], F32)
        epsT = sb("epsT", [128, 1], F32)
        zeroT = sb("zeroT", [128, 64], BF16)
        smallw = sb("smallw_sb", [128, 1024], BF16)
        stat = sb("stat", [128, 64], F32)
        FA = sb("FA", [128, 8, 512], F32)
        FB = sb("FB", [128, 6, 512], F32)
        G = sb("Gt", [128, 3, 4, 512], BF16)
        q_bf = sb("q_bf", [128, 512], BF16)
        pT = sb("pT", [128, 3, 512], BF16)
        awin = sb("awin", [128, 4, 576], BF16)
        plwin = sb("plwin", [128, 4, 544], BF16)
        kvout = awin[:, :, :].rearrange("p a b -> p (a b)").bitcast(F32)[:, 0:1024].rearrange("p (s t c) -> p s t c", s=4, t=2)
        pooled_bf = sb("pooled_bf", [128, 512], BF16)
        psum = [es.enter_context(nc.psum_tensor(f"ps{i}", [128, 512], F32)) for i in range(8)]

        sems = {e: es.enter_context(nc.semaphore(f"s_{e}")) for e in COMPUTE}
        chan_names = ([f'w{i}' for i in range(NSLOT)] + ['x0', 'x1', 'x2', 'x3', 'xo0', 'xo1', 'xo2', 'xo3', 'const', 'kvc', 'kvo', 'ast', 'pst', 'awin', 'plwin',
                                                          'wmod0', 'wmod1', 'small', 'pad', 'vbc', 'xf'])
        csems = {c: es.enter_context(nc.semaphore(f"c_{c}")) for c in chan_names}
        block = es.enter_context(nc.Block())

        P = Prog()
        B = {}
        cur = {'l': 0}

        def buf(name):
            if name not in B:
                B[name] = Buf(name)
            return B[name]

        bFAh = [buf(f'FAh{i}') for i in range(16)]
        bFA = [(bFAh[2 * i], bFAh[2 * i + 1]) for i in range(8)]
        bFB = [buf(f'FB{i}') for i in range(6)]
        bG = [buf('G0'), buf('G1'), buf('G2')]
        bPS = [buf(f'PS{i}') for i in range(8)]
        bX = [buf(f'X{i}') for i in range(4)]
        bHT = [buf(f'HT{i}') for i in range(8)]
        bW = [buf(f'W{i}') for i in range(NSLOT)]
        bPT = [buf(f'pT{i}') for i in range(3)]
        bST = [buf(f'st{i}') for i in range(64)]

        def st(c0, n=1):
            return bST[c0:c0 + n]

        def mm(out, lhsT, rhs, start, stop, reads, writes):
            P.op('pe', lambda e: e.matmul(out, lhsT=lhsT, rhs=rhs, start=start, stop=stop), reads, writes)

        def tr(out, in_, ident, reads, writes):
            P.op('pe', lambda e: e.transpose(out, in_, ident), reads, writes)

        def act(out, in_, func, reads, writes, bias=None, scale=None):
            kw = {}
            if bias is not None:
                kw['bias'] = bias
            if scale is not None:
                kw['scale'] = scale
            P.op('act', lambda e: e.activation(out=out, in_=in_, func=func, **kw), reads, writes)

        def tt(out, in0, in1, op, reads, writes, eng='dve'):
            P.op(eng, lambda e: e.tensor_tensor(out=out, in0=in0, in1=in1, op=op), reads, writes)

        def ts(out, in0, s1, s2, op0, op1, reads, writes, eng='dve'):
            if op1 is None:
                P.op(eng, lambda e: e.tensor_scalar(out=out, in0=in0, scalar1=s1, scalar2=None, op0=op0), reads, writes)
            else:
                P.op(eng, lambda e: e.tensor_scalar(out=out, in0=in0, scalar1=s1, scalar2=s2, op0=op0, op1=op1), reads, writes)

        def stt(out, in0, scalar, in1, op0, op1, reads, writes):
            P.op('dve', lambda e: e.scalar_tensor_tensor(out=out, in0=in0, scalar=scalar, in1=in1, op0=op0, op1=op1), reads, writes)

        def cp(out, in_, reads, writes, eng='dve'):
            if eng == 'act_copy':
                P.op('act', lambda e: e.activation(out=out, in_=in_, func=AF.Copy), reads, writes)
            else:
                P.op(eng, lambda e: e.tensor_copy(out=out, in_=in_), reads, writes)

        def sumsq(junk, in_, acc, reads, writes):
            P.op('act', lambda e: e.activation(out=junk, in_=in_, func=AF.Square, accum_out=acc), reads, writes)

        def recip(out, in_, reads, writes):
            P.op('dve', lambda e: e.reciprocal(out=out, in_=in_), reads, writes)

        def dma(q, out, in_, reads, writes, chan):
            P.op(q, lambda e: e.dma_start(out=out, in_=in_), reads, writes, dma=True, chan=chan)

        class PSPool:
            def __init__(self):
                self.free = list(range(8))

            def get(self):
                i = self.free.pop(0)
                return i

            def put(self, i):
                self.free.append(i)
        PSP = PSPool()

        class WStream:
            def __init__(self):
                self.sched = []
                self.pos = 0
                self.issued = 0

            def _issue(self, k):
                l, name = self.sched[k]
                s = k % NSLOT
                ncol = BLK_COLS[name]
                dma('pool', wslot[s][:, 0:ncol], wts_d[l * NBLK + BLK_ID[name], :, 0:ncol], [], [bW[s]], f'w{s}')

            def need(self, l, name):
                if P.dry:
                    self.sched.append((l, name))
                    return 0
                k = self.pos
                assert self.sched[k] == (l, name), (self.sched[k], l, name)
                lim = min(len(self.sched), k + NSLOT - 1)
                while self.issued < lim:
                    self._issue(self.issued)
                    self.issued += 1
                self.pos += 1
                return k % NSLOT
        WS = WStream()

        def wview(s, kc, ncols):
            return wslot[s][:, 0:kc * ncols].rearrange("p (k c) -> p k c", k=kc)

        def setup():
            dma('sp', identf[:, :], ident_d[:, :], [], [buf('identf')], 'const')
            dma('sp', cvec[:, :, :], cvec_d[:, :, :], [], [buf('cvec')], 'const')
            dma('sp', bmodT[:, :, :], bmodT_d[:, :, :], [], [buf('bmodT')], 'const')
            dma('sp', vecT[:, :, :], vecT_d[:, :, :], [], [buf('vecT')], 'const')
            dma('sp', ropec[:, :, :], ropec_d[:, :, :], [], [buf('ropec')], 'const')
            dma('sp', ropes[:, :, :], ropes_d[:, :, :], [], [buf('ropes')], 'const')
            dma('sp', pcorr[:, :, :, :], pcorr_d[:, :, :, :], [], [buf('pcorr')], 'const')
            dma('sp', fng[:, :], fng_d[:, :], [], [buf('fng')], 'const')
            cp(identb[:, :], identf[:, :], [buf('identf')], [buf('identb')])
            P.op('dve', lambda e: e.memset(onesf[:, :], 1.0), [], [buf('onesf')])
            P.op('dve', lambda e: e.memset(epsT[:, :], EPS), [], [buf('epsT')])
            P.op('dve', lambda e: e.memset(zeroT[:, :], 0.0), [], [buf('zeroT')])
            P.op('dve', lambda e: e.memset(V1[:, :, :, 64:128], 1.0), [], [buf('V1')])
            P.op('dve', lambda e: e.memset(QT[:, :, :], 0.0), [], [buf('QT')])
            act(csilu[:, :, :], cvec[:, :, :], AF.Silu, [buf('cvec')], [buf('csilu')])
            stages = [(FA[:, :, :].rearrange("p a b -> p (a b)"), bFA), (x_sb[:, :, :].rearrange("p a b -> p (a b)"), bX)]
            nblk = 0
            for l in range(NLAYER):
                pb = PSP.get()
                for j12 in range(12):
                    wst, wbufs = stages[nblk % 2]
                    dma('sp', wst, wmod_d[l * 12 + j12, :, :], [], wbufs, f'wmod{nblk % 2}')
                    nblk += 1
                    wv = wst.rearrange("p (k c) -> p k c", k=8)
                    for jj in range(4):
                        j = j12 * 4 + jj
                        for kc in range(KC):
                            mm(psum[pb][:, 2 * j:2 * j + 2], wv[:, kc, jj * 128:(jj + 1) * 128], csilu[:, kc, :],
                               kc == 0, kc == KC - 1, wbufs + [buf('csilu')], [bPS[pb]])
                tt(modT[:, l, :, :], psum[pb][:, 0:96].rearrange("p (j t) -> p j t", t=2),
                   bmodT[:, l, :].unsqueeze(2).to_broadcast([128, 48, 2]), ALU.add,
                   [bPS[pb], buf('bmodT')], [buf('modT')])
                PSP.put(pb)
                for n, (sc0, g0) in enumerate([(8, V_N1G), (32, V_N2G)]):
                    stt(gmT[:, l, n, :, :], modT[:, l, sc0:sc0 + 8, :], 1.0,
                        vecT[:, l, g0:g0 + 8].unsqueeze(2).to_broadcast([128, 8, 2]), ALU.add, ALU.mult,
                        [buf('modT'), buf('vecT')], [buf('gmT')])

        def layer_setup(l):
            dma('sp', vbc[:, :], vbc_d[:, l, :], [], [buf('vbc')], 'vbc')
            dma('pool', smallw[:, :], smallw_d[l, :, :], [], [buf('smallw')], 'small')

        def type_setup(l, t):
            for n, j0 in enumerate([16, 40]):
                for kc in range(8):
                    d = n * 8 + kc
                    ts(FB[:, d // 4, (d % 4) * 128:(d % 4 + 1) * 128], identf[:, :], modT[:, l, j0 + kc, t:t + 1], None, ALU.mult, None,
                       [buf('identf'), buf('modT')], [bFB[d // 4]])
            for n in range(2):
                for half in range(2):
                    pb = PSP.get()
                    for q4 in range(4):
                        d = n * 8 + half * 4 + q4
                        mm(psum[pb][:, q4 * 128:(q4 + 1) * 128], onesf[:, :], FB[:, d // 4, (d % 4) * 128:(d % 4 + 1) * 128], True, True,
                           [buf('onesf'), bFB[d // 4]], [bPS[pb]])
                    cp(gbc[:, n, half * 512:(half + 1) * 512], psum[pb][:, :], [bPS[pb]], [buf('gbc')])
                    PSP.put(pb)

        def load_x(src_d, tok0, nsub, kind):
            rd = [buf('x1' + kind)] if cur['l'] > 0 else []
            for s in range(nsub):
                dma('sp', x_sb[:, s, :], src_d[tok0 + s * 128:tok0 + (s + 1) * 128, :], rd, [bX[s]], f'x{s}')

        def rstd_from_ssq(nsub, scale):
            act(stat[:, 4:4 + nsub], stat[:, 0:nsub], AF.Sqrt, st(0, nsub), st(4, nsub), bias=epsT[:, 0:1], scale=scale)
            recip(stat[:, 8:8 + nsub], stat[:, 4:4 + nsub], st(4, nsub), st(8, nsub))

        xf = {'have': False}
        FBj = FB[:, 4:6, :].rearrange("p a b -> p (a b)")

        xn = {'ready': False}

        def norm_elem_from_x(nsub):
            for s in range(nsub):
                sumsq(FA[:, 2 * s:2 * s + 2, :].rearrange("p a b -> p (a b)"), x_sb[:, s, :], stat[:, s:s + 1],
                      [bX[s]], [bFA[2 * s], bFA[2 * s + 1]] + st(s))
            rstd_from_ssq(nsub, 1.0 / D)
            for s in range(nsub):
                act(FA[:, 2 * s:2 * s + 2, :].rearrange("p a b -> p (a b)"), x_sb[:, s, :], AF.Copy,
                    [bX[s]] + st(8 + s), [bFA[2 * s], bFA[2 * s + 1]], scale=stat[:, 8 + s:9 + s])

        def norm_elem_from_fa(nsub):
            for s in range(nsub):
                fa = FA[:, 2 * s:2 * s + 2, :].rearrange("p a b -> p (a b)")
                sumsq(FBj, fa, stat[:, s:s + 1], [bFA[2 * s], bFA[2 * s + 1]], [bFB[4], bFB[5]] + st(s))
            rstd_from_ssq(nsub, 1.0 / D)
            for s in range(nsub):
                fa = FA[:, 2 * s:2 * s + 2, :].rearrange("p a b -> p (a b)")
                act(fa, fa, AF.Copy, [bFA[2 * s], bFA[2 * s + 1]] + st(8 + s), [bFA[2 * s], bFA[2 * s + 1]], scale=stat[:, 8 + s:9 + s])

        def norm_to_hT(l, t, n, nsub, skip_elem=False):
            Tm = nsub * 128
            if skip_elem:
                pass
            elif n == 0 and xn['ready']:
                xn['ready'] = False
            elif n == 0 and xf['have']:
                xf['have'] = False
                norm_elem_from_fa(nsub)
            else:
                norm_elem_from_x(nsub)
            for kc in range(KC):
                pb = PSP.get()
                for s in range(nsub):
                    xin = FA[:, 2 * s + kc // 4, (kc % 4) * 128:(kc % 4 + 1) * 128]
                    tr(psum[pb][:, s * 128:(s + 1) * 128], xin, identf[:, :], [bFA[2 * s + kc // 4], buf('identf')], [bPS[pb]])
                sh = modT[:, l, (0 if n == 0 else 24) + kc, t:t + 1]
                gm = gmT[:, l, n, kc, t:t + 1]
                if kc % 2 == 0:
                    act(hT[:, kc, 0:Tm], psum[pb][:, 0:Tm], AF.Identity, [bPS[pb], buf('gmT'), buf('modT')], [bHT[kc]],
                        bias=sh, scale=gm)
                else:
                    ts(hT[:, kc, 0:Tm], psum[pb][:, 0:Tm], gm, sh, ALU.mult, ALU.add,
                       [bPS[pb], buf('gmT'), buf('modT')], [bHT[kc]])
                PSP.put(pb)

        def headnorm_rope(src_ps, pbuf, nh, g_off, gscale, rope_tile, out_ap, out_bufs, tmpi):
            W = nh * 64
            f_sq, f_n, f_r0, f_r1 = tmpi
            act(FB[:, f_sq, 0:W], src_ps, AF.Square, [pbuf], [bFB[f_sq]])
            P.op('dve', lambda e: e.tensor_reduce(out=stat[:, 16:16 + nh], in_=FB[:, f_sq, 0:W].rearrange("p (h d) -> p h d", h=nh),
                                                  axis=AX.X, op=ALU.add), [bFB[f_sq]], st(16, nh))
            act(stat[:, 24:24 + nh], stat[:, 16:16 + nh], AF.Sqrt, st(16, nh), st(24, nh), bias=epsT[:, 0:1], scale=1.0 / 64)
            recip(stat[:, 32:32 + nh], stat[:, 24:24 + nh], st(24, nh), st(32, nh))
            qn = FB[:, f_n, 0:W].rearrange("p (h d) -> p h d", h=nh)
            tt(qn, src_ps.rearrange("p (h d) -> p h d", h=nh), stat[:, 32:32 + nh].unsqueeze(2).to_broadcast([128, nh, 64]),
               ALU.mult, [pbuf] + st(32, nh), [bFB[f_n]])
            gb = vbc[:, g_off:g_off + 64].unsqueeze(1).to_broadcast([128, nh, 64])
            if rope_tile is None:
                stt(out_ap.rearrange("p (h d) -> p h d", h=nh), qn, gscale, gb, ALU.mult, ALU.mult,
                    [bFB[f_n], buf('vbc')], out_bufs)
                return
            stt(qn, qn, gscale, gb, ALU.mult, ALU.mult, [bFB[f_n], buf('vbc')], [bFB[f_n]])
            q4 = FB[:, f_n, 0:W].rearrange("p (h j t) -> p h j t", h=nh, t=2)
            x0 = q4[:, :, :, 0]
            x1 = q4[:, :, :, 1]
            cb_ = ropec[:, rope_tile, :].unsqueeze(1).to_broadcast([128, nh, 32])
            sb_ = ropes[:, rope_tile, :].unsqueeze(1).to_broadcast([128, nh, 32])
            H = nh * 32
            t1 = FB[:, f_r0, 0:H].rearrange("p (h j) -> p h j", h=nh)
            t2 = FB[:, f_r0, 256:256 + H].rearrange("p (h j) -> p h j", h=nh)
            t3 = FB[:, f_r1, 0:H].rearrange("p (h j) -> p h j", h=nh)
            t4 = FB[:, f_r1, 256:256 + H].rearrange("p (h j) -> p h j", h=nh)
            o4 = out_ap.rearrange("p (h j t) -> p h j t", h=nh, t=2)
            rd = [bFB[f_n], buf('ropec'), buf('ropes')]
            tt(t1, x0, cb_, ALU.mult, rd, [bFB[f_r0]])
            tt(t2, x1, sb_, ALU.mult, rd, [bFB[f_r0]])
            tt(t3, x0, sb_, ALU.mult, rd, [bFB[f_r1]])
            tt(t4, x1, cb_, ALU.mult, rd, [bFB[f_r1]])
            tt(o4[:, :, :, 0], t1, t2, ALU.subtract, [bFB[f_r0]], out_bufs)
            tt(o4[:, :, :, 1], t3, t4, ALU.add, [bFB[f_r1]], out_bufs)

        def seq_setup(l, kind, N):
            aTv = aT_d.rearrange("j p n -> p j n")
            plv = plT_d.rearrange("j p n -> p j n")
            z16 = zeroT[:, 0:64].rearrange("p (j n) -> p j n", j=4)
            z8 = zeroT[:, 0:32].rearrange("p (j n) -> p j n", j=4)
            nseg = 2 if kind == 'P' else 1
            L = SEQ_P if kind == 'P' else N
            for g in range(nseg):
                a0 = g * (L + 32)
                p0 = g * (L + 16)
                dma('sp', aTv[:, :, a0:a0 + 16], z16, [buf('zeroT')], [buf('aT_d')], 'pad')
                dma('sp', aTv[:, :, a0 + 16 + L:a0 + 32 + L], z16, [buf('zeroT')], [buf('aT_d')], 'pad')
                dma('sp', plv[:, :, p0:p0 + 8], z8, [buf('zeroT')], [buf('plT_d')], 'pad')
                dma('sp', plv[:, :, p0 + 8 + L:p0 + 16 + L], z8, [buf('zeroT')], [buf('plT_d')], 'pad')
            if kind == 'S':
                dma('sp', FA[:, 0, :].rearrange("p (k c) -> p k c", k=4), ck_d[l].rearrange("(k p) c -> p k c", p=128),
                    [], [bFA[0]], 'kvc')
                dma('sp', FA[:, 1, :].rearrange("p (k c) -> p k c", k=4), cv_d[l].rearrange("(k p) c -> p k c", p=128),
                    [], [bFA[1]], 'kvc')
                cp(q_bf[:, :], FA[:, 0, :], [bFA[0]], [buf('q_bf')])
                pb = PSP.get()
                pbv = psum[pb][:, :].bitcast(BF16)
                for k in range(4):
                    tr(pbv[:, k * 128:(k + 1) * 128], q_bf[:, k * 128:(k + 1) * 128], identb[:, :], [buf('q_bf'), buf('identb')], [bPS[pb]])
                cp(KT[:, 0:512], pbv[:, 0:512], [bPS[pb]], [buf('KT')])
                PSP.put(pb)
                cp(V1[:, 0:4, :, 0:64], FA[:, 1, :].rearrange("p (k v d) -> p k v d", k=4, v=2), [bFA[1]], [buf('V1')])

        xpre = {'have': False}

        def load_x_once(src_d, tok0, nsub, kind):
            if xpre['have']:
                xpre['have'] = False
            else:
                load_x(src_d, tok0, nsub, kind)

        def phaseA(l, kind, t, src_d, tok0, nsub, key0, segs, nxt):
            Tm = nsub * 128
            load_x_once(src_d, tok0, nsub, kind)
            norm_to_hT(l, t, 0, nsub)
            if nxt is not None:
                load_x(src_d, nxt[0], nxt[1], kind)
                xpre['have'] = True
            s_kv = WS.need(l, 'in_kv')
            wkv = wview(s_kv, 8, 256)
            kvb = []
            for s in range(nsub):
                pb = PSP.get()
                kvb.append(pb)
                for kc in range(KC):
                    mm(psum[pb][:, 0:256], hT[:, kc, s * 128:(s + 1) * 128], wkv[:, kc, :], kc == 0, kc == KC - 1,
                       bHT + [bW[s_kv]], [bPS[pb]])
            kt0 = (key0 + tok0) // 128
            for s in range(nsub):
                pb = kvb[s]
                gt = (tok0 // 128 + s) if kind == 'S' else None
                headnorm_rope(psum[pb][:, 0:128], bPS[pb], 2, B_KG, 1.0, gt, kvout[:, s, 0, :], [buf('awin')], (0, 1, 2, 3))
                cp(kvout[:, s, 1, :], psum[pb][:, 128:256], [bPS[pb]], [buf('awin')], eng='act_copy')
                PSP.put(pb)
                cp(V1[:, kt0 + s, :, 0:64], kvout[:, s, 1, :].rearrange("p (v d) -> p v d", v=2), [buf('awin')], [buf('V1')])
                cp(q_bf[:, s * 128:(s + 1) * 128], kvout[:, s, 0, :], [buf('awin')], [buf('q_bf')])
            if kind == 'P':
                for g in range(2):
                    dma('sp', nk_d[g, l, :, :].rearrange("(s p) c -> p s c", p=128), kvout[:, 2 * g:2 * g + 2, 0, :],
                        [buf('awin')], [], 'kvo')
                    dma('sp', nv_d[g, l, :, :].rearrange("(s p) c -> p s c", p=128), kvout[:, 2 * g:2 * g + 2, 1, :],
                        [buf('awin')], [], 'kvo')
            s_a0 = WS.need(l, 'in_a0')
            wa0 = wview(s_a0, 8, 512)
            s_a1 = WS.need(l, 'in_a1')
            wa1 = wview(s_a1, 8, 512)
            for j in range(4):
                pa = PSP.get()
                for kc in range(KC):
                    mm(psum[pa][:, 0:Tm], wa0[:, kc, j * 128:(j + 1) * 128], hT[:, kc, 0:Tm], kc == 0, kc == KC - 1,
                       bHT + [bW[s_a0]], [bPS[pa]])
                pb = PSP.get()
                for kc in range(KC):
                    mm(psum[pb][:, 0:Tm], wa1[:, kc, j * 128:(j + 1) * 128], hT[:, kc, 0:Tm], kc == 0, kc == KC - 1,
                       bHT + [bW[s_a1]], [bPS[pb]])
                fb = 4 + (j % 2)
                act(FB[:, fb, 0:Tm], psum[pb][:, 0:Tm], AF.Sigmoid, [bPS[pb]], [bFB[fb]])
                PSP.put(pb)
                tt(G[:, 0, j, 0:Tm], psum[pa][:, 0:Tm], FB[:, fb, 0:Tm], ALU.mult, [bPS[pa], bFB[fb]], [bG[0]])
                PSP.put(pa)
            for sg in segs:
                dma('sp', aT_d.rearrange("j p n -> p j n")[:, :, sg['apos']:sg['apos'] + sg['n']], G[:, 0, :, sg['col0']:sg['col0'] + sg['n']],
                    [bG[0]], [buf('aT_d')], 'ast')
            if nxt is not None:
                norm_elem_from_x(nxt[1])
                xn['ready'] = True
            s_pl = WS.need(l, 'in_pl')
            wpl = wview(s_pl, 8, 512)
            for j in range(4):
                pb = PSP.get()
                for kc in range(KC):
                    mm(psum[pb][:, 0:Tm], wpl[:, kc, j * 128:(j + 1) * 128], hT[:, kc, 0:Tm], kc == 0, kc == KC - 1,
                       bHT + [bW[s_pl]], [bPS[pb]])
                if j % 2 == 0:
                    cp(G[:, 1, j, 0:Tm], psum[pb][:, 0:Tm], [bPS[pb]], [bG[1]])
                else:
                    cp(G[:, 1, j, 0:Tm], psum[pb][:, 0:Tm], [bPS[pb]], [bG[1]], eng='act_copy')
                PSP.put(pb)
            for sg in segs:
                dma('sp', plT_d.rearrange("j p n -> p j n")[:, :, sg['ppos']:sg['ppos'] + sg['n']], G[:, 1, :, sg['col0']:sg['col0'] + sg['n']],
                    [bG[1]], [buf('plT_d')], 'pst')
            pb2 = PSP.get()
            pbv = psum[pb2][:, :].bitcast(BF16)
            for s in range(nsub):
                tr(pbv[:, s * 128:(s + 1) * 128], q_bf[:, s * 128:(s + 1) * 128], identb[:, :], [buf('q_bf'), buf('identb')], [bPS[pb2]])
            cp(KT[:, kt0 * 128:(kt0 + nsub) * 128], pbv[:, 0:nsub * 128], [bPS[pb2]], [buf('KT')])
            PSP.put(pb2)

        def gate_and_project(l, i, gi, Tm, pos, pump=None):
            for half in range(2):
                s_b = WS.need(l, f'br{i}_{half}')
                wb = wview(s_b, 4, 512)
                s_g = WS.need(l, f'gate{i}_{half}')
                wg = wview(s_g, 8, 512)
                for dq in range(4):
                    dc = half * 4 + dq
                    if pump is not None:
                        pump()
                    pg = PSP.get()
                    for kc in range(KC):
                        mm(psum[pg][:, 0:Tm], wg[:, kc, dq * 128:(dq + 1) * 128], hT[:, kc, 0:Tm], kc == 0, kc == KC - 1,
                           bHT + [bW[s_g]], [bPS[pg]])
                    fg = dc % 2
                    act(FB[:, fg, 0:Tm], psum[pg][:, 0:Tm], AF.Sigmoid, [bPS[pg], buf('vecT')], [bFB[fg]],
                        bias=vecT[:, l, V_BG + i * 8 + dc:V_BG + i * 8 + dc + 1])
                    PSP.put(pg)
                    pb = PSP.get()
                    for kc in range(4):
                        mm(psum[pb][:, 0:Tm], wb[:, kc, dq * 128:(dq + 1) * 128], G[:, gi, kc, 0:Tm], kc == 0, kc == 3,
                           [bG[gi], bW[s_b]], [bPS[pb]])
                    bm = buf(f'merged{dc}')
                    if pos == 'first':
                        tt(merged[:, dc, 0:Tm], FB[:, fg, 0:Tm], psum[pb][:, 0:Tm], ALU.mult, [bFB[fg], bPS[pb]], [bm])
                    else:
                        ft = 2 + dc % 2
                        tt(FB[:, ft, 0:Tm], FB[:, fg, 0:Tm], psum[pb][:, 0:Tm], ALU.mult, [bFB[fg], bPS[pb]], [bFB[ft]])
                        if pos != 'last':
                            tt(merged[:, dc, 0:Tm], merged[:, dc, 0:Tm], FB[:, ft, 0:Tm], ALU.add, [bm, bFB[ft]], [bm])
                        else:
                            tt(merged_bf[:, dc, 0:Tm], merged[:, dc, 0:Tm], FB[:, ft, 0:Tm], ALU.add, [bm, bFB[ft]],
                               [buf(f'mbf{dc}')])
                    PSP.put(pb)

        def phaseB(l, kind, t, src_d, dst_d, tok0, nsub, segs, final, nxt):
            Tm = nsub * 128
            nsg = len(segs)
            wa0_ = segs[0]['apos'] - 16
            wp0_ = segs[0]['ppos'] - 8
            load_x_once(src_d, tok0, nsub, kind)
            dma('sp', awin[:, :, 0:Tm + 32 * nsg], aT_d.rearrange("j p n -> p j n")[:, :, wa0_:wa0_ + Tm + 32 * nsg], [buf('aT_d')], [buf('awin')], 'awin')
            dma('sp', plwin[:, :, 0:Tm + 16 * nsg], plT_d.rearrange("j p n -> p j n")[:, :, wp0_:wp0_ + Tm + 16 * nsg], [buf('plT_d')], [buf('plwin')], 'plwin')
            norm_to_hT(l, t, 0, nsub)
            s_q = WS.need(l, 'in_q')
            wq = wview(s_q, 8, 512)
            qb = []
            for s in range(nsub):
                pb = PSP.get()
                qb.append(pb)
                for kc in range(KC):
                    mm(psum[pb][:, :], hT[:, kc, s * 128:(s + 1) * 128], wq[:, kc, :], kc == 0, kc == KC - 1,
                       bHT + [bW[s_q]], [bPS[pb]])
            for s in range(nsub):
                gt = (tok0 // 128 + s) if kind == 'S' else None
                headnorm_rope(psum[qb[s]][:, :], bPS[qb[s]], 8, B_QG, 0.125, gt, G[:, 2, s, :], [bG[2]], (2, 3, 4, 5))
                PSP.put(qb[s])
            s_u = WS.need(l, 'in_u')
            wu = wview(s_u, 8, 512)
            for j in range(4):
                pb = PSP.get()
                for kc in range(KC):
                    mm(psum[pb][:, 0:Tm], wu[:, kc, j * 128:(j + 1) * 128], hT[:, kc, 0:Tm], kc == 0, kc == KC - 1,
                       bHT + [bW[s_u]], [bPS[pb]])
                act(FA[:, j, 0:Tm], psum[pb][:, 0:Tm], AF.Gelu_apprx_tanh, [bPS[pb]], [bFA[j]])
                PSP.put(pb)
            s_v = WS.need(l, 'in_v')
            wv = wview(s_v, 8, 512)
            vn = FA[:, 6:8, :].rearrange("p a b -> p (a b)").bitcast(BF16).rearrange("p (s c) -> p s c", s=4)
            vb = []
            for s in range(nsub):
                pb = PSP.get()
                vb.append(pb)
                for kc in range(KC):
                    mm(psum[pb][:, :], hT[:, kc, s * 128:(s + 1) * 128], wv[:, kc, :], kc == 0, kc == KC - 1,
                       bHT + [bW[s_v]], [bPS[pb]])
            gel = [(FA[:, 4, :], bFA[4]), (FA[:, 5, :], bFA[5]), (FB[:, 4, :], bFB[4]), (FB[:, 5, :], bFB[5])]
            for s in range(nsub):
                act(gel[s][0], psum[vb[s]][:, :], AF.Gelu_apprx_tanh, [bPS[vb[s]]], [gel[s][1]])
                PSP.put(vb[s])
            for s in range(nsub):
                pb2 = PSP.get()
                pbv = psum[pb2][:, :].bitcast(BF16)
                for j in range(4):
                    tr(pbv[:, j * 128:(j + 1) * 128], G[:, 2, s, j * 128:(j + 1) * 128], identb[:, :], [bG[2], buf('identb')], [bPS[pb2]])
                cp(QT[0:64, 0:4, s * 128:(s + 1) * 128], pbv[0:64, 0:512].rearrange("p (j c) -> p j c", j=4), [bPS[pb2]], [buf('QT')])
                cp(QT[64:128, 4:8, s * 128:(s + 1) * 128], pbv[64:128, 0:512].rearrange("p (j c) -> p j c", j=4), [bPS[pb2]], [buf('QT')],
                   eng='act_copy')
                PSP.put(pb2)
            v_ops = []

            def v_stage_a():
                for s in range(nsub):
                    ft = s % 2
                    tt(FB[:, ft, :], gel[s][0], gel[s][0], ALU.mult, [gel[s][1]], [bFB[ft]])
                    P.op('dve', lambda e, s=s, ft=ft: e.tensor_reduce(out=stat[:, 40 + s:41 + s], in_=FB[:, ft, :], axis=AX.X, op=ALU.add),
                         [bFB[ft]], st(40 + s))

            def v_stage_b():
                act(stat[:, 44:44 + nsub], stat[:, 40:40 + nsub], AF.Ln, st(40, nsub), st(44, nsub), bias=epsT[:, 0:1], scale=1.0 / 512)
                act(stat[:, 48:48 + nsub], stat[:, 44:44 + nsub], AF.Exp, st(44, nsub), st(48, nsub), scale=-0.5)
                for s in range(nsub):
                    stt(vn[:, s, :], gel[s][0], stat[:, 48 + s:49 + s], vbc[:, B_SG:B_SG + 512], ALU.mult, ALU.mult,
                        [gel[s][1], buf('vbc')] + st(48 + s), [bFA[6], bFA[7]])
            v_ops.append(v_stage_a)
            v_ops.append(v_stage_b)
            def sgu_spatial():
                swT = smallw[:, 512:1024].rearrange("p (g i) -> p g i", g=4)
                sgb = vbc[:, B_SB:B_SB + 512].rearrange("p (g i) -> p g i", g=4)
                for g in range(4):
                    pb = PSP.get()
                    for s in range(nsub):
                        mm(psum[pb][:, s * 128:(s + 1) * 128], vn[:, s, g * 128:(g + 1) * 128], swT[:, g, :], True, True,
                           [bFA[6], bFA[7], buf('smallw')], [bPS[pb]])
                    ft = g % 2
                    tt(FB[:, ft, 0:Tm].rearrange("p (s i) -> p s i", s=nsub), psum[pb][:, 0:Tm].rearrange("p (s i) -> p s i", s=nsub),
                       sgb[:, g, :].unsqueeze(1).to_broadcast([128, nsub, 128]), ALU.add, [bPS[pb], buf('vbc')], [bFB[ft]])
                    PSP.put(pb)
                    tt(G[:, 2, g, 0:Tm], FB[:, ft, 0:Tm], FA[:, g, 0:Tm], ALU.mult, [bFB[ft], bFA[g]], [bG[2]])
            cw = vecT[:, l, V_CW:V_CW + 124].rearrange("p (c j) -> p c j", c=4)
            conv_ops = []
            if nsg == 2:
                n2 = segs[0]['n']
                for j in range(31):
                    for c in range(4):
                        outv = FA[:, c, 0:2 * n2].rearrange("p (g n) -> p g n", g=2)
                        inv = awin[:, c, 0:2 * (n2 + 32)].rearrange("p (g n) -> p g n", g=2)[:, :, j + 1:j + 1 + n2]
                        if j == 0:
                            conv_ops.append(lambda c=c, outv=outv, inv=inv: ts(
                                outv, inv, cw[:, c, 0:1], vecT[:, l, V_CB + c:V_CB + c + 1],
                                ALU.mult, ALU.add, [buf('awin'), buf('vecT')], [bFA[c]]))
                        else:
                            conv_ops.append(lambda c=c, j=j, outv=outv, inv=inv: stt(
                                outv, inv, cw[:, c, j:j + 1], outv,
                                ALU.mult, ALU.add, [buf('awin'), buf('vecT'), bFA[c]], [bFA[c]]))
            else:
                for j in range(31):
                    for c in range(4):
                        for gi_, sg in enumerate(segs):
                            c0, n_, wo_ = sg['col0'], sg['n'], gi_ * (sg['n'] + 32)
                            if j == 0:
                                conv_ops.append(lambda c=c, c0=c0, n_=n_, wo_=wo_: ts(
                                    FA[:, c, c0:c0 + n_], awin[:, c, wo_ + 1:wo_ + 1 + n_], cw[:, c, 0:1], vecT[:, l, V_CB + c:V_CB + c + 1],
                                    ALU.mult, ALU.add, [buf('awin'), buf('vecT')], [bFA[c]]))
                            else:
                                conv_ops.append(lambda c=c, j=j, c0=c0, n_=n_, wo_=wo_: stt(
                                    FA[:, c, c0:c0 + n_], awin[:, c, wo_ + j + 1:wo_ + j + 1 + n_], cw[:, c, j:j + 1], FA[:, c, c0:c0 + n_],
                                    ALU.mult, ALU.add, [buf('awin'), buf('vecT'), bFA[c]], [bFA[c]]))

            def pump(ops, n):
                for _ in range(min(n, len(ops))):
                    ops.pop(0)()
            sqt = [0, 1, 4, 5]
            presq = {'done': False}
            for h in range(8):
                kcA, hh = h // 2, h % 2
                hf = h // 4
                if h == 1:
                    while v_ops:
                        v_ops.pop(0)()
                if h == 2:
                    sgu_spatial()
                if h >= 2:
                    pump(conv_ops, 21 if nsg == 1 else 10)
                if h == 7 and nsg == 1:
                    pump(conv_ops, 1000)
                    for c in range(4):
                        tt(FB[:, sqt[c], 0:Tm], FA[:, c, 0:Tm], FA[:, c, 0:Tm], ALU.mult, [bFA[c]], [bFB[sqt[c]]])
                    presq['done'] = True
                po = PSP.get()
                pss = []
                work = [(sg, kt) for sg in segs for kt in range(sg['kt0'], sg['kt0'] + sg['nkt'])]

                def s_mm(i):
                    sg, kt = work[i]
                    pb = PSP.get()
                    mm(psum[pb][:, 0:sg['n']], KT[:, kt * 128:(kt + 1) * 128], QT[:, h, sg['col0']:sg['col0'] + sg['n']],
                       True, True, [buf('KT'), buf('QT')], [bPS[pb]])
                    pss.append(pb)
                s_mm(0)
                if len(work) > 1:
                    s_mm(1)
                for i, (sg, kt) in enumerate(work):
                    pb = pss[i]
                    pi = i % 3
                    c0, n_ = sg['col0'], sg['n']
                    act(pT[:, pi, 0:n_], psum[pb][:, 0:n_], AF.Exp, [bPS[pb]], [bPT[pi]])
                    PSP.put(pb)
                    if h == 0 and v_ops and i % 6 == 5:
                        v_ops.pop(0)()
                    if i + 2 < len(work):
                        s_mm(i + 2)
                    mm(psum[po][:, c0:c0 + n_], V1[:, kt, hf, :], pT[:, pi, 0:n_], kt == sg['kt0'], kt == sg['kt0'] + sg['nkt'] - 1,
                       [buf('V1'), bPT[pi]], [bPS[po]])
                fr = 2 + h % 2
                recip(FB[0:64, fr, 0:Tm], psum[po][64:128, 0:Tm], [bPS[po]], [bFB[fr]])
                tt(G[hh * 64:(hh + 1) * 64, 0, kcA, 0:Tm], psum[po][0:64, 0:Tm], FB[0:64, fr, 0:Tm], ALU.mult, [bPS[po], bFB[fr]], [bG[0]])
                PSP.put(po)
            def conv_ln():
                p1 = PSP.get()
                p2 = PSP.get()
                for c in range(4):
                    mm(psum[p1][:, 0:Tm], onesf[:, :], FA[:, c, 0:Tm], c == 0, c == 3, [buf('onesf'), bFA[c]], [bPS[p1]])
                for c in range(4):
                    if presq['done']:
                        fs = sqt[c]
                    else:
                        fs = 4 + c % 2
                        act(FB[:, fs, 0:Tm], FA[:, c, 0:Tm], AF.Square, [bFA[c]], [bFB[fs]])
                    mm(psum[p2][:, 0:Tm], onesf[:, :], FB[:, fs, 0:Tm], c == 0, c == 3, [buf('onesf'), bFB[fs]], [bPS[p2]])
                act(FA[:, 4, 0:Tm], psum[p1][:, 0:Tm], AF.Copy, [bPS[p1]], [bFA[4]], scale=1.0 / 512)
                act(FA[:, 5, 0:Tm], psum[p1][:, 0:Tm], AF.Square, [bPS[p1]], [bFA[5]], scale=1.0 / 512)
                stt(FA[:, 6, 0:Tm], psum[p2][:, 0:Tm], 1.0 / 512, FA[:, 5, 0:Tm], ALU.mult, ALU.subtract, [bPS[p2], bFA[5]], [bFA[6]])
                PSP.put(p1)
                PSP.put(p2)
                act(FA[:, 5, 0:Tm], FA[:, 6, 0:Tm], AF.Sqrt, [bFA[6]], [bFA[5]], bias=epsT[:, 0:1], scale=1.0)
                recip(FA[:, 6, 0:Tm], FA[:, 5, 0:Tm], [bFA[5]], [bFA[6]])
                for c in range(4):
                    fy = 4 + c % 2
                    tt(FB[:, fy, 0:Tm], FA[:, c, 0:Tm], FA[:, 4, 0:Tm], ALU.subtract, [bFA[c], bFA[4]], [bFB[fy]])
                    tt(FB[:, fy, 0:Tm], FB[:, fy, 0:Tm], FA[:, 6, 0:Tm], ALU.mult, [bFB[fy], bFA[6]], [bFB[fy]])
                    act(G[:, 1, c, 0:Tm], FB[:, fy, 0:Tm], AF.Silu, [bFB[fy], buf('vecT')], [bG[1]],
                        bias=vecT[:, l, V_LB + c:V_LB + c + 1], scale=vecT[:, l, V_LG + c:V_LG + c + 1])

            if nsg == 1:
                pump(conv_ops, 1000)
                conv_ln()
                gate_and_project(l, 0, 0, Tm, 'first')
                gate_and_project(l, 3, 2, Tm, 'mid')
            else:
                gate_and_project(l, 0, 0, Tm, 'first', pump=lambda: pump(conv_ops, 12))
                gate_and_project(l, 3, 2, Tm, 'mid', pump=lambda: pump(conv_ops, 12))
                pump(conv_ops, 1000)
                conv_ln()
            fa_flat = FA[:, :, :].rearrange("p a b -> p (a b)")
            pw_ = smallw[:, 0:512].rearrange("p (g d) -> p g d", g=4)
            pool_ops = []

            def pool_group(g):
                w = 2 ** (g + 1)
                hw = w // 2
                for gi_, sg in enumerate(segs):
                    c0, n_ = sg['col0'], sg['n']
                    xw = plwin[:, g, gi_ * (n_ + 16):gi_ * (n_ + 16) + n_ + 16]

                    def seg_ops(c0=c0, n_=n_, xw=xw, sg=sg):
                        if g == 0:
                            src, srcb = xw, [buf('plwin')]
                        else:
                            L1 = n_ + 15
                            pool_ops.append(lambda: tt(fa_flat[:, 0:L1], xw[:, 0:L1], xw[:, 1:L1 + 1], ALU.add, [buf('plwin')], [bFA[0], bFA[1]]))
                            src, srcb = fa_flat[:, 0:1024], [bFA[0], bFA[1]]
                            if g >= 2:
                                L2 = n_ + 13
                                pool_ops.append(lambda: tt(fa_flat[:, 1024:1024 + L2], fa_flat[:, 0:L2], fa_flat[:, 2:L2 + 2], ALU.add, [bFA[0], bFA[1]], [bFA[2], bFA[3]]))
                                src, srcb = fa_flat[:, 1024:2048], [bFA[2], bFA[3]]
                            if g >= 3:
                                L3 = n_ + 9
                                pool_ops.append(lambda: tt(fa_flat[:, 2048:2048 + L3], fa_flat[:, 1024:1024 + L3], fa_flat[:, 1028:1028 + L3], ALU.add, [bFA[2], bFA[3]], [bFA[4], bFA[5]]))
                                src, srcb = fa_flat[:, 2048:3072], [bFA[4], bFA[5]]
                        pool_ops.append(lambda: tt(FA[:, 6, c0:c0 + n_], src[:, 8 - hw:8 - hw + n_], src[:, 8:8 + n_], ALU.add, srcb, [bFA[6]]))
                        pool_ops.append(lambda: ts(FA[:, 6, c0:c0 + n_], FA[:, 6, c0:c0 + n_], 1.0 / w, None, ALU.mult, None, [bFA[6]], [bFA[6]]))
                        if sg['first']:
                            pool_ops.append(lambda: tt(FA[:, 6, c0:c0 + 8], FA[:, 6, c0:c0 + 8], pcorr[:, g, 0, :], ALU.mult, [bFA[6], buf('pcorr')], [bFA[6]]))
                        if sg['last']:
                            pool_ops.append(lambda: tt(FA[:, 6, c0 + n_ - 8:c0 + n_], FA[:, 6, c0 + n_ - 8:c0 + n_], pcorr[:, g, 1, :], ALU.mult, [bFA[6], buf('pcorr')], [bFA[6]]))
                        pool_ops.append(lambda: tt(pooled_bf[:, c0:c0 + n_], FA[:, 6, c0:c0 + n_], xw[:, 8:8 + n_], ALU.subtract, [bFA[6], buf('plwin')], [buf('pooled')]))
                    seg_ops()

                def fin():
                    pb = PSP.get()
                    mm(psum[pb][:, 0:Tm], pw_[:, g, :], pooled_bf[:, 0:Tm], True, True, [buf('smallw'), buf('pooled')], [bPS[pb]])
                    act(G[:, 0, g, 0:Tm], psum[pb][:, 0:Tm], AF.Copy, [bPS[pb], buf('vecT')], [bG[0]], scale=vecT[:, l, V_PS + g:V_PS + g + 1])
                    PSP.put(pb)
                pool_ops.append(fin)
            for g in range(4):
                pool_group(g)
            gate_and_project(l, 1, 1, Tm, 'mid', pump=lambda: pump(pool_ops, 4 * nsg))
            pump(pool_ops, 1000)
            gate_and_project(l, 2, 0, Tm, 'last')
            mbf = [buf(f'mbf{dc}') for dc in range(8)]
            s_oo = [WS.need(l, 'out_0'), WS.need(l, 'out_1')]

            def norm2_scale(s):
                recip(stat[:, 8 + s:9 + s], stat[:, 4 + s:5 + s], st(4 + s), st(8 + s))
                act(FA[:, 2 * s:2 * s + 2, :].rearrange("p a b -> p (a b)"), x_sb[:, s, :], AF.Copy,
                    [bX[s]] + st(8 + s), [bFA[2 * s], bFA[2 * s + 1]], scale=stat[:, 8 + s:9 + s])
            for s in range(nsub):
                for cb in range(2):
                    s_o = s_oo[cb]
                    wo = wview(s_o, 8, 512)
                    pb = PSP.get()
                    for kc in range(KC):
                        mm(psum[pb][:, :], merged_bf[:, kc, s * 128:(s + 1) * 128], wo[:, kc, :], kc == 0, kc == KC - 1,
                           mbf + [bW[s_o]], [bPS[pb]])
                    ft = 2 + cb
                    tt(FB[:, ft, :], psum[pb][:, :], gbc[:, 0, cb * 512:(cb + 1) * 512], ALU.mult, [bPS[pb], buf('gbc')], [bFB[ft]])
                    PSP.put(pb)
                    tt(x_sb[:, s, cb * 512:(cb + 1) * 512], x_sb[:, s, cb * 512:(cb + 1) * 512], FB[:, ft, :], ALU.add, [bX[s], bFB[ft]], [bX[s]])
                sumsq(FA[:, 2 * s:2 * s + 2, :].rearrange("p a b -> p (a b)"), x_sb[:, s, :], stat[:, s:s + 1],
                      [bX[s]], [bFA[2 * s], bFA[2 * s + 1]] + st(s))
                act(stat[:, 4 + s:5 + s], stat[:, s:s + 1], AF.Sqrt, st(s), st(4 + s), bias=epsT[:, 0:1], scale=1.0 / D)
                if s > 0:
                    norm2_scale(s - 1)
            norm2_scale(nsub - 1)
            norm_to_hT(l, t, 1, nsub, skip_elem=True)
            f1 = [FA[:, 0:4, :].rearrange("p a b -> p (a b)").bitcast(BF16).rearrange("p (k c) -> p k c", k=8),
                  FA[:, 4:8, :].rearrange("p a b -> p (a b)").bitcast(BF16).rearrange("p (k c) -> p k c", k=8)]
            f1.append(merged_bf)
            mbfb = [buf(f'mbf{dc}') for dc in range(8)]
            f1bufs = [[bFAh[jc] for jc in range(8)], [bFAh[8 + jc] for jc in range(8)], mbfb]
            for gq in range(4):
                fi = gq % 2 if gq < 3 else 2
                if gq == 3 and nxt is not None:
                    dma('sp', FA[:, 0:2 * nxt[1], :].rearrange("p (s a) b -> p s (a b)", a=2),
                        src_d[nxt[0]:nxt[0] + nxt[1] * 128, :].rearrange("(s p) d -> p s d", p=128),
                        [buf('x1' + kind)] if cur['l'] > 0 else [], bFA[0:2 * nxt[1]], 'xf')
                    xf['have'] = True
                for half in range(2):
                    s_i = WS.need(l, f'mi_{gq * 2 + half}')
                    wi = wview(s_i, 8, 512)
                    for dq in range(4):
                        jc = half * 4 + dq
                        pb = PSP.get()
                        for kc in range(KC):
                            mm(psum[pb][:, 0:Tm], wi[:, kc, dq * 128:(dq + 1) * 128], hT[:, kc, 0:Tm], kc == 0, kc == KC - 1,
                               bHT + [bW[s_i]], [bPS[pb]])
                        fr = 4 + jc % 2
                        act(FB[:, fr, 0:Tm], psum[pb][:, 0:Tm], AF.Relu, [bPS[pb]], [bFB[fr]])
                        PSP.put(pb)
                        tt(f1[fi][:, jc, 0:Tm], FB[:, fr, 0:Tm], FB[:, fr, 0:Tm], ALU.mult, [bFB[fr]], [f1bufs[fi][jc]])
                if gq == 3 and xf['have']:
                    xf['have'] = False
                    norm_elem_from_fa(nxt[1])
                    xn['ready'] = True

                def mo_step(cb, s_o, s):
                    wo = wview(s_o, 8, 512)
                    pb = PSP.get()
                    for jc in range(8):
                        mm(psum[pb][:, :], f1[fi][:, jc, s * 128:(s + 1) * 128], wo[:, jc, :], jc == 0, jc == 7,
                           [f1bufs[fi][jc], bW[s_o]], [bPS[pb]])
                    ft = 2 + s % 2
                    tt(FB[:, ft, :], psum[pb][:, :], gbc[:, 1, cb * 512:(cb + 1) * 512], ALU.mult, [bPS[pb], buf('gbc')], [bFB[ft]])
                    PSP.put(pb)
                    tt(x_sb[:, s, cb * 512:(cb + 1) * 512], x_sb[:, s, cb * 512:(cb + 1) * 512], FB[:, ft, :], ALU.add, [bX[s], bFB[ft]], [bX[s]])
                if gq < 3:
                    for cb in range(2):
                        s_o = WS.need(l, f'mo_{cb}_{gq}')
                        for s in range(nsub):
                            mo_step(cb, s_o, s)
                else:
                    s_o0 = WS.need(l, f'mo_0_{gq}')
                    s_o1 = WS.need(l, f'mo_1_{gq}')
                    def finish_subtile(s):
                        if final:
                            stt(x_sb[:, s, :], x_sb[:, s, :], stat[:, 8 + s:9 + s], fng[:, :], ALU.mult, ALU.mult,
                                [bX[s], buf('fng')] + st(8 + s), [bX[s]])
                        dma('sp', dst_d[tok0 + s * 128:tok0 + (s + 1) * 128, :], x_sb[:, s, :], [bX[s]],
                            [buf(('y' if final else 'x1') + kind)], f'xo{s}')
                        if nxt is not None and s < nxt[1]:
                            rd = [buf('x1' + kind)] if cur['l'] > 0 else []
                            dma('sp', x_sb[:, s, :], src_d[nxt[0] + s * 128:nxt[0] + (s + 1) * 128, :], rd, [bX[s]], f'x{s}')
                    for s in range(nsub):
                        mo_step(0, s_o0, s)
                        mo_step(1, s_o1, s)
                        if final:
                            sumsq(FB[:, 4:6, :].rearrange("p a b -> p (a b)"), x_sb[:, s, :], stat[:, s:s + 1],
                                  [bX[s]], [bFB[4], bFB[5]] + st(s))
                            act(stat[:, 4 + s:5 + s], stat[:, s:s + 1], AF.Sqrt, st(s), st(4 + s), bias=epsT[:, 0:1], scale=1.0 / D)
                            if s > 0:
                                recip(stat[:, 8 + s - 1:9 + s - 1], stat[:, 4 + s - 1:5 + s - 1], st(4 + s - 1), st(8 + s - 1))
                                finish_subtile(s - 1)
                        else:
                            finish_subtile(s)
                    if final:
                        recip(stat[:, 8 + nsub - 1:9 + nsub - 1], stat[:, 4 + nsub - 1:5 + nsub - 1], st(4 + nsub - 1), st(8 + nsub - 1))
                        finish_subtile(nsub - 1)
                    if nxt is not None:
                        xpre['have'] = True

        def whole_program():
            if not P.dry:
                setup()
            for l in range(NLAYER):
                cur['l'] = l
                if not P.dry:
                    layer_setup(l)
                final = (l == NLAYER - 1)
                for kind in ('P', 'S'):
                    t = 0 if kind == 'P' else 1
                    if kind == 'P':
                        N = 2 * SEQ_P
                        src = xp_d if l == 0 else x1p_d
                        dst = yp_d if final else x1p_d
                        key0 = 0
                    else:
                        N = NS
                        src = xs_d if l == 0 else x1s_d
                        dst = ys_d if final else x1s_d
                        key0 = PAST
                    tiles = []
                    tk = 0
                    while tk < N:
                        ns_ = min(4, (N - tk) // 128)
                        if kind == 'P':
                            segs = [dict(col0=g * SEQ_P, n=SEQ_P, kt0=2 * g, nkt=2, apos=g * (SEQ_P + 32) + 16, ppos=g * (SEQ_P + 16) + 8,
                                         first=True, last=True) for g in range(2)]
                        else:
                            segs = [dict(col0=0, n=ns_ * 128, kt0=0, nkt=(N + key0) // 128, apos=16 + tk, ppos=8 + tk,
                                         first=(tk == 0), last=(tk + ns_ * 128 == N))]
                        tiles.append((tk, ns_, segs))
                        tk += ns_ * 128
                    if not P.dry:
                        type_setup(l, t)
                        seq_setup(l, kind, N)
                    for ti, (tok0, ns_, segs) in enumerate(tiles):
                        nxt = tiles[ti + 1][0:2] if ti + 1 < len(tiles) else tiles[0][0:2]
                        phaseA(l, kind, t, src, tok0, ns_, key0, segs, nxt)
                    for ti, (tok0, ns_, segs) in enumerate(tiles):
                        nxt = tiles[ti + 1][0:2] if ti + 1 < len(tiles) else None
                        phaseB(l, kind, t, src, dst, tok0, ns_, segs, final, nxt)

        P.dry = True
        whole_program()
        P.dry = False
        whole_program()
        assert WS.pos == len(WS.sched)
        fw = {'sp': [('c', c, P.chan_cnt[c]) for c in ('xo0', 'xo1', 'xo2', 'xo3', 'kvo') if c in P.chan_cnt]}
        finalize_and_emit(P, block, sems, csems, fw)
    return nc


_CACHE = {}


def run_cores(inputs, NS, ncores):
    hw = host_weights(inputs)
    hc = host_consts(NS)
    xp = np.asarray(inputs['x_prompt'], np.float32)
    xs = np.asarray(inputs['x_sample'], np.float32)
    ck = np.asarray(inputs['cache_k'], np.float32)
    cv = np.asarray(inputs['cache_v'], np.float32)
    c = np.asarray(inputs['c'], np.float32)
    c_ctx = np.asarray(inputs['c_ctx'], np.float32)
    in_maps = []
    for i in range(ncores):
        cvec = np.stack([c_ctx.reshape(8, 128).T, c[i].reshape(8, 128).T], axis=-1).astype(np.float32)
        m = dict(xp=np.ascontiguousarray(xp[2 * i:2 * i + 2].reshape(2 * SEQ_P, D)),
                 xs=np.ascontiguousarray(xs[i]),
                 ck=np.ascontiguousarray(ck[i].reshape(NLAYER, PAST, 128)),
                 cv=np.ascontiguousarray(cv[i].reshape(NLAYER, PAST, 128)),
                 cvec=np.ascontiguousarray(cvec))
        m.update(hw)
        m.update(hc)
        in_maps.append(m)
    if NS not in _CACHE:
        _CACHE[NS] = build_program(NS)
    nc = _CACHE[NS]
    res = run_bass_kernel_spmd(nc, in_maps, core_ids=list(range(ncores)))
    yp = np.stack([r['yp'].reshape(2, SEQ_P, D) for r in res.results]).reshape(2 * ncores, SEQ_P, D)
    ys = np.stack([r['ys'] for r in res.results])
    nk = np.stack([r['nk'] for r in res.results]).reshape(2 * ncores, NLAYER, SEQ_P, 2, 64)
    nv = np.stack([r['nv'] for r in res.results]).reshape(2 * ncores, NLAYER, SEQ_P, 2, 64)
    return (yp.astype(np.float32), ys.astype(np.float32), nk.astype(np.float32), nv.astype(np.float32))


def kernel(**inputs):
    NS = int(np.asarray(inputs['x_sample']).shape[1])
    ncores = int(np.asarray(inputs['x_sample']).shape[0])
    return run_cores(inputs, NS, ncores)
```

```python
import numpy as np
from contextlib import ExitStack
import concourse.bass as bass
import concourse.mybir as mybir
from concourse.bass_utils import run_bass_kernel_spmd

F32 = mybir.dt.float32
BF16 = mybir.dt.bfloat16
AF = mybir.ActivationFunctionType
ALU = mybir.AluOpType
AX = mybir.AxisListType

D = 1024
KC = 8
EPS = 1e-6
NLAYER = 2
SEQ_P = 256
PAST = 512
NSLOT = 4
COMPUTE = ('pe', 'act', 'dve', 'pool')


class Buf:
    __slots__ = ('name', 'w', 'r')

    def __init__(self, name):
        self.name = name
        self.w = None
        self.r = []


class Ins:
    __slots__ = ('eng', 'fn', 'deps', 'sig', 'ticket', 'dma', 'chan', 'dmaval')

    def __init__(self, eng, fn, dma=False, chan=None):
        self.eng = eng
        self.fn = fn
        self.deps = []
        self.sig = False
        self.ticket = None
        self.dma = dma
        self.chan = chan
        self.dmaval = None


def _flat(x):
    out = []
    for b in x:
        if isinstance(b, (list, tuple)):
            out.extend(_flat(b))
        else:
            out.append(b)
    return out


class Prog:
    def __init__(self):
        self.q = {e: [] for e in ('pe', 'act', 'dve', 'pool', 'sp')}
        self.chan_cnt = {}
        self.dry = False

    def op(self, eng, fn, reads=(), writes=(), dma=False, chan=None):
        if self.dry:
            return None
        ins = Ins(eng, fn, dma=dma, chan=chan)
        reads = _flat(reads)
        writes = _flat(writes)
        raw = []
        oth = []
        for b in reads:
            if b.w is not None:
                raw.append(b.w)
        for b in writes:
            if b.r:
                oth.extend(b.r)
            elif b.w is not None:
                oth.append(b.w)
        for b in reads:
            if not dma:
                b.r = [x for x in b.r if not (x.eng == eng and not x.dma)]
            b.r.append(ins)
        for b in writes:
            b.w = ins
            b.r = []
        seen = set()
        for d in raw:
            if d is ins or id(d) in seen:
                continue
            seen.add(id(d))
            ins.deps.append((d, self.chan_cnt[d.chan] if d.dma else None))
        for d in oth:
            if d is ins or id(d) in seen:
                continue
            seen.add(id(d))
            if (not dma) and (not d.dma) and d.eng == eng and eng == 'pe':
                continue
            ins.deps.append((d, self.chan_cnt[d.chan] if d.dma else None))
        if dma:
            c = self.chan_cnt.get(chan, 0) + 16
            self.chan_cnt[chan] = c
            ins.dmaval = c
        self.q[eng].append(ins)
        return ins


def finalize_and_emit(prog, block, sems, chan_sems, final_waits):
    for e in prog.q:
        for ins in prog.q[e]:
            for d, _v in ins.deps:
                d.sig = True
    for e in COMPUTE:
        t = 0
        for ins in prog.q[e]:
            if ins.dma:
                continue
            if ins.sig:
                t += 1
                ins.ticket = t

    def emit_engine(eng_name, eng):
        waited = {}
        for ins in prog.q[eng_name]:
            need = {}
            for d, v in ins.deps:
                if d.dma:
                    key = ('c', d.chan)
                    val = v
                else:
                    key = ('e', d.eng)
                    val = d.ticket
                if val > need.get(key, 0):
                    need[key] = val
            for key, val in need.items():
                if waited.get(key, 0) >= val:
                    continue
                waited[key] = val
                sem = chan_sems[key[1]] if key[0] == 'c' else sems[key[1]]
                eng.wait_ge(sem, val)
            r = ins.fn(eng)
            if ins.dma:
                r.then_inc(chan_sems[ins.chan], 16)
            elif ins.sig:
                r.then_inc(sems[ins.eng], 1)
        for (kind, name, val) in final_waits.get(eng_name, []):
            sem = chan_sems[name] if kind == 'c' else sems[name]
            eng.wait_ge(sem, val)

    @block.tensor
    def _(e):
        emit_engine('pe', e)

    @block.scalar
    def _(e):
        emit_engine('act', e)

    @block.vector
    def _(e):
        emit_engine('dve', e)

    @block.gpsimd
    def _(e):
        emit_engine('pool', e)

    @block.sync
    def _(e):
        emit_engine('sp', e)


BLK_NAMES = (['in_kv', 'in_a0', 'in_a1', 'in_pl', 'in_q', 'in_u', 'in_v']
             + [f'gate{i}_{h}' for i in range(4) for h in range(2)]
             + [f'br{i}_{h}' for i in range(4) for h in range(2)]
             + ['out_0', 'out_1']
             + [f'mi_{j}' for j in range(8)]
             + [f'mo_{cb}_{g}' for cb in range(2) for g in range(4)])
BLK_ID = {n: i for i, n in enumerate(BLK_NAMES)}
NBLK = len(BLK_NAMES)
BLK_COLS = {n: 4096 for n in BLK_NAMES}
BLK_COLS['in_kv'] = 2048
for _i in range(4):
    for _h in range(2):
        BLK_COLS[f'br{_i}_{_h}'] = 2048

V_N1G, V_N2G, V_BG, V_CW, V_CB, V_LG, V_LB, V_PS = 0, 8, 16, 48, 172, 176, 180, 184
NVEC = 188
B_QG, B_KG, B_SG, B_SB = 0, 64, 128, 640
NVBC = 1152


def _blockify(W, c0, ncols):
    K = W.shape[0]
    kc = K // 128
    b = W[:, c0:c0 + ncols].reshape(kc, 128, ncols).transpose(1, 0, 2).reshape(128, kc * ncols)
    out = np.zeros((128, 4096), np.float32)
    out[:, :kc * ncols] = b
    return out


def host_weights(inp):
    wts = np.zeros((NLAYER, NBLK, 128, 4096), np.float32)
    qperm = np.array([(j + 4 * hf) * 64 + d for j in range(4) for hf in range(2) for d in range(64)])
    for l in range(NLAYER):
        w_in = np.asarray(inp['w_in'][l])
        wq = w_in[:, 0:512][:, qperm]
        wts[l, BLK_ID['in_q']] = _blockify(wq, 0, 512)
        wts[l, BLK_ID['in_kv']] = _blockify(w_in, 512, 256)
        wts[l, BLK_ID['in_a0']] = _blockify(w_in, 768, 512)
        wts[l, BLK_ID['in_a1']] = _blockify(w_in, 1280, 512)
        wts[l, BLK_ID['in_pl']] = _blockify(w_in, 1792, 512)
        wts[l, BLK_ID['in_u']] = _blockify(w_in, 2304, 512)
        wts[l, BLK_ID['in_v']] = _blockify(w_in, 2816, 512)
        wg = np.asarray(inp['w_gate'][l])
        wb = np.asarray(inp['w_branch'][l])
        for i in range(4):
            for h in range(2):
                wts[l, BLK_ID[f'gate{i}_{h}']] = _blockify(wg, i * 1024 + h * 512, 512)
                wts[l, BLK_ID[f'br{i}_{h}']] = _blockify(wb[i], h * 512, 512)
        wo = np.asarray(inp['w_out'][l])
        for cb in range(2):
            wts[l, BLK_ID[f'out_{cb}']] = _blockify(wo, cb * 512, 512)
        wmi = np.asarray(inp['w_mlp_in'][l])
        for j in range(8):
            wts[l, BLK_ID[f'mi_{j}']] = _blockify(wmi, j * 512, 512)
        wmo = np.asarray(inp['w_mlp_out'][l])
        for cb in range(2):
            for g in range(4):
                wts[l, BLK_ID[f'mo_{cb}_{g}']] = _blockify(wmo[g * 1024:(g + 1) * 1024], cb * 512, 512)
    wmod = np.zeros((NLAYER, 12, 128, 4096), np.float32)
    for l in range(NLAYER):
        wm = np.asarray(inp['w_mod'][l])
        for j in range(12):
            wmod[l, j] = _blockify(wm, j * 512, 512)
    smallw = np.zeros((NLAYER, 128, 1024), np.float32)
    for l in range(NLAYER):
        smallw[l, :, 0:512] = np.asarray(inp['pool_w'][l]).transpose(1, 0, 2).reshape(128, 512)
        smallw[l, :, 512:1024] = np.asarray(inp['sgu_w'][l]).transpose(2, 0, 1).reshape(128, 512)
    vecT = np.zeros((128, NLAYER, NVEC), np.float32)
    vbc = np.zeros((128, NLAYER, NVBC), np.float32)
    bmodT = np.zeros((128, NLAYER, 48), np.float32)
    for l in range(NLAYER):
        vecT[:, l, V_N1G:V_N1G + 8] = np.asarray(inp['norm1_g'][l]).reshape(8, 128).T
        vecT[:, l, V_N2G:V_N2G + 8] = np.asarray(inp['norm2_g'][l]).reshape(8, 128).T
        vecT[:, l, V_BG:V_BG + 32] = np.asarray(inp['b_gate'][l]).reshape(32, 128).T
        vecT[:, l, V_CW:V_CW + 124] = np.asarray(inp['conv_w'][l]).reshape(31, 4, 128).transpose(2, 1, 0).reshape(128, 124)
        vecT[:, l, V_CB:V_CB + 4] = np.asarray(inp['conv_b'][l]).reshape(4, 128).T
        vecT[:, l, V_LG:V_LG + 4] = np.asarray(inp['conv_ln_g'][l]).reshape(4, 128).T
        vecT[:, l, V_LB:V_LB + 4] = np.asarray(inp['conv_ln_b'][l]).reshape(4, 128).T
        vecT[:, l, V_PS:V_PS + 4] = np.asarray(inp['pool_scale'][l]).reshape(4, 128).T
        vbc[:, l, B_QG:B_QG + 64] = np.asarray(inp['q_norm_g'][l])[None, :]
        vbc[:, l, B_KG:B_KG + 64] = np.asarray(inp['k_norm_g'][l])[None, :]
        vbc[:, l, B_SG:B_SG + 512] = np.asarray(inp['sgu_norm_g'][l])[None, :]
        vbc[:, l, B_SB:B_SB + 512] = np.asarray(inp['sgu_b'][l]).reshape(512)[None, :]
        bmodT[:, l, :] = np.asarray(inp['b_mod'][l]).reshape(48, 128).T
    fng = np.broadcast_to(np.asarray(inp['final_norm_g'])[None, :], (128, D)).astype(np.float32).copy()
    return dict(wts=wts.reshape(NLAYER * NBLK, 128, 4096), wmod=wmod.reshape(NLAYER * 12, 128, 4096),
                smallw=smallw, vecT=vecT, vbc=vbc, bmodT=bmodT, fng=fng)


def host_consts(NS):
    nt = NS // 128
    t = np.arange(NS)
    row = (t // 64).astype(np.float32)
    col = (t % 64).astype(np.float32)
    inv = (10000.0 ** (-np.arange(0, 32, 2, dtype=np.float32) / 32)).astype(np.float32)
    ang = np.concatenate([row[:, None] * inv, col[:, None] * inv], axis=-1).astype(np.float32)
    cos = np.cos(ang).astype(np.float32).reshape(nt, 128, 32).transpose(1, 0, 2).copy()
    sin = np.sin(ang).astype(np.float32).reshape(nt, 128, 32).transpose(1, 0, 2).copy()
    corr = np.ones((128, 4, 2, 8), np.float32)
    for g in range(4):
        w = 2 ** (g + 1)
        for i in range(8):
            if i < w // 2:
                corr[:, g, 0, i] = w / (i + w // 2)
            e = 7 - i
            if e < w // 2 - 1:
                corr[:, g, 1, i] = w / (e + 1 + w // 2)
    ident = np.eye(128, dtype=np.float32)
    return dict(ropec=cos, ropes=sin, pcorr=corr, ident=ident)


def build_program(NS):
    nc = bass.Bass("TRN2", target_bir_lowering=False)
    NTS = NS // 128
    NK = NS + PAST
    NKT = NK // 128
    NPT = 2 * SEQ_P

    def din(name, shape, dt=F32):
        return nc.dram_tensor(name, shape, dt, kind="ExternalInput").ap()

    def dout(name, shape):
        return nc.dram_tensor(name, shape, F32, kind="ExternalOutput").ap()

    xp_d = din("xp", [NPT, D])
    xs_d = din("xs", [NS, D])
    ck_d = din("ck", [NLAYER, PAST, 128])
    cv_d = din("cv", [NLAYER, PAST, 128])
    cvec_d = din("cvec", [128, 8, 2])
    wts_d = din("wts", [NLAYER * NBLK, 128, 4096])
    wmod_d = din("wmod", [NLAYER * 12, 128, 4096])
    smallw_d = din("smallw", [NLAYER, 128, 1024])
    vecT_d = din("vecT", [128, NLAYER, NVEC])
    vbc_d = din("vbc", [128, NLAYER, NVBC])
    bmodT_d = din("bmodT", [128, NLAYER, 48])
    fng_d = din("fng", [128, D])
    ropec_d = din("ropec", [128, NTS, 32])
    ropes_d = din("ropes", [128, NTS, 32])
    pcorr_d = din("pcorr", [128, 4, 2, 8])
    ident_d = din("ident", [128, 128])
    yp_d = dout("yp", [NPT, D])
    ys_d = dout("ys", [NS, D])
    nk_d = dout("nk", [2, NLAYER, SEQ_P, 128])
    nv_d = dout("nv", [2, NLAYER, SEQ_P, 128])
    x1p_d = nc.dram_tensor("x1p", [NPT, D], F32).ap()
    x1s_d = nc.dram_tensor("x1s", [NS, D], F32).ap()
    aT_d = nc.dram_tensor("aT_s", [4, 128, NS + 32], BF16).ap()
    plT_d = nc.dram_tensor("plT_s", [4, 128, NS + 16], BF16).ap()

    es = ExitStack()
    with es:
        def sb(name, shape, dt):
            return es.enter_context(nc.sbuf_tensor(name, shape, dt))

        wslot = [sb(f"wslot{i}", [128, 4096], BF16) for i in range(NSLOT)]
        x_sb = sb("x_sb", [128, 4, D], F32)
        hT = sb("hT", [128, KC, 512], BF16)
        KT = sb("KT", [128, NK], BF16)
        V1 = sb("V1", [128, NKT, 2, 128], BF16)
        QT = sb("QT", [128, 8, 512], BF16)
        merged = sb("merged", [128, 8, 512], F32)
        merged_bf = sb("merged_bf", [128, 8, 512], BF16)
        gbc = sb("gbc", [128, 2, D], F32)
        fng = sb("fng_sb", [128, D], F32)
        ropec = sb("ropec_sb", [128, NTS, 32], F32)
        ropes = sb("ropes_sb", [128, NTS, 32], F32)
        vbc = sb("vbc_sb", [128, NVBC], F32)
        vecT = sb("vecT_sb", [128, NLAYER, NVEC], F32)
        bmodT = sb("bmodT_sb", [128, NLAYER, 48], F32)
        modT = sb("modT", [128, NLAYER, 48, 2], F32)
        gmT = sb("gmT", [128, NLAYER, 2, 8, 2], F32)
        cvec = sb("cvec_sb", [128, 8, 2], F32)
        csilu = sb("csilu", [128, 8, 2], F32)
        pcorr = sb("pcorr_sb", [128, 4, 2, 8], F32)
        identf = sb("identf", [128, 128], F32)
        identb = sb("identb", [128, 128], BF16)
        onesf = sb("onesf", [128, 128], F32)
        epsT = sb("epsT", [128, 1], F32)
        zeroT = sb("zeroT", [128, 64], BF16)
        smallw = sb("smallw_sb", [128, 1024], BF16)
        stat = sb("stat", [128, 64], F32)
        FA = sb("FA", [128, 8, 512], F32)
        FB = sb("FB", [128, 6, 512], F32)
        G = sb("Gt", [128, 3, 4, 512], BF16)
        q_bf = sb("q_bf", [128, 512], BF16)
        pT = sb("pT", [128, 3, 512], BF16)
        awin = sb("awin", [128, 4, 576], BF16)
        plwin = sb("plwin", [128, 4, 544], BF16)
        kvout = awin[:, :, :].rearrange("p a b -> p (a b)").bitcast(F32)[:, 0:1024].rearrange("p (s t c) -> p s t c", s=4, t=2)
        pooled_bf = sb("pooled_bf", [128, 512], BF16)
        psum = [es.enter_context(nc.psum_tensor(f"ps{i}", [128, 512], F32)) for i in range(8)]

        sems = {e: es.enter_context(nc.semaphore(f"s_{e}")) for e in COMPUTE}
        chan_names = ([f'w{i}' for i in range(NSLOT)] + ['x0', 'x1', 'x2', 'x3', 'xo0', 'xo1', 'xo2', 'xo3', 'const', 'kvc', 'kvo', 'ast', 'pst', 'awin', 'plwin',
                                                          'wmod0', 'wmod1', 'small', 'pad', 'vbc', 'xf'])
        csems = {c: es.enter_context(nc.semaphore(f"c_{c}")) for c in chan_names}
        block = es.enter_context(nc.Block())

        P = Prog()
        B = {}
        cur = {'l': 0}

        def buf(name):
            if name not in B:
                B[name] = Buf(name)
            return B[name]

        bFAh = [buf(f'FAh{i}') for i in range(16)]
        bFA = [(bFAh[2 * i], bFAh[2 * i + 1]) for i in range(8)]
        bFB = [buf(f'FB{i}') for i in range(6)]
        bG = [buf('G0'), buf('G1'), buf('G2')]
        bPS = [buf(f'PS{i}') for i in range(8)]
        bX = [buf(f'X{i}') for i in range(4)]
        bHT = [buf(f'HT{i}') for i in range(8)]
        bW = [buf(f'W{i}') for i in range(NSLOT)]
        bPT = [buf(f'pT{i}') for i in range(3)]
        bST = [buf(f'st{i}') for i in range(64)]

        def st(c0, n=1):
            return bST[c0:c0 + n]

        def mm(out, lhsT, rhs, start, stop, reads, writes):
            P.op('pe', lambda e: e.matmul(out, lhsT=lhsT, rhs=rhs, start=start, stop=stop), reads, writes)

        def tr(out, in_, ident, reads, writes):
            P.op('pe', lambda e: e.transpose(out, in_, ident), reads, writes)

        def act(out, in_, func, reads, writes, bias=None, scale=None):
            kw = {}
            if bias is not None:
                kw['bias'] = bias
            if scale is not None:
                kw['scale'] = scale
            P.op('act', lambda e: e.activation(out=out, in_=in_, func=func, **kw), reads, writes)

        def tt(out, in0, in1, op, reads, writes, eng='dve'):
            P.op(eng, lambda e: e.tensor_tensor(out=out, in0=in0, in1=in1, op=op), reads, writes)

        def ts(out, in0, s1, s2, op0, op1, reads, writes, eng='dve'):
            if op1 is None:
                P.op(eng, lambda e: e.tensor_scalar(out=out, in0=in0, scalar1=s1, scalar2=None, op0=op0), reads, writes)
            else:
                P.op(eng, lambda e: e.tensor_scalar(out=out, in0=in0, scalar1=s1, scalar2=s2, op0=op0, op1=op1), reads, writes)

        def stt(out, in0, scalar, in1, op0, op1, reads, writes):
            P.op('dve', lambda e: e.scalar_tensor_tensor(out=out, in0=in0, scalar=scalar, in1=in1, op0=op0, op1=op1), reads, writes)

        def cp(out, in_, reads, writes, eng='dve'):
            if eng == 'act_copy':
                P.op('act', lambda e: e.activation(out=out, in_=in_, func=AF.Copy), reads, writes)
            else:
                P.op(eng, lambda e: e.tensor_copy(out=out, in_=in_), reads, writes)

        def sumsq(junk, in_, acc, reads, writes):
            P.op('act', lambda e: e.activation(out=junk, in_=in_, func=AF.Square, accum_out=acc), reads, writes)

        def recip(out, in_, reads, writes):
            P.op('dve', lambda e: e.reciprocal(out=out, in_=in_), reads, writes)

        def dma(q, out, in_, reads, writes, chan):
            P.op(q, lambda e: e.dma_start(out=out, in_=in_), reads, writes, dma=True, chan=chan)

        class PSPool:
            def __init__(self):
                self.free = list(range(8))

            def get(self):
                i = self.free.pop(0)
                return i

            def put(self, i):
                self.free.append(i)
        PSP = PSPool()

        class WStream:
            def __init__(self):
                self.sched = []
                self.pos = 0
                self.issued = 0

            def _issue(self, k):
                l, name = self.sched[k]
                s = k % NSLOT
                ncol = BLK_COLS[name]
                dma('pool', wslot[s][:, 0:ncol], wts_d[l * NBLK + BLK_ID[name], :, 0:ncol], [], [bW[s]], f'w{s}')

            def need(self, l, name):
                if P.dry:
                    self.sched.append((l, name))
                    return 0
                k = self.pos
                assert self.sched[k] == (l, name), (self.sched[k], l, name)
                lim = min(len(self.sched), k + NSLOT - 1)
                while self.issued < lim:
                    self._issue(self.issued)
                    self.issued += 1
                self.pos += 1
                return k % NSLOT
        WS = WStream()

        def wview(s, kc, ncols):
            return wslot[s][:, 0:kc * ncols].rearrange("p (k c) -> p k c", k=kc)

        def setup():
            dma('sp', identf[:, :], ident_d[:, :], [], [buf('identf')], 'const')
            dma('sp', cvec[:, :, :], cvec_d[:, :, :], [], [buf('cvec')], 'const')
            dma('sp', bmodT[:, :, :], bmodT_d[:, :, :], [], [buf('bmodT')], 'const')
            dma('sp', vecT[:, :, :], vecT_d[:, :, :], [], [buf('vecT')], 'const')
            dma('sp', ropec[:, :, :], ropec_d[:, :, :], [], [buf('ropec')], 'const')
            dma('sp', ropes[:, :, :], ropes_d[:, :, :], [], [buf('ropes')], 'const')
            dma('sp', pcorr[:, :, :, :], pcorr_d[:, :, :, :], [], [buf('pcorr')], 'const')
            dma('sp', fng[:, :], fng_d[:, :], [], [buf('fng')], 'const')
            cp(identb[:, :], identf[:, :], [buf('identf')], [buf('identb')])
            P.op('dve', lambda e: e.memset(onesf[:, :], 1.0), [], [buf('onesf')])
            P.op('dve', lambda e: e.memset(epsT[:, :], EPS), [], [buf('epsT')])
            P.op('dve', lambda e: e.memset(zeroT[:, :], 0.0), [], [buf('zeroT')])
            P.op('dve', lambda e: e.memset(V1[:, :, :, 64:128], 1.0), [], [buf('V1')])
            P.op('dve', lambda e: e.memset(QT[:, :, :], 0.0), [], [buf('QT')])
            act(csilu[:, :, :], cvec[:, :, :], AF.Silu, [buf('cvec')], [buf('csilu')])
            stages = [(FA[:, :, :].rearrange("p a b -> p (a b)"), bFA), (x_sb[:, :, :].rearrange("p a b -> p (a b)"), bX)]
            nblk = 0
            for l in range(NLAYER):
                pb = PSP.get()
                for j12 in range(12):
                    wst, wbufs = stages[nblk % 2]
                    dma('sp', wst, wmod_d[l * 12 + j12, :, :], [], wbufs, f'wmod{nblk % 2}')
                    nblk += 1
                    wv = wst.rearrange("p (k c) -> p k c", k=8)
                    for jj in range(4):
                        j = j12 * 4 + jj
                        for kc in range(KC):
                            mm(psum[pb][:, 2 * j:2 * j + 2], wv[:, kc, jj * 128:(jj + 1) * 128], csilu[:, kc, :],
                               kc == 0, kc == KC - 1, wbufs + [buf('csilu')], [bPS[pb]])
                tt(modT[:, l, :, :], psum[pb][:, 0:96].rearrange("p (j t) -> p j t", t=2),
                   bmodT[:, l, :].unsqueeze(2).to_broadcast([128, 48, 2]), ALU.add,
                   [bPS[pb], buf('bmodT')], [buf('modT')])
                PSP.put(pb)
                for n, (sc0, g0) in enumerate([(8, V_N1G), (32, V_N2G)]):
                    stt(gmT[:, l, n, :, :], modT[:, l, sc0:sc0 + 8, :], 1.0,
                        vecT[:, l, g0:g0 + 8].unsqueeze(2).to_broadcast([128, 8, 2]), ALU.add, ALU.mult,
                        [buf('modT'), buf('vecT')], [buf('gmT')])

        def layer_setup(l):
            dma('sp', vbc[:, :], vbc_d[:, l, :], [], [buf('vbc')], 'vbc')
            dma('pool', smallw[:, :], smallw_d[l, :, :], [], [buf('smallw')], 'small')

        def type_setup(l, t):
            for n, j0 in enumerate([16, 40]):
                for kc in range(8):
                    d = n * 8 + kc
                    ts(FB[:, d // 4, (d % 4) * 128:(d % 4 + 1) * 128], identf[:, :], modT[:, l, j0 + kc, t:t + 1], None, ALU.mult, None,
                       [buf('identf'), buf('modT')], [bFB[d // 4]])
            for n in range(2):
                for half in range(2):
                    pb = PSP.get()
                    for q4 in range(4):
                        d = n * 8 + half * 4 + q4
                        mm(psum[pb][:, q4 * 128:(q4 + 1) * 128], onesf[:, :], FB[:, d // 4, (d % 4) * 128:(d % 4 + 1) * 128], True, True,
                           [buf('onesf'), bFB[d // 4]], [bPS[pb]])
                    cp(gbc[:, n, half * 512:(half + 1) * 512], psum[pb][:, :], [bPS[pb]], [buf('gbc')])
                    PSP.put(pb)

        def load_x(src_d, tok0, nsub, kind):
            rd = [buf('x1' + kind)] if cur['l'] > 0 else []
            for s in range(nsub):
                dma('sp', x_sb[:, s, :], src_d[tok0 + s * 128:tok0 + (s + 1) * 128, :], rd, [bX[s]], f'x{s}')

        def rstd_from_ssq(nsub, scale):
            act(stat[:, 4:4 + nsub], stat[:, 0:nsub], AF.Sqrt, st(0, nsub), st(4, nsub), bias=epsT[:, 0:1], scale=scale)
            recip(stat[:, 8:8 + nsub], stat[:, 4:4 + nsub], st(4, nsub), st(8, nsub))

        xf = {'have': False}
        FBj = FB[:, 4:6, :].rearrange("p a b -> p (a b)")

        xn = {'ready': False}

        def norm_elem_from_x(nsub):
            for s in range(nsub):
                sumsq(FA[:, 2 * s:2 * s + 2, :].rearrange("p a b -> p (a b)"), x_sb[:, s, :], stat[:, s:s + 1],
                      [bX[s]], [bFA[2 * s], bFA[2 * s + 1]] + st(s))
            rstd_from_ssq(nsub, 1.0 / D)
            for s in range(nsub):
                act(FA[:, 2 * s:2 * s + 2, :].rearrange("p a b -> p (a b)"), x_sb[:, s, :], AF.Copy,
                    [bX[s]] + st(8 + s), [bFA[2 * s], bFA[2 * s + 1]], scale=stat[:, 8 + s:9 + s])

        def norm_elem_from_fa(nsub):
            for s in range(nsub):
                fa = FA[:, 2 * s:2 * s + 2, :].rearrange("p a b -> p (a b)")
                sumsq(FBj, fa, stat[:, s:s + 1], [bFA[2 * s], bFA[2 * s + 1]], [bFB[4], bFB[5]] + st(s))
            rstd_from_ssq(nsub, 1.0 / D)
            for s in range(nsub):
                fa = FA[:, 2 * s:2 * s + 2, :].rearrange("p a b -> p (a b)")
                act(fa, fa, AF.Copy, [bFA[2 * s], bFA[2 * s + 1]] + st(8 + s), [bFA[2 * s], bFA[2 * s + 1]], scale=stat[:, 8 + s:9 + s])

        def norm_to_hT(l, t, n, nsub, skip_elem=False):
            Tm = nsub * 128
            if skip_elem:
                pass
            elif n == 0 and xn['ready']:
                xn['ready'] = False
            elif n == 0 and xf['have']:
                xf['have'] = False
                norm_elem_from_fa(nsub)
            else:
                norm_elem_from_x(nsub)
            for kc in range(KC):
                pb = PSP.get()
                for s in range(nsub):
                    xin = FA[:, 2 * s + kc // 4, (kc % 4) * 128:(kc % 4 + 1) * 128]
                    tr(psum[pb][:, s * 128:(s + 1) * 128], xin, identf[:, :], [bFA[2 * s + kc // 4], buf('identf')], [bPS[pb]])
                sh = modT[:, l, (0 if n == 0 else 24) + kc, t:t + 1]
                gm = gmT[:, l, n, kc, t:t + 1]
                if kc % 2 == 0:
                    act(hT[:, kc, 0:Tm], psum[pb][:, 0:Tm], AF.Identity, [bPS[pb], buf('gmT'), buf('modT')], [bHT[kc]],
                        bias=sh, scale=gm)
                else:
                    ts(hT[:, kc, 0:Tm], psum[pb][:, 0:Tm], gm, sh, ALU.mult, ALU.add,
                       [bPS[pb], buf('gmT'), buf('modT')], [bHT[kc]])
                PSP.put(pb)

        def headnorm_rope(src_ps, pbuf, nh, g_off, gscale, rope_tile, out_ap, out_bufs, tmpi):
            W = nh * 64
            f_sq, f_n, f_r0, f_r1 = tmpi
            act(FB[:, f_sq, 0:W], src_ps, AF.Square, [pbuf], [bFB[f_sq]])
            P.op('dve', lambda e: e.tensor_reduce(out=stat[:, 16:16 + nh], in_=FB[:, f_sq, 0:W].rearrange("p (h d) -> p h d", h=nh),
                                                  axis=AX.X, op=ALU.add), [bFB[f_sq]], st(16, nh))
            act(stat[:, 24:24 + nh], stat[:, 16:16 + nh], AF.Sqrt, st(16, nh), st(24, nh), bias=epsT[:, 0:1], scale=1.0 / 64)
            recip(stat[:, 32:32 + nh], stat[:, 24:24 + nh], st(24, nh), st(32, nh))
            qn = FB[:, f_n, 0:W].rearrange("p (h d) -> p h d", h=nh)
            tt(qn, src_ps.rearrange("p (h d) -> p h d", h=nh), stat[:, 32:32 + nh].unsqueeze(2).to_broadcast([128, nh, 64]),
               ALU.mult, [pbuf] + st(32, nh), [bFB[f_n]])
            gb = vbc[:, g_off:g_off + 64].unsqueeze(1).to_broadcast([128, nh, 64])
            if rope_tile is None:
                stt(out_ap.rearrange("p (h d) -> p h d", h=nh), qn, gscale, gb, ALU.mult, ALU.mult,
                    [bFB[f_n], buf('vbc')], out_bufs)
                return
            stt(qn, qn, gscale, gb, ALU.mult, ALU.mult, [bFB[f_n], buf('vbc')], [bFB[f_n]])
            q4 = FB[:, f_n, 0:W].rearrange("p (h j t) -> p h j t", h=nh, t=2)
            x0 = q4[:, :, :, 0]
            x1 = q4[:, :, :, 1]
            cb_ = ropec[:, rope_tile, :].unsqueeze(1).to_broadcast([128, nh, 32])
            sb_ = ropes[:, rope_tile, :].unsqueeze(1).to_broadcast([128, nh, 32])
            H = nh * 32
            t1 = FB[:, f_r0, 0:H].rearrange("p (h j) -> p h j", h=nh)
            t2 = FB[:, f_r0, 256:256 + H].rearrange("p (h j) -> p h j", h=nh)
            t3 = FB[:, f_r1, 0:H].rearrange("p (h j) -> p h j", h=nh)
            t4 = FB[:, f_r1, 256:256 + H].rearrange("p (h j) -> p h j", h=nh)
            o4 = out_ap.rearrange("p (h j t) -> p h j t", h=nh, t=2)
            rd = [bFB[f_n], buf('ropec'), buf('ropes')]
            tt(t1, x0, cb_, ALU.mult, rd, [bFB[f_r0]])
            tt(t2, x1, sb_, ALU.mult, rd, [bFB[f_r0]])
            tt(t3, x0, sb_, ALU.mult, rd, [bFB[f_r1]])
            tt(t4, x1, cb_, ALU.mult, rd, [bFB[f_r1]])
            tt(o4[:, :, :, 0], t1, t2, ALU.subtract, [bFB[f_r0]], out_bufs)
            tt(o4[:, :, :, 1], t3, t4, ALU.add, [bFB[f_r1]], out_bufs)

        def seq_setup(l, kind, N):
            aTv = aT_d.rearrange("j p n -> p j n")
            plv = plT_d.rearrange("j p n -> p j n")
            z16 = zeroT[:, 0:64].rearrange("p (j n) -> p j n", j=4)
            z8 = zeroT[:, 0:32].rearrange("p (j n) -> p j n", j=4)
            nseg = 2 if kind == 'P' else 1
            L = SEQ_P if kind == 'P' else N
            for g in range(nseg):
                a0 = g * (L + 32)
                p0 = g * (L + 16)
                dma('sp', aTv[:, :, a0:a0 + 16], z16, [buf('zeroT')], [buf('aT_d')], 'pad')
                dma('sp', aTv[:, :, a0 + 16 + L:a0 + 32 + L], z16, [buf('zeroT')], [buf('aT_d')], 'pad')
                dma('sp', plv[:, :, p0:p0 + 8], z8, [buf('zeroT')], [buf('plT_d')], 'pad')
                dma('sp', plv[:, :, p0 + 8 + L:p0 + 16 + L], z8, [buf('zeroT')], [buf('plT_d')], 'pad')
            if kind == 'S':
                dma('sp', FA[:, 0, :].rearrange("p (k c) -> p k c", k=4), ck_d[l].rearrange("(k p) c -> p k c", p=128),
                    [], [bFA[0]], 'kvc')
                dma('sp', FA[:, 1, :].rearrange("p (k c) -> p k c", k=4), cv_d[l].rearrange("(k p) c -> p k c", p=128),
                    [], [bFA[1]], 'kvc')
                cp(q_bf[:, :], FA[:, 0, :], [bFA[0]], [buf('q_bf')])
                pb = PSP.get()
                pbv = psum[pb][:, :].bitcast(BF16)
                for k in range(4):
                    tr(pbv[:, k * 128:(k + 1) * 128], q_bf[:, k * 128:(k + 1) * 128], identb[:, :], [buf('q_bf'), buf('identb')], [bPS[pb]])
                cp(KT[:, 0:512], pbv[:, 0:512], [bPS[pb]], [buf('KT')])
                PSP.put(pb)
                cp(V1[:, 0:4, :, 0:64], FA[:, 1, :].rearrange("p (k v d) -> p k v d", k=4, v=2), [bFA[1]], [buf('V1')])

        xpre = {'have': False}

        def load_x_once(src_d, tok0, nsub, kind):
            if xpre['have']:
                xpre['have'] = False
            else:
                load_x(src_d, tok0, nsub, kind)

        def phaseA(l, kind, t, src_d, tok0, nsub, key0, segs, nxt):
            Tm = nsub * 128
            load_x_once(src_d, tok0, nsub, kind)
            norm_to_hT(l, t, 0, nsub)
            if nxt is not None:
                load_x(src_d, nxt[0], nxt[1], kind)
                xpre['have'] = True
            s_kv = WS.need(l, 'in_kv')
            wkv = wview(s_kv, 8, 256)
            kvb = []
            for s in range(nsub):
                pb = PSP.get()
                kvb.append(pb)
                for kc in range(KC):
                    mm(psum[pb][:, 0:256], hT[:, kc, s * 128:(s + 1) * 128], wkv[:, kc, :], kc == 0, kc == KC - 1,
                       bHT + [bW[s_kv]], [bPS[pb]])
            kt0 = (key0 + tok0) // 128
            for s in range(nsub):
                pb = kvb[s]
                gt = (tok0 // 128 + s) if kind == 'S' else None
                headnorm_rope(psum[pb][:, 0:128], bPS[pb], 2, B_KG, 1.0, gt, kvout[:, s, 0, :], [buf('awin')], (0, 1, 2, 3))
                cp(kvout[:, s, 1, :], psum[pb][:, 128:256], [bPS[pb]], [buf('awin')], eng='act_copy')
                PSP.put(pb)
                cp(V1[:, kt0 + s, :, 0:64], kvout[:, s, 1, :].rearrange("p (v d) -> p v d", v=2), [buf('awin')], [buf('V1')])
                cp(q_bf[:, s * 128:(s + 1) * 128], kvout[:, s, 0, :], [buf('awin')], [buf('q_bf')])
            if kind == 'P':
                for g in range(2):
                    dma('sp', nk_d[g, l, :, :].rearrange("(s p) c -> p s c", p=128), kvout[:, 2 * g:2 * g + 2, 0, :],
                        [buf('awin')], [], 'kvo')
                    dma('sp', nv_d[g, l, :, :].rearrange("(s p) c -> p s c", p=128), kvout[:, 2 * g:2 * g + 2, 1, :],
                        [buf('awin')], [], 'kvo')
            s_a0 = WS.need(l, 'in_a0')
            wa0 = wview(s_a0, 8, 512)
            s_a1 = WS.need(l, 'in_a1')
            wa1 = wview(s_a1, 8, 512)
            for j in range(4):
                pa = PSP.get()
                for kc in range(KC):
                    mm(psum[pa][:, 0:Tm], wa0[:, kc, j * 128:(j + 1) * 128], hT[:, kc, 0:Tm], kc == 0, kc == KC - 1,
                       bHT + [bW[s_a0]], [bPS[pa]])
                pb = PSP.get()
                for kc in range(KC):
                    mm(psum[pb][:, 0:Tm], wa1[:, kc, j * 128:(j + 1) * 128], hT[:, kc, 0:Tm], kc == 0, kc == KC - 1,
                       bHT + [bW[s_a1]], [bPS[pb]])
                fb = 4 + (j % 2)
                act(FB[:, fb, 0:Tm], psum[pb][:, 0:Tm], AF.Sigmoid, [bPS[pb]], [bFB[fb]])
                PSP.put(pb)
                tt(G[:, 0, j, 0:Tm], psum[pa][:, 0:Tm], FB[:, fb, 0:Tm], ALU.mult, [bPS[pa], bFB[fb]], [bG[0]])
                PSP.put(pa)
            for sg in segs:
                dma('sp', aT_d.rearrange("j p n -> p j n")[:, :, sg['apos']:sg['apos'] + sg['n']], G[:, 0, :, sg['col0']:sg['col0'] + sg['n']],
                    [bG[0]], [buf('aT_d')], 'ast')
            if nxt is not None:
                norm_elem_from_x(nxt[1])
                xn['ready'] = True
            s_pl = WS.need(l, 'in_pl')
            wpl = wview(s_pl, 8, 512)
            for j in range(4):
                pb = PSP.get()
                for kc in range(KC):
                    mm(psum[pb][:, 0:Tm], wpl[:, kc, j * 128:(j + 1) * 128], hT[:, kc, 0:Tm], kc == 0, kc == KC - 1,
                       bHT + [bW[s_pl]], [bPS[pb]])
                if j % 2 == 0:
                    cp(G[:, 1, j, 0:Tm], psum[pb][:, 0:Tm], [bPS[pb]], [bG[1]])
                else:
                    cp(G[:, 1, j, 0:Tm], psum[pb][:, 0:Tm], [bPS[pb]], [bG[1]], eng='act_copy')
                PSP.put(pb)
            for sg in segs:
                dma('sp', plT_d.rearrange("j p n -> p j n")[:, :, sg['ppos']:sg['ppos'] + sg['n']], G[:, 1, :, sg['col0']:sg['col0'] + sg['n']],
                    [bG[1]], [buf('plT_d')], 'pst')
            pb2 = PSP.get()
            pbv = psum[pb2][:, :].bitcast(BF16)
            for s in range(nsub):
                tr(pbv[:, s * 128:(s + 1) * 128], q_bf[:, s * 128:(s + 1) * 128], identb[:, :], [buf('q_bf'), buf('identb')], [bPS[pb2]])
            cp(KT[:, kt0 * 128:(kt0 + nsub) * 128], pbv[:, 0:nsub * 128], [bPS[pb2]], [buf('KT')])
            PSP.put(pb2)

        def gate_and_project(l, i, gi, Tm, pos, pump=None):
            for half in range(2):
                s_b = WS.need(l, f'br{i}_{half}')
                wb = wview(s_b, 4, 512)
                s_g = WS.need(l, f'gate{i}_{half}')
                wg = wview(s_g, 8, 512)
                for dq in range(4):
                    dc = half * 4 + dq
                    if pump is not None:
                        pump()
                    pg = PSP.get()
                    for kc in range(KC):
                        mm(psum[pg][:, 0:Tm], wg[:, kc, dq * 128:(dq + 1) * 128], hT[:, kc, 0:Tm], kc == 0, kc == KC - 1,
                           bHT + [bW[s_g]], [bPS[pg]])
                    fg = dc % 2
                    act(FB[:, fg, 0:Tm], psum[pg][:, 0:Tm], AF.Sigmoid, [bPS[pg], buf('vecT')], [bFB[fg]],
                        bias=vecT[:, l, V_BG + i * 8 + dc:V_BG + i * 8 + dc + 1])
                    PSP.put(pg)
                    pb = PSP.get()
                    for kc in range(4):
                        mm(psum[pb][:, 0:Tm], wb[:, kc, dq * 128:(dq + 1) * 128], G[:, gi, kc, 0:Tm], kc == 0, kc == 3,
                           [bG[gi], bW[s_b]], [bPS[pb]])
                    bm = buf(f'merged{dc}')
                    if pos == 'first':
                        tt(merged[:, dc, 0:Tm], FB[:, fg, 0:Tm], psum[pb][:, 0:Tm], ALU.mult, [bFB[fg], bPS[pb]], [bm])
                    else:
                        ft = 2 + dc % 2
                        tt(FB[:, ft, 0:Tm], FB[:, fg, 0:Tm], psum[pb][:, 0:Tm], ALU.mult, [bFB[fg], bPS[pb]], [bFB[ft]])
                        if pos != 'last':
                            tt(merged[:, dc, 0:Tm], merged[:, dc, 0:Tm], FB[:, ft, 0:Tm], ALU.add, [bm, bFB[ft]], [bm])
                        else:
                            tt(merged_bf[:, dc, 0:Tm], merged[:, dc, 0:Tm], FB[:, ft, 0:Tm], ALU.add, [bm, bFB[ft]],
                               [buf(f'mbf{dc}')])
                    PSP.put(pb)

        def phaseB(l, kind, t, src_d, dst_d, tok0, nsub, segs, final, nxt):
            Tm = nsub * 128
            nsg = len(segs)
            wa0_ = segs[0]['apos'] - 16
            wp0_ = segs[0]['ppos'] - 8
            load_x_once(src_d, tok0, nsub, kind)
            dma('sp', awin[:, :, 0:Tm + 32 * nsg], aT_d.rearrange("j p n -> p j n")[:, :, wa0_:wa0_ + Tm + 32 * nsg], [buf('aT_d')], [buf('awin')], 'awin')
            dma('sp', plwin[:, :, 0:Tm + 16 * nsg], plT_d.rearrange("j p n -> p j n")[:, :, wp0_:wp0_ + Tm + 16 * nsg], [buf('plT_d')], [buf('plwin')], 'plwin')
            norm_to_hT(l, t, 0, nsub)
            s_q = WS.need(l, 'in_q')
            wq = wview(s_q, 8, 512)
            qb = []
            for s in range(nsub):
                pb = PSP.get()
                qb.append(pb)
                for kc in range(KC):
                    mm(psum[pb][:, :], hT[:, kc, s * 128:(s + 1) * 128], wq[:, kc, :], kc == 0, kc == KC - 1,
                       bHT + [bW[s_q]], [bPS[pb]])
            for s in range(nsub):
                gt = (tok0 // 128 + s) if kind == 'S' else None
                headnorm_rope(psum[qb[s]][:, :], bPS[qb[s]], 8, B_QG, 0.125, gt, G[:, 2, s, :], [bG[2]], (2, 3, 4, 5))
                PSP.put(qb[s])
            s_u = WS.need(l, 'in_u')
            wu = wview(s_u, 8, 512)
            for j in range(4):
                pb = PSP.get()
                for kc in range(KC):
                    mm(psum[pb][:, 0:Tm], wu[:, kc, j * 128:(j + 1) * 128], hT[:, kc, 0:Tm], kc == 0, kc == KC - 1,
                       bHT + [bW[s_u]], [bPS[pb]])
                act(FA[:, j, 0:Tm], psum[pb][:, 0:Tm], AF.Gelu_apprx_tanh, [bPS[pb]], [bFA[j]])
                PSP.put(pb)
            s_v = WS.need(l, 'in_v')
            wv = wview(s_v, 8, 512)
            vn = FA[:, 6:8, :].rearrange("p a b -> p (a b)").bitcast(BF16).rearrange("p (s c) -> p s c", s=4)
            vb = []
            for s in range(nsub):
                pb = PSP.get()
                vb.append(pb)
                for kc in range(KC):
                    mm(psum[pb][:, :], hT[:, kc, s * 128:(s + 1) * 128], wv[:, kc, :], kc == 0, kc == KC - 1,
                       bHT + [bW[s_v]], [bPS[pb]])
            gel = [(FA[:, 4, :], bFA[4]), (FA[:, 5, :], bFA[5]), (FB[:, 4, :], bFB[4]), (FB[:, 5, :], bFB[5])]
            for s in range(nsub):
                act(gel[s][0], psum[vb[s]][:, :], AF.Gelu_apprx_tanh, [bPS[vb[s]]], [gel[s][1]])
                PSP.put(vb[s])
            for s in range(nsub):
                pb2 = PSP.get()
                pbv = psum[pb2][:, :].bitcast(BF16)
                for j in range(4):
                    tr(pbv[:, j * 128:(j + 1) * 128], G[:, 2, s, j * 128:(j + 1) * 128], identb[:, :], [bG[2], buf('identb')], [bPS[pb2]])
                cp(QT[0:64, 0:4, s * 128:(s + 1) * 128], pbv[0:64, 0:512].rearrange("p (j c) -> p j c", j=4), [bPS[pb2]], [buf('QT')])
                cp(QT[64:128, 4:8, s * 128:(s + 1) * 128], pbv[64:128, 0:512].rearrange("p (j c) -> p j c", j=4), [bPS[pb2]], [buf('QT')],
                   eng='act_copy')
                PSP.put(pb2)
            v_ops = []

            def v_stage_a():
                for s in range(nsub):
                    ft = s % 2
                    tt(FB[:, ft, :], gel[s][0], gel[s][0], ALU.mult, [gel[s][1]], [bFB[ft]])
                    P.op('dve', lambda e, s=s, ft=ft: e.tensor_reduce(out=stat[:, 40 + s:41 + s], in_=FB[:, ft, :], axis=AX.X, op=ALU.add),
                         [bFB[ft]], st(40 + s))

            def v_stage_b():
                act(stat[:, 44:44 + nsub], stat[:, 40:40 + nsub], AF.Ln, st(40, nsub), st(44, nsub), bias=epsT[:, 0:1], scale=1.0 / 512)
                act(stat[:, 48:48 + nsub], stat[:, 44:44 + nsub], AF.Exp, st(44, nsub), st(48, nsub), scale=-0.5)
                for s in range(nsub):
                    stt(vn[:, s, :], gel[s][0], stat[:, 48 + s:49 + s], vbc[:, B_SG:B_SG + 512], ALU.mult, ALU.mult,
                        [gel[s][1], buf('vbc')] + st(48 + s), [bFA[6], bFA[7]])
            v_ops.append(v_stage_a)
            v_ops.append(v_stage_b)
            def sgu_spatial():
                swT = smallw[:, 512:1024].rearrange("p (g i) -> p g i", g=4)
                sgb = vbc[:, B_SB:B_SB + 512].rearrange("p (g i) -> p g i", g=4)
                for g in range(4):
                    pb = PSP.get()
                    for s in range(nsub):
                        mm(psum[pb][:, s * 128:(s + 1) * 128], vn[:, s, g * 128:(g + 1) * 128], swT[:, g, :], True, True,
                           [bFA[6], bFA[7], buf('smallw')], [bPS[pb]])
                    ft = g % 2
                    tt(FB[:, ft, 0:Tm].rearrange("p (s i) -> p s i", s=nsub), psum[pb][:, 0:Tm].rearrange("p (s i) -> p s i", s=nsub),
                       sgb[:, g, :].unsqueeze(1).to_broadcast([128, nsub, 128]), ALU.add, [bPS[pb], buf('vbc')], [bFB[ft]])
                    PSP.put(pb)
                    tt(G[:, 2, g, 0:Tm], FB[:, ft, 0:Tm], FA[:, g, 0:Tm], ALU.mult, [bFB[ft], bFA[g]], [bG[2]])
            cw = vecT[:, l, V_CW:V_CW + 124].rearrange("p (c j) -> p c j", c=4)
            conv_ops = []
            if nsg == 2:
                n2 = segs[0]['n']
                for j in range(31):
                    for c in range(4):
                        outv = FA[:, c, 0:2 * n2].rearrange("p (g n) -> p g n", g=2)
                        inv = awin[:, c, 0:2 * (n2 + 32)].rearrange("p (g n) -> p g n", g=2)[:, :, j + 1:j + 1 + n2]
                        if j == 0:
                            conv_ops.append(lambda c=c, outv=outv, inv=inv: ts(
                                outv, inv, cw[:, c, 0:1], vecT[:, l, V_CB + c:V_CB + c + 1],
                                ALU.mult, ALU.add, [buf('awin'), buf('vecT')], [bFA[c]]))
                        else:
                            conv_ops.append(lambda c=c, j=j, outv=outv, inv=inv: stt(
                                outv, inv, cw[:, c, j:j + 1], outv,
                                ALU.mult, ALU.add, [buf('awin'), buf('vecT'), bFA[c]], [bFA[c]]))
            else:
                for j in range(31):
                    for c in range(4):
                        for gi_, sg in enumerate(segs):
                            c0, n_, wo_ = sg['col0'], sg['n'], gi_ * (sg['n'] + 32)
                            if j == 0:
                                conv_ops.append(lambda c=c, c0=c0, n_=n_, wo_=wo_: ts(
                                    FA[:, c, c0:c0 + n_], awin[:, c, wo_ + 1:wo_ + 1 + n_], cw[:, c, 0:1], vecT[:, l, V_CB + c:V_CB + c + 1],
                                    ALU.mult, ALU.add, [buf('awin'), buf('vecT')], [bFA[c]]))
                            else:
                                conv_ops.append(lambda c=c, j=j, c0=c0, n_=n_, wo_=wo_: stt(
                                    FA[:, c, c0:c0 + n_], awin[:, c, wo_ + j + 1:wo_ + j + 1 + n_], cw[:, c, j:j + 1], FA[:, c, c0:c0 + n_],
                                    ALU.mult, ALU.add, [buf('awin'), buf('vecT'), bFA[c]], [bFA[c]]))

            def pump(ops, n):
                for _ in range(min(n, len(ops))):
                    ops.pop(0)()
            sqt = [0, 1, 4, 5]
            presq = {'done': False}
            for h in range(8):
                kcA, hh = h // 2, h % 2
                hf = h // 4
                if h == 1:
                    while v_ops:
                        v_ops.pop(0)()
                if h == 2:
                    sgu_spatial()
                if h >= 2:
                    pump(conv_ops, 21 if nsg == 1 else 10)
                if h == 7 and nsg == 1:
                    pump(conv_ops, 1000)
                    for c in range(4):
                        tt(FB[:, sqt[c], 0:Tm], FA[:, c, 0:Tm], FA[:, c, 0:Tm], ALU.mult, [bFA[c]], [bFB[sqt[c]]])
                    presq['done'] = True
                po = PSP.get()
                pss = []
                work = [(sg, kt) for sg in segs for kt in range(sg['kt0'], sg['kt0'] + sg['nkt'])]

                def s_mm(i):
                    sg, kt = work[i]
                    pb = PSP.get()
                    mm(psum[pb][:, 0:sg['n']], KT[:, kt * 128:(kt + 1) * 128], QT[:, h, sg['col0']:sg['col0'] + sg['n']],
                       True, True, [buf('KT'), buf('QT')], [bPS[pb]])
                    pss.append(pb)
                s_mm(0)
                if len(work) > 1:
                    s_mm(1)
                for i, (sg, kt) in enumerate(work):
                    pb = pss[i]
                    pi = i % 3
                    c0, n_ = sg['col0'], sg['n']
                    act(pT[:, pi, 0:n_], psum[pb][:, 0:n_], AF.Exp, [bPS[pb]], [bPT[pi]])
                    PSP.put(pb)
                    if h == 0 and v_ops and i % 6 == 5:
                        v_ops.pop(0)()
                    if i + 2 < len(work):
                        s_mm(i + 2)
                    mm(psum[po][:, c0:c0 + n_], V1[:, kt, hf, :], pT[:, pi, 0:n_], kt == sg['kt0'], kt == sg['kt0'] + sg['nkt'] - 1,
                       [buf('V1'), bPT[pi]], [bPS[po]])
                fr = 2 + h % 2
                recip(FB[0:64, fr, 0:Tm], psum[po][64:128, 0:Tm], [bPS[po]], [bFB[fr]])
                tt(G[hh * 64:(hh + 1) * 64, 0, kcA, 0:Tm], psum[po][0:64, 0:Tm], FB[0:64, fr, 0:Tm], ALU.mult, [bPS[po], bFB[fr]], [bG[0]])
                PSP.put(po)
            def conv_ln():
                p1 = PSP.get()
                p2 = PSP.get()
                for c in range(4):
                    mm(psum[p1][:, 0:Tm], onesf[:, :], FA[:, c, 0:Tm], c == 0, c == 3, [buf('onesf'), bFA[c]], [bPS[p1]])
                for c in range(4):
                    if presq['done']:
                        fs = sqt[c]
                    else:
                        fs = 4 + c % 2
                        act(FB[:, fs, 0:Tm], FA[:, c, 0:Tm], AF.Square, [bFA[c]], [bFB[fs]])
                    mm(psum[p2][:, 0:Tm], onesf[:, :], FB[:, fs, 0:Tm], c == 0, c == 3, [buf('onesf'), bFB[fs]], [bPS[p2]])
                act(FA[:, 4, 0:Tm], psum[p1][:, 0:Tm], AF.Copy, [bPS[p1]], [bFA[4]], scale=1.0 / 512)
                act(FA[:, 5, 0:Tm], psum[p1][:, 0:Tm], AF.Square, [bPS[p1]], [bFA[5]], scale=1.0 / 512)
                stt(FA[:, 6, 0:Tm], psum[p2][:, 0:Tm], 1.0 / 512, FA[:, 5, 0:Tm], ALU.mult, ALU.subtract, [bPS[p2], bFA[5]], [bFA[6]])
                PSP.put(p1)
                PSP.put(p2)
                act(FA[:, 5, 0:Tm], FA[:, 6, 0:Tm], AF.Ln, [bFA[6]], [bFA[5]], bias=epsT[:, 0:1], scale=1.0)
                act(FA[:, 6, 0:Tm], FA[:, 5, 0:Tm], AF.Exp, [bFA[5]], [bFA[6]], scale=-0.5)
                for c in range(4):
                    fy = 4 + c % 2
                    tt(FB[:, fy, 0:Tm], FA[:, c, 0:Tm], FA[:, 4, 0:Tm], ALU.subtract, [bFA[c], bFA[4]], [bFB[fy]])
                    tt(FB[:, fy, 0:Tm], FB[:, fy, 0:Tm], FA[:, 6, 0:Tm], ALU.mult, [bFB[fy], bFA[6]], [bFB[fy]])
                    act(G[:, 1, c, 0:Tm], FB[:, fy, 0:Tm], AF.Silu, [bFB[fy], buf('vecT')], [bG[1]],
                        bias=vecT[:, l, V_LB + c:V_LB + c + 1], scale=vecT[:, l, V_LG + c:V_LG + c + 1])

            if nsg == 1:
                pump(conv_ops, 1000)
                conv_ln()
                gate_and_project(l, 0, 0, Tm, 'first')
                gate_and_project(l, 3, 2, Tm, 'mid')
            else:
                gate_and_project(l, 0, 0, Tm, 'first', pump=lambda: pump(conv_ops, 12))
                gate_and_project(l, 3, 2, Tm, 'mid', pump=lambda: pump(conv_ops, 12))
                pump(conv_ops, 1000)
                conv_ln()
            fa_flat = FA[:, :, :].rearrange("p a b -> p (a b)")
            pw_ = smallw[:, 0:512].rearrange("p (g d) -> p g d", g=4)
            pool_ops = []

            def pool_group(g):
                w = 2 ** (g + 1)
                hw = w // 2
                for gi_, sg in enumerate(segs):
                    c0, n_ = sg['col0'], sg['n']
                    xw = plwin[:, g, gi_ * (n_ + 16):gi_ * (n_ + 16) + n_ + 16]

                    def seg_ops(c0=c0, n_=n_, xw=xw, sg=sg):
                        if g == 0:
                            src, srcb = xw, [buf('plwin')]
                        else:
                            L1 = n_ + 15
                            pool_ops.append(lambda: tt(fa_flat[:, 0:L1], xw[:, 0:L1], xw[:, 1:L1 + 1], ALU.add, [buf('plwin')], [bFA[0], bFA[1]]))
                            src, srcb = fa_flat[:, 0:1024], [bFA[0], bFA[1]]
                            if g >= 2:
                                L2 = n_ + 13
                                pool_ops.append(lambda: tt(fa_flat[:, 1024:1024 + L2], fa_flat[:, 0:L2], fa_flat[:, 2:L2 + 2], ALU.add, [bFA[0], bFA[1]], [bFA[2], bFA[3]]))
                                src, srcb = fa_flat[:, 1024:2048], [bFA[2], bFA[3]]
                            if g >= 3:
                                L3 = n_ + 9
                                pool_ops.append(lambda: tt(fa_flat[:, 2048:2048 + L3], fa_flat[:, 1024:1024 + L3], fa_flat[:, 1028:1028 + L3], ALU.add, [bFA[2], bFA[3]], [bFA[4], bFA[5]]))
                                src, srcb = fa_flat[:, 2048:3072], [bFA[4], bFA[5]]
                        pool_ops.append(lambda: tt(FA[:, 6, c0:c0 + n_], src[:, 8 - hw:8 - hw + n_], src[:, 8:8 + n_], ALU.add, srcb, [bFA[6]]))
                        pool_ops.append(lambda: ts(FA[:, 6, c0:c0 + n_], FA[:, 6, c0:c0 + n_], 1.0 / w, None, ALU.mult, None, [bFA[6]], [bFA[6]]))
                        if sg['first']:
                            pool_ops.append(lambda: tt(FA[:, 6, c0:c0 + 8], FA[:, 6, c0:c0 + 8], pcorr[:, g, 0, :], ALU.mult, [bFA[6], buf('pcorr')], [bFA[6]]))
                        if sg['last']:
                            pool_ops.append(lambda: tt(FA[:, 6, c0 + n_ - 8:c0 + n_], FA[:, 6, c0 + n_ - 8:c0 + n_], pcorr[:, g, 1, :], ALU.mult, [bFA[6], buf('pcorr')], [bFA[6]]))
                        pool_ops.append(lambda: tt(pooled_bf[:, c0:c0 + n_], FA[:, 6, c0:c0 + n_], xw[:, 8:8 + n_], ALU.subtract, [bFA[6], buf('plwin')], [buf('pooled')]))
                    seg_ops()

                def fin():
                    pb = PSP.get()
                    mm(psum[pb][:, 0:Tm], pw_[:, g, :], pooled_bf[:, 0:Tm], True, True, [buf('smallw'), buf('pooled')], [bPS[pb]])
                    act(G[:, 0, g, 0:Tm], psum[pb][:, 0:Tm], AF.Copy, [bPS[pb], buf('vecT')], [bG[0]], scale=vecT[:, l, V_PS + g:V_PS + g + 1])
                    PSP.put(pb)
                pool_ops.append(fin)
            for g in range(4):
                pool_group(g)
            gate_and_project(l, 1, 1, Tm, 'mid', pump=lambda: pump(pool_ops, 4 * nsg))
            pump(pool_ops, 1000)
            gate_and_project(l, 2, 0, Tm, 'last')
            mbf = [buf(f'mbf{dc}') for dc in range(8)]
            s_oo = [WS.need(l, 'out_0'), WS.need(l, 'out_1')]

            def norm2_scale(s):
                recip(stat[:, 8 + s:9 + s], stat[:, 4 + s:5 + s], st(4 + s), st(8 + s))
                act(FA[:, 2 * s:2 * s + 2, :].rearrange("p a b -> p (a b)"), x_sb[:, s, :], AF.Copy,
                    [bX[s]] + st(8 + s), [bFA[2 * s], bFA[2 * s + 1]], scale=stat[:, 8 + s:9 + s])
            for s in range(nsub):
                for cb in range(2):
                    s_o = s_oo[cb]
                    wo = wview(s_o, 8, 512)
                    pb = PSP.get()
                    for kc in range(KC):
                        mm(psum[pb][:, :], merged_bf[:, kc, s * 128:(s + 1) * 128], wo[:, kc, :], kc == 0, kc == KC - 1,
                           mbf + [bW[s_o]], [bPS[pb]])
                    ft = 2 + cb
                    tt(FB[:, ft, :], psum[pb][:, :], gbc[:, 0, cb * 512:(cb + 1) * 512], ALU.mult, [bPS[pb], buf('gbc')], [bFB[ft]])
                    PSP.put(pb)
                    tt(x_sb[:, s, cb * 512:(cb + 1) * 512], x_sb[:, s, cb * 512:(cb + 1) * 512], FB[:, ft, :], ALU.add, [bX[s], bFB[ft]], [bX[s]])
                sumsq(FA[:, 2 * s:2 * s + 2, :].rearrange("p a b -> p (a b)"), x_sb[:, s, :], stat[:, s:s + 1],
                      [bX[s]], [bFA[2 * s], bFA[2 * s + 1]] + st(s))
                act(stat[:, 4 + s:5 + s], stat[:, s:s + 1], AF.Sqrt, st(s), st(4 + s), bias=epsT[:, 0:1], scale=1.0 / D)
                if s > 0:
                    norm2_scale(s - 1)
            norm2_scale(nsub - 1)
            norm_to_hT(l, t, 1, nsub, skip_elem=True)
            f1 = [FA[:, 0:4, :].rearrange("p a b -> p (a b)").bitcast(BF16).rearrange("p (k c) -> p k c", k=8),
                  FA[:, 4:8, :].rearrange("p a b -> p (a b)").bitcast(BF16).rearrange("p (k c) -> p k c", k=8)]
            f1.append(merged_bf)
            mbfb = [buf(f'mbf{dc}') for dc in range(8)]
            f1bufs = [[bFAh[jc] for jc in range(8)], [bFAh[8 + jc] for jc in range(8)], mbfb]
            for gq in range(4):
                fi = gq % 2 if gq < 3 else 2
                if gq == 3 and nxt is not None:
                    dma('sp', FA[:, 0:2 * nxt[1], :].rearrange("p (s a) b -> p s (a b)", a=2),
                        src_d[nxt[0]:nxt[0] + nxt[1] * 128, :].rearrange("(s p) d -> p s d", p=128),
                        [buf('x1' + kind)] if cur['l'] > 0 else [], bFA[0:2 * nxt[1]], 'xf')
                    xf['have'] = True
                for half in range(2):
                    s_i = WS.need(l, f'mi_{gq * 2 + half}')
                    wi = wview(s_i, 8, 512)
                    for dq in range(4):
                        jc = half * 4 + dq
                        pb = PSP.get()
                        for kc in range(KC):
                            mm(psum[pb][:, 0:Tm], wi[:, kc, dq * 128:(dq + 1) * 128], hT[:, kc, 0:Tm], kc == 0, kc == KC - 1,
                               bHT + [bW[s_i]], [bPS[pb]])
                        fr = 4 + jc % 2
                        act(FB[:, fr, 0:Tm], psum[pb][:, 0:Tm], AF.Relu, [bPS[pb]], [bFB[fr]])
                        PSP.put(pb)
                        tt(f1[fi][:, jc, 0:Tm], FB[:, fr, 0:Tm], FB[:, fr, 0:Tm], ALU.mult, [bFB[fr]], [f1bufs[fi][jc]])
                if gq == 3 and xf['have']:
                    xf['have'] = False
                    norm_elem_from_fa(nxt[1])
                    xn['ready'] = True

                def mo_step(cb, s_o, s):
                    wo = wview(s_o, 8, 512)
                    pb = PSP.get()
                    for jc in range(8):
                        mm(psum[pb][:, :], f1[fi][:, jc, s * 128:(s + 1) * 128], wo[:, jc, :], jc == 0, jc == 7,
                           [f1bufs[fi][jc], bW[s_o]], [bPS[pb]])
                    ft = 2 + s % 2
                    tt(FB[:, ft, :], psum[pb][:, :], gbc[:, 1, cb * 512:(cb + 1) * 512], ALU.mult, [bPS[pb], buf('gbc')], [bFB[ft]])
                    PSP.put(pb)
                    tt(x_sb[:, s, cb * 512:(cb + 1) * 512], x_sb[:, s, cb * 512:(cb + 1) * 512], FB[:, ft, :], ALU.add, [bX[s], bFB[ft]], [bX[s]])
                if gq < 3:
                    for cb in range(2):
                        s_o = WS.need(l, f'mo_{cb}_{gq}')
                        for s in range(nsub):
                            mo_step(cb, s_o, s)
                else:
                    s_o0 = WS.need(l, f'mo_0_{gq}')
                    s_o1 = WS.need(l, f'mo_1_{gq}')
                    def finish_subtile(s):
                        if final:
                            stt(x_sb[:, s, :], x_sb[:, s, :], stat[:, 8 + s:9 + s], fng[:, :], ALU.mult, ALU.mult,
                                [bX[s], buf('fng')] + st(8 + s), [bX[s]])
                        dma('sp', dst_d[tok0 + s * 128:tok0 + (s + 1) * 128, :], x_sb[:, s, :], [bX[s]],
                            [buf(('y' if final else 'x1') + kind)], f'xo{s}')
                        if nxt is not None and s < nxt[1]:
                            rd = [buf('x1' + kind)] if cur['l'] > 0 else []
                            dma('sp', x_sb[:, s, :], src_d[nxt[0] + s * 128:nxt[0] + (s + 1) * 128, :], rd, [bX[s]], f'x{s}')
                    for s in range(nsub):
                        mo_step(0, s_o0, s)
                        mo_step(1, s_o1, s)
                        if final:
                            sumsq(FB[:, 4:6, :].rearrange("p a b -> p (a b)"), x_sb[:, s, :], stat[:, s:s + 1],
                                  [bX[s]], [bFB[4], bFB[5]] + st(s))
                            act(stat[:, 4 + s:5 + s], stat[:, s:s + 1], AF.Sqrt, st(s), st(4 + s), bias=epsT[:, 0:1], scale=1.0 / D)
                            if s > 0:
                                recip(stat[:, 8 + s - 1:9 + s - 1], stat[:, 4 + s - 1:5 + s - 1], st(4 + s - 1), st(8 + s - 1))
                                finish_subtile(s - 1)
                        else:
                            finish_subtile(s)
                    if final:
                        recip(stat[:, 8 + nsub - 1:9 + nsub - 1], stat[:, 4 + nsub - 1:5 + nsub - 1], st(4 + nsub - 1), st(8 + nsub - 1))
                        finish_subtile(nsub - 1)
                    if nxt is not None:
                        xpre['have'] = True

        def whole_program():
            if not P.dry:
                setup()
            for l in range(NLAYER):
                cur['l'] = l
                if not P.dry:
                    layer_setup(l)
                final = (l == NLAYER - 1)
                for kind in ('P', 'S'):
                    t = 0 if kind == 'P' else 1
                    if kind == 'P':
                        N = 2 * SEQ_P
                        src = xp_d if l == 0 else x1p_d
                        dst = yp_d if final else x1p_d
                        key0 = 0
                    else:
                        N = NS
                        src = xs_d if l == 0 else x1s_d
                        dst = ys_d if final else x1s_d
                        key0 = PAST
                    tiles = []
                    tk = 0
                    while tk < N:
                        ns_ = min(4, (N - tk) // 128)
                        if kind == 'P':
                            segs = [dict(col0=g * SEQ_P, n=SEQ_P, kt0=2 * g, nkt=2, apos=g * (SEQ_P + 32) + 16, ppos=g * (SEQ_P + 16) + 8,
                                         first=True, last=True) for g in range(2)]
                        else:
                            segs = [dict(col0=0, n=ns_ * 128, kt0=0, nkt=(N + key0) // 128, apos=16 + tk, ppos=8 + tk,
                                         first=(tk == 0), last=(tk + ns_ * 128 == N))]
                        tiles.append((tk, ns_, segs))
                        tk += ns_ * 128
                    if not P.dry:
                        type_setup(l, t)
                        seq_setup(l, kind, N)
                    for ti, (tok0, ns_, segs) in enumerate(tiles):
                        nxt = tiles[ti + 1][0:2] if ti + 1 < len(tiles) else tiles[0][0:2]
                        phaseA(l, kind, t, src, tok0, ns_, key0, segs, nxt)
                    for ti, (tok0, ns_, segs) in enumerate(tiles):
                        nxt = tiles[ti + 1][0:2] if ti + 1 < len(tiles) else None
                        phaseB(l, kind, t, src, dst, tok0, ns_, segs, final, nxt)

        P.dry = True
        whole_program()
        P.dry = False
        whole_program()
        assert WS.pos == len(WS.sched)
        fw = {'sp': [('c', c, P.chan_cnt[c]) for c in ('xo0', 'xo1', 'xo2', 'xo3', 'kvo') if c in P.chan_cnt]}
        finalize_and_emit(P, block, sems, csems, fw)
    return nc


_CACHE = {}


def run_cores(inputs, NS, ncores):
    hw = host_weights(inputs)
    hc = host_consts(NS)
    xp = np.asarray(inputs['x_prompt'], np.float32)
    xs = np.asarray(inputs['x_sample'], np.float32)
    ck = np.asarray(inputs['cache_k'], np.float32)
    cv = np.asarray(inputs['cache_v'], np.float32)
    c = np.asarray(inputs['c'], np.float32)
    c_ctx = np.asarray(inputs['c_ctx'], np.float32)
    in_maps = []
    for i in range(ncores):
        cvec = np.stack([c_ctx.reshape(8, 128).T, c[i].reshape(8, 128).T], axis=-1).astype(np.float32)
        m = dict(xp=np.ascontiguousarray(xp[2 * i:2 * i + 2].reshape(2 * SEQ_P, D)),
                 xs=np.ascontiguousarray(xs[i]),
                 ck=np.ascontiguousarray(ck[i].reshape(NLAYER, PAST, 128)),
                 cv=np.ascontiguousarray(cv[i].reshape(NLAYER, PAST, 128)),
                 cvec=np.ascontiguousarray(cvec))
        m.update(hw)
        m.update(hc)
        in_maps.append(m)
    if NS not in _CACHE:
        _CACHE[NS] = build_program(NS)
    nc = _CACHE[NS]
    res = run_bass_kernel_spmd(nc, in_maps, core_ids=list(range(ncores)))
    yp = np.stack([r['yp'].reshape(2, SEQ_P, D) for r in res.results]).reshape(2 * ncores, SEQ_P, D)
    ys = np.stack([r['ys'] for r in res.results])
    nk = np.stack([r['nk'] for r in res.results]).reshape(2 * ncores, NLAYER, SEQ_P, 2, 64)
    nv = np.stack([r['nv'] for r in res.results]).reshape(2 * ncores, NLAYER, SEQ_P, 2, 64)
    return (yp.astype(np.float32), ys.astype(np.float32), nk.astype(np.float32), nv.astype(np.float32))


def kernel(**inputs):
    NS = int(np.asarray(inputs['x_sample']).shape[1])
    ncores = int(np.asarray(inputs['x_sample']).shape[0])
    return run_cores(inputs, NS, ncores)
```
